# Optimizing a Trainium2 kernel written in Bass

```python
import math
import jax, jax.numpy as jnp
from jax import lax
import numpy as np

D_MODEL = 1024
BATCH = 16
SEQ = 2048
DEPTH = 1

GRID_W = 64
CTX_LEN = 256
EPS = 1e-6
N_MOD = 9
D_FF = 2816
HEAD_DIM = 64
N_Q_HEADS = 8
N_KV_HEADS = 2
GROUP = N_Q_HEADS // N_KV_HEADS
ATT_WIDTH = N_Q_HEADS * HEAD_DIM
KV_WIDTH = N_KV_HEADS * HEAD_DIM
Q_BLOCK = 128
ROPE_THETA = 10000.0
ROPE_PAIRS = HEAD_DIM // 4
ATT_SCALE = HEAD_DIM ** -0.5
HG_HEADS = 4
HG_DK = 128
HG_DV = 128
HG_WIDTH = HG_HEADS * HG_DK
HG_VWIDTH = HG_HEADS * HG_DV
HG_SCALE = HG_DK ** -0.5
CHUNK = 64
IN_SPLITS = (ATT_WIDTH, KV_WIDTH, KV_WIDTH, HG_WIDTH, HG_VWIDTH, HG_WIDTH, HG_WIDTH, HG_VWIDTH, D_MODEL, D_MODEL)
D_IN = sum(IN_SPLITS)

kernel_name = "hybrid_gqa_hgrn2_macaron_prefix_block"


def rms_norm(x, gain):
    xf = x.astype(jnp.float32)
    y = xf * lax.rsqrt(jnp.mean(xf * xf, axis=-1, keepdims=True) + EPS)
    return (y * gain.astype(jnp.float32)).astype(x.dtype)


def modulation(cvec, w_mod_l, b_mod_l):
    m = jax.nn.silu(cvec) @ w_mod_l + b_mod_l
    return jnp.split(m[:, None, :], N_MOD, axis=-1)


def pre(h, g_pre, shift, scale):
    return rms_norm(h, g_pre) * (1.0 + scale) + shift


def post(y, g_post, gate):
    return gate * rms_norm(y, g_post)


def swiglu(u, w_gate, w_up, w_down):
    return (jax.nn.silu(u @ w_gate) * (u @ w_up)) @ w_down


def heads(a, d):
    return a.reshape(*a.shape[:-1], -1, d)


def split_in(p):
    out, start = [], 0
    for size in IN_SPLITS:
        out.append(p[..., start:start + size])
        start += size
    return out


def axial_rope(rows):
    row = jnp.repeat(jnp.arange(rows, dtype=jnp.float32), GRID_W)
    col = jnp.tile(jnp.arange(GRID_W, dtype=jnp.float32), rows)
    inv_freq = ROPE_THETA ** (-jnp.arange(ROPE_PAIRS, dtype=jnp.float32) / ROPE_PAIRS)
    ang_r = row[:, None] * inv_freq
    ang_c = col[:, None] * inv_freq
    ang = jnp.concatenate([ang_r, ang_r, ang_c, ang_c], axis=-1)
    return jnp.cos(ang), jnp.sin(ang)


def apply_rope(x, cos, sin):
    xa = x.reshape(*x.shape[:-1], 2, 2, ROPE_PAIRS)
    rot = jnp.stack([-xa[..., 1, :], xa[..., 0, :]], axis=-2).reshape(x.shape)
    return x * cos[:, None, :].astype(x.dtype) + rot * sin[:, None, :].astype(x.dtype)


def gqa_softmax(q, k, v):
    s = jnp.einsum('bqkgd,bskd->bkgqs', q, k).astype(jnp.float32) * ATT_SCALE
    p = jax.nn.softmax(s, axis=-1).astype(v.dtype)
    return jnp.einsum('bkgqs,bskd->bqkgd', p, v)


def latent_attention(q, k, v, k_ctx, v_ctx):
    bsz, t = q.shape[:2]
    k_all = jnp.concatenate([k, k_ctx], axis=1)
    v_all = jnp.concatenate([v, v_ctx], axis=1)
    nb = t // Q_BLOCK
    qb = jnp.moveaxis(q.reshape(bsz, nb, Q_BLOCK, N_KV_HEADS, GROUP, HEAD_DIM), 1, 0)
    o = lax.map(lambda qi: gqa_softmax(qi, k_all, v_all), qb)
    return jnp.moveaxis(o, 0, 1).reshape(bsz, t, ATT_WIDTH)


def att_kv(p, k_gain, cos=None, sin=None):
    k = rms_norm(heads(p[1], HEAD_DIM), k_gain)
    if cos is not None:
        k = apply_rope(k, cos, sin)
    return k, heads(p[2], HEAD_DIM)


def att_q(p, q_gain, cos=None, sin=None):
    q = rms_norm(heads(p[0], HEAD_DIM), q_gain)
    if cos is not None:
        q = apply_rope(q, cos, sin)
    return q.reshape(*q.shape[:2], N_KV_HEADS, GROUP, HEAD_DIM)


def forget_gate(f_raw, lb):
    lbh = lb.reshape(HG_HEADS, HG_DK)
    f = lbh + (1.0 - lbh) * jax.nn.sigmoid(heads(f_raw, HG_DK).astype(jnp.float32))
    return 1.0 - f, jnp.log(f)


def chunk_states(k, v, g, s0):
    bsz, t, h, dk = k.shape
    n = t // CHUNK
    kc = k.reshape(bsz, n, CHUNK, h, dk)
    vc = v.reshape(bsz, n, CHUNK, h, -1)
    b = jnp.cumsum(g.reshape(bsz, n, CHUNK, h, dk), axis=2)
    b_last = b[:, :, -1]
    ds = jnp.einsum('bnshk,bnshv->bnhkv', kc * jnp.exp(b_last[:, :, None] - b), vc)

    def step(s, inp):
        ds_n, dec_n = inp
        return dec_n[..., None] * s + ds_n, s

    s_fin, s_prev = lax.scan(step, s0, (jnp.moveaxis(ds, 1, 0), jnp.moveaxis(jnp.exp(b_last), 1, 0)))
    return jnp.moveaxis(s_prev, 0, 1), s_fin, b


def chunk_output(q, k, v, b, s_prev):
    bsz, t, h, dk = q.shape
    n = t // CHUNK
    qc = q.reshape(bsz, n, CHUNK, h, dk)
    kc = k.reshape(bsz, n, CHUNK, h, dk)
    vc = v.reshape(bsz, n, CHUNK, h, -1)
    b_mid = b[:, :, CHUNK // 2 - 1:CHUNK // 2]
    s = jnp.einsum('bnchk,bnshk->bnhcs', qc * jnp.exp(b - b_mid), kc * jnp.exp(b_mid - b))
    s = jnp.where(jnp.tril(jnp.ones((CHUNK, CHUNK), dtype=bool)), s, 0.0)
    o = (jnp.einsum('bnhcs,bnshv->bnchv', s, vc)
         + jnp.einsum('bnchk,bnhkv->bnchv', qc * jnp.exp(b), s_prev))
    return o.reshape(bsz, t, h, -1)


def hgrn_direction(q_x, v_x, f_x, q_c, v_c, f_c, lb):
    k_x, g_x = forget_gate(f_x, lb)
    k_c, g_c = forget_gate(f_c, lb)
    s0 = jnp.zeros((v_c.shape[0], HG_HEADS, HG_DK, HG_DV), jnp.float32)
    sp_c, sf_c, b_c = chunk_states(k_c, v_c, g_c, s0)
    sp_x, _, b_x = chunk_states(k_x, v_x, g_x, sf_c)
    o_x = chunk_output(q_x, k_x, v_x, b_x, sp_x)
    o_c = chunk_output(q_c, k_c, v_c, b_c, sp_c) if q_c is not None else None
    return o_x, o_c


def hg_query(p):
    return jax.nn.silu(heads(p[3], HG_DK).astype(jnp.float32)) * HG_SCALE


def hg_out(o, g_raw, hg_gain, dtype):
    o = rms_norm(o, hg_gain) * jax.nn.silu(heads(g_raw, HG_DV).astype(jnp.float32))
    return o.reshape(*o.shape[:2], HG_VWIDTH).astype(dtype)


def rev(a):
    return jnp.flip(a, axis=1)


def hgrn_branch(px, pc, lb_l, hg_gain, with_ctx):
    q_x = hg_query(px)
    v_x = heads(px[4], HG_DV).astype(jnp.float32)
    v_c = heads(pc[4], HG_DV).astype(jnp.float32)
    q_c = hg_query(pc) if with_ctx else None
    of_x, of_c = hgrn_direction(q_x, v_x, px[5], q_c, v_c, pc[5], lb_l[0])
    ob_x, ob_c = hgrn_direction(rev(q_x), rev(v_x), rev(px[6]),
                                rev(q_c) if with_ctx else None, rev(v_c), rev(pc[6]), lb_l[1])
    o_x = hg_out(of_x + rev(ob_x), px[7], hg_gain, px[7].dtype)
    o_c = hg_out(of_c + rev(ob_c), pc[7], hg_gain, pc[7].dtype) if with_ctx else None
    return o_x, o_c


def merge(p, o_att, o_hg, w_att_out_l, w_hg_out_l, w_o_l):
    y = jax.nn.sigmoid(p[8]) * (o_att @ w_att_out_l) + jax.nn.sigmoid(p[9]) * (o_hg @ w_hg_out_l)
    return y @ w_o_l


def token_mixer(ux, uc, cos, sin, w_in_l, q_gain, k_gain, lb_l, hg_gain,
                w_att_out_l, w_hg_out_l, w_o_l, with_ctx):
    px = split_in(ux @ w_in_l)
    pc = split_in(uc @ w_in_l)
    k_x, v_x = att_kv(px, k_gain, cos, sin)
    k_c, v_c = att_kv(pc, k_gain)
    o_att_x = latent_attention(att_q(px, q_gain, cos, sin), k_x, v_x, k_c, v_c)
    o_hg_x, o_hg_c = hgrn_branch(px, pc, lb_l, hg_gain, with_ctx)
    y_x = merge(px, o_att_x, o_hg_x, w_att_out_l, w_hg_out_l, w_o_l)
    y_c = None
    if with_ctx:
        q_c = att_q(pc, q_gain)
        o_att_c = gqa_softmax(q_c, k_c, v_c).reshape(*uc.shape[:2], ATT_WIDTH)
        y_c = merge(pc, o_att_c, o_hg_c, w_att_out_l, w_hg_out_l, w_o_l)
    return y_x, y_c


def setup_inputs(seed: int = 0) -> dict:
    key = jax.random.key(seed)
    ks = jax.random.split(key, 20)
    nrm = lambda k, shape, s: jax.random.normal(k, shape, jnp.float32) * s
    gain = lambda k, shape: 1.0 + 0.02 * jax.random.normal(k, shape, jnp.float32)
    return {
        "x": nrm(ks[0], (BATCH, SEQ, D_MODEL), 1.0),
        "c": nrm(ks[1], (BATCH, D_MODEL), 1.0),
        "ctx": nrm(ks[2], (BATCH, CTX_LEN, D_MODEL), 1.0),
        "c_ctx": nrm(ks[3], (D_MODEL,), 1.0),
        "w_mod": nrm(ks[4], (DEPTH, D_MODEL, N_MOD * D_MODEL), 0.5 * D_MODEL ** -0.5),
        "b_mod": nrm(ks[5], (DEPTH, N_MOD * D_MODEL), 0.02),
        "norm_pre": gain(ks[6], (DEPTH, 3, D_MODEL)),
        "norm_post": gain(ks[7], (DEPTH, 3, D_MODEL)),
        "ffn_w_gate": nrm(ks[8], (DEPTH, 2, D_MODEL, D_FF), D_MODEL ** -0.5),
        "ffn_w_up": nrm(ks[9], (DEPTH, 2, D_MODEL, D_FF), D_MODEL ** -0.5),
        "ffn_w_down": nrm(ks[10], (DEPTH, 2, D_FF, D_MODEL), D_FF ** -0.5),
        "w_in": nrm(ks[11], (DEPTH, D_MODEL, D_IN), D_MODEL ** -0.5),
        "q_norm": gain(ks[12], (DEPTH, HEAD_DIM)),
        "k_norm": gain(ks[13], (DEPTH, HEAD_DIM)),
        "hg_lower_bound": nrm(ks[14], (2, DEPTH + 1, HG_WIDTH), 0.1),
        "hg_norm": gain(ks[15], (DEPTH, HG_DV)),
        "w_att_out": nrm(ks[16], (DEPTH, ATT_WIDTH, D_MODEL), ATT_WIDTH ** -0.5),
        "w_hg_out": nrm(ks[17], (DEPTH, HG_VWIDTH, D_MODEL), HG_VWIDTH ** -0.5),
        "w_o": nrm(ks[18], (DEPTH, D_MODEL, D_MODEL), D_MODEL ** -0.5),
    }


def reference(x, c, ctx, c_ctx, w_mod, b_mod, norm_pre, norm_post, ffn_w_gate, ffn_w_up, ffn_w_down,
              w_in, q_norm, k_norm, hg_lower_bound, hg_norm, w_att_out, w_hg_out, w_o):
    t = x.shape[1]
    rows = t // GRID_W
    cos, sin = axial_rope(rows)
    lb_all = jnp.cumsum(jax.nn.softmax(hg_lower_bound.astype(jnp.float32), axis=1), axis=1)
    h_c = ctx
    for l in range(DEPTH):
        last = l == DEPTH - 1
        mx = modulation(c, w_mod[l], b_mod[l])
        mc = modulation(c_ctx[None, :], w_mod[l], b_mod[l])
        ffn1 = lambda h, m: 0.5 * post(swiglu(pre(h, norm_pre[l, 0], m[0], m[1]), ffn_w_gate[l, 0],
                                              ffn_w_up[l, 0], ffn_w_down[l, 0]), norm_post[l, 0], m[2])
        x = x + ffn1(x, mx)
        h_c = h_c + ffn1(h_c, mc)
        ux = pre(x, norm_pre[l, 1], mx[3], mx[4])
        uc = pre(h_c, norm_pre[l, 1], mc[3], mc[4])
        y_x, y_c = token_mixer(ux, uc, cos, sin, w_in[l], q_norm[l], k_norm[l], lb_all[:, l], hg_norm[l],
                               w_att_out[l], w_hg_out[l], w_o[l], with_ctx=not last)
        x = x + post(y_x, norm_post[l, 1], mx[5])
        ffn2 = lambda h, m: 0.5 * post(swiglu(pre(h, norm_pre[l, 2], m[6], m[7]), ffn_w_gate[l, 1],
                                              ffn_w_up[l, 1], ffn_w_down[l, 1]), norm_post[l, 2], m[8])
        if not last:
            h_c = h_c + post(y_c, norm_post[l, 1], mc[5])
            h_c = h_c + ffn2(h_c, mc)
        x = x + ffn2(x, mx)
    return x
```

```python
import numpy as np
from contextlib import ExitStack
import concourse.bass as bass
import concourse.mybir as mybir
from concourse.bass_utils import run_bass_kernel_spmd

F32 = mybir.dt.float32
BF16 = mybir.dt.bfloat16
F32R = mybir.dt.float32r
AF = mybir.ActivationFunctionType
ALU = mybir.AluOpType

D = 1024
FF = 2816
NFC = FF // 128
DIN = 5376
TC = 256
TL = 2048
T = TC + TL
EPS = 1e-6
ATT_SCALE = 64 ** -0.5
HG_SCALE = 128 ** -0.5
OQ, OK_, OV, OHQ, OHV, OFF, OFB, OHG, OGA = 0, 512, 640, 768, 1280, 1792, 2304, 2816, 3328


class _Op:
    __slots__ = ("eng", "fn", "deps", "need_inc", "cnt", "dma_sem", "dma_val", "ndma", "idx")


class Sched:
    COMPUTE = ("pe", "act", "dve", "pool")
    ENGS = ("pe", "act", "dve", "pool", "sp")

    def __init__(self):
        self.ops = {e: [] for e in self.ENGS}
        self.last_w = {}
        self.readers = {}
        self.dma_tot = {}
        self.all_dma = []
        self.nops = 0

    def _new(self, eng, fn):
        op = _Op()
        op.eng = eng
        op.fn = fn
        op.deps = []
        op.need_inc = False
        op.cnt = None
        op.dma_sem = None
        op.dma_val = None
        op.ndma = 0
        op.idx = self.nops
        self.nops += 1
        return op

    def _add(self, eng, fn, reads, writes, dma_key=None, ndma=0):
        op = self._new(eng, fn)
        if dma_key is not None:
            op.dma_sem = dma_key
            op.ndma = ndma
            self.dma_tot[dma_key] = self.dma_tot.get(dma_key, 0) + 16 * ndma
            op.dma_val = self.dma_tot[dma_key]
            self.all_dma.append(op)
        deps = {}
        for k in reads:
            w = self.last_w.get(k)
            if w is not None:
                deps[w.idx] = w
        for k in writes:
            w = self.last_w.get(k)
            if w is not None:
                deps[w.idx] = w
            for r in self.readers.get(k, ()):
                deps[r.idx] = r
        for d in deps.values():
            if d is op:
                continue
            if d.dma_sem is None and d.eng == "pe" and eng == "pe" and dma_key is None:
                continue
            op.deps.append(d)
            if d.dma_sem is None:
                d.need_inc = True
        for k in writes:
            self.last_w[k] = op
            self.readers[k] = []
        for k in reads:
            self.readers.setdefault(k, []).append(op)
        self.ops[eng].append(op)
        return op

    def op(self, eng, fn, reads=(), writes=()):
        return self._add(eng, fn, list(reads), list(writes))

    def dma(self, eng, pairs, sem_key, reads=(), writes=()):
        pairs = list(pairs)
        fn = lambda e: [e.dma_start(out=o, in_=i) for (o, i) in pairs]
        return self._add(eng, fn, list(reads), list(writes), dma_key=sem_key, ndma=len(pairs))

    def barrier(self):
        lasts = []
        for e in self.COMPUTE:
            for o in reversed(self.ops[e]):
                if o.dma_sem is None and o.fn is not None:
                    lasts.append(o)
                    break
        dmas = list(self.all_dma)
        self.all_dma = []
        for e in self.ENGS:
            op = self._new(e, None)
            for d in lasts:
                if d.dma_sem is None and d.eng != e and d.fn is not None:
                    d.need_inc = True
                    op.deps.append(d)
            for d in dmas:
                op.deps.append(d)
            self.ops[e].append(op)
        self.last_w = {}
        self.readers = {}

    def emit(self, nc, stack):
        esem = {e: stack.enter_context(nc.semaphore("s_" + e)) for e in self.COMPUTE}
        dsem = {k: stack.enter_context(nc.semaphore("d_" + str(k))) for k in self.dma_tot}
        for e in self.COMPUTE:
            c = 0
            for op in self.ops[e]:
                if op.need_inc:
                    c += 1
                    op.cnt = c
        ops = self.ops

        def run(e, eng):
            waited = {}
            for op in ops[e]:
                for d in op.deps:
                    if d.dma_sem is not None:
                        key, val, sem = ("d", d.dma_sem), d.dma_val, dsem[d.dma_sem]
                    else:
                        key, val, sem = ("e", d.eng), d.cnt, esem[d.eng]
                    if waited.get(key, 0) >= val:
                        continue
                    waited[key] = val
                    eng.wait_ge(sem, val)
                if op.fn is None:
                    continue
                r = op.fn(eng)
                if op.dma_sem is not None:
                    for ins in r:
                        ins.then_inc(dsem[op.dma_sem], 16)
                elif op.need_inc:
                    r.then_inc(esem[e], 1)

        with nc.Block() as block:
            @block.tensor
            def _(eng):
                run("pe", eng)

            @block.scalar
            def _(eng):
                run("act", eng)

            @block.vector
            def _(eng):
                run("dve", eng)

            @block.gpsimd
            def _(eng):
                run("pool", eng)

            @block.sync
            def _(eng):
                run("sp", eng)


def MMS(lst):
    lst = list(lst)

    def f(e):
        r = None
        for (o, l, rr, s, p) in lst:
            r = e.matmul(o, lhsT=l, rhs=rr, start=s, stop=p)
        return r
    return f


def TRS(lst, ident):
    lst = list(lst)

    def f(e):
        r = None
        for (o, i) in lst:
            r = e.transpose(out=o, in_=i, identity=ident)
        return r
    return f


def ACTF(out, in_, func, **kw):
    return lambda e: e.activation(out=out, in_=in_, func=func, **kw)


def TT(out, in0, in1, op):
    return lambda e: e.tensor_tensor(out=out, in0=in0, in1=in1, op=op)


def TS(out, in0, s1, s2, op0, op1=None):
    if op1 is None:
        return lambda e: e.tensor_scalar(out=out, in0=in0, scalar1=s1, scalar2=None, op0=op0)
    return lambda e: e.tensor_scalar(out=out, in0=in0, scalar1=s1, scalar2=s2, op0=op0, op1=op1)


def STT(out, in0, scalar, in1, op0, op1):
    return lambda e: e.scalar_tensor_tensor(out=out, in0=in0, scalar=scalar, in1=in1, op0=op0, op1=op1)


def CP(out, in_):
    return lambda e: e.tensor_copy(out=out, in_=in_)


def MSET(out, v):
    return lambda e: e.memset(out, v)


class Arena:
    def __init__(self, ap, nwords):
        self.ap = ap
        self.n = nwords
        self.off = 0

    def reset(self):
        self.off = 0

    def alloc(self, free_shape, dtype, parts=128):
        n = int(np.prod(free_shape))
        words = n if dtype == F32 else (n + 1) // 2
        words = (words + 7) // 8 * 8
        assert self.off + words <= self.n, ("arena overflow", self.off, words, self.n)
        v = self.ap[0:parts, self.off:self.off + words]
        self.off += words
        if dtype != F32:
            v = v.bitcast(dtype)
        v = v[:, 0:n]
        if len(free_shape) == 2:
            v = v.rearrange("p (a b) -> p a b", b=free_shape[1])
        elif len(free_shape) == 3:
            v = v.rearrange("p (a b c) -> p a b c", b=free_shape[1], c=free_shape[2])
        elif len(free_shape) == 4:
            v = v.rearrange("p (a b c d) -> p a b c d", b=free_shape[1], c=free_shape[2], d=free_shape[3])
        return v


class Rot:
    def __init__(self, n):
        self.n = n
        self.i = -1

    def next(self):
        self.i = (self.i + 1) % self.n
        return self.i


def build_program(debug=False):
    nc = bass.Bass("TRN2", target_bir_lowering=False)

    def din(name, shape, dt=F32):
        return nc.dram_tensor(name, list(shape), dt, kind="ExternalInput").ap()

    def dscr(name, shape, dt):
        kind = "ExternalOutput" if debug else "Internal"
        return nc.dram_tensor(name, list(shape), dt, kind=kind).ap()

    x_in = din("x", [2, TL, D])
    ctx_in = din("ctx", [2, TC, D])
    cT_in = din("cT", [128, 24])
    w_mod = din("w_mod", [D, 9 * D])
    b_mod = din("b_mod", [9 * D])
    bmodT_in = din("bmodT", [128, 72])
    npreT_in = din("npreT", [128, 24])
    norm_post = din("norm_post", [3, D])
    w_gate = din("ffn_w_gate", [2, D, FF])
    w_up = din("ffn_w_up", [2, D, FF])
    w_down = din("ffn_w_down", [2, FF, D])
    w_in = din("w_in", [D, DIN])
    qk_gain_in = din("qk_gain", [128, 2])
    lbraw = din("hg_lower_bound", [2, 2, 512])
    hg_norm = din("hg_norm", [128])
    w_att_out = din("w_att_out", [512, D])
    w_hg_out = din("w_hg_out", [512, D])
    w_o = din("w_o", [D, D])
    c_mat = din("c_mat", [128, 384])
    c_rope = din("c_rope", [128, 2, TL])
    c_hg = din("c_hg", [128, 2, 640])
    out = nc.dram_tensor("out", [2, TL, D], F32, kind="ExternalOutput").ap()

    X1 = dscr("X1", [2, T, D], F32)
    QT = dscr("QT", [2, 4, 128, TL], BF16)
    KT = dscr("KT", [2, 128, T], BF16)
    VV = dscr("VV", [2, T, 128], BF16)
    HQT = dscr("HQT", [2, 4, 128, TL], BF16)
    HV = dscr("HV", [2, T, 512], BF16)
    KF = dscr("KF", [2, 2, T, 512], F32)
    HG = dscr("HG", [2, TL, 512], BF16)
    SG = dscr("SG", [2, 16, 128, TL], BF16)
    OAT = dscr("OAT", [2, 4, 128, TL], BF16)
    OHT = dscr("OHT", [2, 4, 128, TL], BF16)

    def bcast(a):
        return bass.AP(a.tensor, a.offset, [[0, 128]] + [list(v) for v in a.ap])

    S = Sched()
    with ExitStack() as st:
        NP = 7700
        NR = 1536
        r32_t = st.enter_context(nc.sbuf_tensor("r32", [128, NR], F32))
        NW = (int(nc.sbuf_bytes_remaining) // 4 - NP - 64) // 8 * 8
        arena_t = st.enter_context(nc.sbuf_tensor("arena", [128, NW], F32))
        pers_t = st.enter_context(nc.sbuf_tensor("pers", [128, NP], F32))
        ps_t = st.enter_context(nc.psum_tensor("ps", [128, 4096], F32))
        A = Arena(arena_t[:, :], NW)
        PA = Arena(pers_t[:, :], NP)
        ps = ps_t[:, :]
        psb = ps.bitcast(BF16)

        def bank(b, n=512, o=0):
            return ps[:, b * 512 + o:b * 512 + o + n]

        def bankb(b, n=1024, o=0):
            return psb[:, b * 1024 + o:b * 1024 + o + n]

        ABt = PA.alloc([3, 2, 3, 8], F32)
        Gb = {}
        for (i, r) in [(0, 0), (0, 1), (0, 2), (1, 0), (1, 1), (2, 0), (2, 1)]:
            Gb[(i, r)] = PA.alloc([D], F32)
        cm_b = PA.alloc([384], BF16)
        chalf = PA.alloc([8], F32)
        ones_f = PA.alloc([128], F32)
        epsc = PA.alloc([8], F32)
        ident_b = cm_b[:, 0:128]
        bd64_b = cm_b[:, 128:256]
        perm_b = cm_b[:, 256:384]

        S.op("pool", MSET(chalf, -0.5), writes=["chalf"])
        S.op("pool", MSET(ones_f, 1.0), writes=["ones_f"])
        S.op("pool", MSET(epsc, EPS), writes=["epsc"])

        A.reset()
        Wg0 = A.alloc([8, FF], BF16)
        Wu0 = A.alloc([8, FF], BF16)
        for k in range(8):
            S.dma("pool", [(Wg0[:, k, :], w_gate[0, k * 128:(k + 1) * 128, :])], "wldg", writes=["Wg"])
            S.dma("pool", [(Wu0[:, k, :], w_up[0, k * 128:(k + 1) * 128, :])], "wldu", writes=["Wu"])
        cm_f = A.alloc([384], F32)
        S.dma("sp", [(cm_f, c_mat)], "c0", writes=["cm_f"])
        S.op("dve", CP(cm_b, cm_f), reads=["cm_f"], writes=["cm_b"])
        cT = A.alloc([24], F32)
        scT = A.alloc([24], F32)
        bmodT = A.alloc([72], F32)
        npreT = A.alloc([24], F32)
        screp = A.alloc([3, 8, 128], F32)
        wm = [A.alloc([8, 512], F32) for _ in range(2)]
        bmb = [A.alloc([D], F32) for _ in range(2)]
        gpb = [A.alloc([D], F32) for _ in range(2)]
        mtmp = [A.alloc([512], F32) for _ in range(2)]
        mT = A.alloc([6, 8, 3], F32)
        abt = A.alloc([8], F32)
        S.dma("sp", [(cT, cT_in), (bmodT, bmodT_in), (npreT, npreT_in)], "c1", writes=["cT", "bmodT", "npreT"])
        S.op("act", ACTF(scT, cT, AF.Silu), reads=["cT"], writes=["scT"])
        for r in range(3):
            for k in range(8):
                S.op("pool", TS(screp[:, r, k, :], ones_f, scT[:, k * 3 + r:k * 3 + r + 1], None, ALU.mult),
                     reads=["scT", "ones_f"], writes=["screp"])
        psM = bank(0, 144).rearrange("p (a b c) -> p a b c", b=8, c=3)
        fm_js = [0, 1, 3, 4, 6, 7]
        rb = Rot(2)
        rm = Rot(2)
        wrot = Rot(2)
        for j in range(9):
            i = j // 3
            if j not in fm_js:
                b = rb.next()
                S.dma("sp", [(bmb[b], bcast(b_mod[j * D:(j + 1) * D])),
                             (gpb[b], bcast(norm_post[i, :]))], "bg%d" % b,
                      writes=["bmb%d" % b, "gpb%d" % b])
            for half in range(2):
                sl = wrot.next()
                c0_ = j * D + half * 512
                S.dma("sp", [(wm[sl][:, k, :], w_mod[k * 128:(k + 1) * 128, c0_:c0_ + 512]) for k in range(8)],
                      "wm%d" % sl, writes=["wm%d" % sl])
                if j in fm_js:
                    jj = fm_js.index(j)
                    lst = []
                    for c in range(4):
                        for k in range(8):
                            lst.append((psM[:, jj, half * 4 + c, :], wm[sl][:, k, c * 128:(c + 1) * 128], scT[:, k * 3:k * 3 + 3],
                                        k == 0, k == 7))
                    S.op("pe", MMS(lst), reads=["wm%d" % sl, "scT"], writes=["psM"])
                else:
                    for r in ([0, 1, 2] if i == 0 else [0, 1]):
                        pb = 1 + (r * 2 + half) % 4
                        lst = [(bank(pb), screp[:, r, k, :], wm[sl][:, k, :], k == 0, k == 7) for k in range(8)]
                        S.op("pe", MMS(lst), reads=["wm%d" % sl, "screp"], writes=["ps%d" % pb])
                        m = rm.next()
                        S.op("dve", TT(mtmp[m], bank(pb), bmb[b][:, half * 512:(half + 1) * 512], ALU.add),
                             reads=["ps%d" % pb, "bmb%d" % b], writes=["mtmp%d" % m])
                        S.op("dve", STT(Gb[(i, r)][:, half * 512:(half + 1) * 512], mtmp[m], 1.0 if i == 1 else 0.5,
                                         gpb[b][:, half * 512:(half + 1) * 512], ALU.mult, ALU.mult),
                             reads=["mtmp%d" % m, "gpb%d" % b], writes=["Gb%d%d" % (i, r)])
        S.op("act", ACTF(mT, psM, AF.Identity), reads=["psM"], writes=["mT"])
        for i in range(3):
            for r in range(3):
                S.op("dve", TT(ABt[:, i, 1, r, :], mT[:, 2 * i, :, r], bmodT[:, (3 * i) * 8:(3 * i) * 8 + 8], ALU.add),
                     reads=["mT", "bmodT"], writes=["AB"])
                S.op("dve", TT(abt, mT[:, 2 * i + 1, :, r], bmodT[:, (3 * i + 1) * 8:(3 * i + 1) * 8 + 8], ALU.add),
                     reads=["mT", "bmodT"], writes=["abt"])
                S.op("dve", STT(ABt[:, i, 0, r, :], abt, 1.0, npreT[:, i * 8:i * 8 + 8], ALU.add, ALU.mult),
                     reads=["abt", "npreT"], writes=["AB"])
        S.barrier()

        def pre_stats(xt, nsub, keys_x, bufs):
            ss, rstd, junk, xn = bufs["ss"], bufs["rstd"], bufs["junk"], bufs["xn"]
            for n in range(nsub):
                S.op("act", ACTF(junk, xt[:, n, :], AF.Square, accum_out=ss[:, n:n + 1]),
                     reads=keys_x, writes=["junk", "ss"])
            S.op("pool", TS(rstd[:, 0:nsub], ss[:, 0:nsub], 1.0 / D, EPS, ALU.mult, ALU.add), reads=["ss"], writes=["rstd"])
            S.op("pool", TT(rstd[:, 0:nsub], rstd[:, 0:nsub], chalf[:, 0:nsub], ALU.pow), reads=["rstd", "chalf"],
                 writes=["rstd"])
            for n in range(nsub):
                S.op("dve", TS(xn[:, n, :], xt[:, n, :], rstd[:, n:n + 1], None, ALU.mult), reads=keys_x + ["rstd"],
                     writes=[bufs.get("shared_key", "xn%d" % n)])

        def pre_trans(nsub, i_sub, r, uT, key_u, bufs, region_rot):
            xn = bufs["xn"]
            nt = nsub * 128
            for c in range(8):
                reg, rkey = region_rot()
                lst = [(reg[:, n * 128:(n + 1) * 128], xn[:, n, c * 128:(c + 1) * 128]) for n in range(nsub)]
                S.op("pe", TRS(lst, ident_b), reads=[bufs.get("shared_key", "xn%d" % n) for n in range(nsub)] + ["cm_b"],
                     writes=[rkey])
                S.op("act", ACTF(uT[:, c, 0:nt], reg[:, 0:nt], AF.Identity, scale=ABt[:, i_sub, 0, r, c:c + 1],
                                 bias=ABt[:, i_sub, 1, r, c:c + 1]), reads=[rkey, "AB"], writes=[key_u])

        def pre_norm(xt, nsub, i_sub, r, uT, keys_x, key_u, bufs, pbank_rot):
            pre_stats(xt, nsub, keys_x, bufs)
            pre_trans(nsub, i_sub, r, uT, key_u, bufs, pbank_rot)

        def post_resid(psY, ykey, xt_n, key_x, gb, gbkey, bufs):
            junk, ss2, rstd2, yt = bufs["junk"], bufs["ss2"], bufs["rstd2"], bufs["yt"]
            S.op("act", ACTF(junk, psY, AF.Square, accum_out=ss2[:, 0:1]), reads=[ykey], writes=["junk", "ss2"])
            ytk = bufs.get("shared_key", "yt")
            S.op("dve", TT(yt, psY, gb, ALU.mult), reads=[ykey, gbkey, "ss2"], writes=[ytk])
            S.op("pool", TS(rstd2[:, 0:1], ss2[:, 0:1], 1.0 / D, EPS, ALU.mult, ALU.add), reads=["ss2"], writes=["rstd2"])
            S.op("pool", TT(rstd2[:, 0:1], rstd2[:, 0:1], chalf[:, 0:1], ALU.pow), reads=["rstd2", "chalf"], writes=["rstd2"])
            S.op("dve", STT(xt_n, yt, rstd2[:, 0:1], xt_n, ALU.mult, ALU.add), reads=[ytk, "rstd2", key_x], writes=[key_x])

        def ffn_phase(fi, tiles, preloaded=False):
            A.reset()
            Wg = A.alloc([8, FF], BF16)
            Wu = A.alloc([8, FF], BF16)
            Wd = A.alloc([NFC, D], BF16)
            xts = [A.alloc([2, D], F32) for _ in range(2)]
            uT = A.alloc([8, 256], BF16)
            hT = A.alloc([NFC, 256], BF16)
            sg = [A.alloc([256], BF16) for _ in range(2)]
            shr = A.alloc([D], F32)
            bufs = {"ss": A.alloc([8], F32), "rstd": A.alloc([8], F32), "junk": A.alloc([D], BF16),
                    "xn": shr.bitcast(BF16).rearrange("p (a b) -> p a b", b=D), "ss2": A.alloc([8], F32),
                    "rstd2": A.alloc([8], F32), "yt": shr, "shared_key": "shr"}
            for k in range(8):
                if preloaded:
                    break
                S.dma("pool", [(Wg[:, k, :], w_gate[fi, k * 128:(k + 1) * 128, :])], "wldg", writes=["Wg"])
                S.dma("pool", [(Wu[:, k, :], w_up[fi, k * 128:(k + 1) * 128, :])], "wldu", writes=["Wu"])
            for f0 in range(0, NFC, 2):
                S.dma("pool", [(Wd[:, f, :], w_down[fi, f * 128:(f + 1) * 128, :]) for f in (f0, f0 + 1)], "wldd",
                      writes=["Wd"])
            psYs = [ps[:, 4 * 512:6 * 512], ps[:, 6 * 512:8 * 512]]
            trot = Rot(2)
            grot = Rot(2)
            srot = Rot(2)
            yrot = Rot(2)

            def tr_region():
                i_ = trot.next()
                return bankb(i_, 256), "ps%d" % i_

            def f_stats(ti):
                src, dst, r = tiles[ti]
                sl = ti % 2
                S.dma("sp", [(xts[sl], src.rearrange("(n p) d -> p n d", p=128))], "xld%d" % sl, writes=["xt%d" % sl])
                pre_stats(xts[sl], 2, ["xt%d" % sl], bufs)

            def f_trans(ti):
                src, dst, r = tiles[ti]
                pre_trans(2, 2 * fi, r, uT, "uT", bufs, tr_region)

            def f_gu(ti):
                for f in range(NFC):
                    pb = 2 + grot.next()
                    lst = [(bank(pb, 256, 0), Wg[:, k, f * 128:(f + 1) * 128], uT[:, k, :], k == 0, k == 7) for k in range(8)]
                    lst += [(bank(pb, 256, 256), Wu[:, k, f * 128:(f + 1) * 128], uT[:, k, :], k == 0, k == 7) for k in range(8)]
                    S.op("pe", MMS(lst), reads=["Wg", "Wu", "uT"], writes=["ps%d" % pb])
                    s_ = srot.next()
                    S.op("act", ACTF(sg[s_], bank(pb, 256, 0), AF.Silu), reads=["ps%d" % pb], writes=["sg%d" % s_])
                    S.op("dve", TT(hT[:, f, :], bank(pb, 256, 256), sg[s_], ALU.mult), reads=["ps%d" % pb, "sg%d" % s_],
                         writes=["hT"])

            def f_down(ti):
                src, dst, r = tiles[ti]
                sl = ti % 2
                xt = xts[sl]
                kx = "xt%d" % sl
                for n in range(2):
                    y_ = yrot.next()
                    psY = psYs[y_]
                    lst = []
                    for half in range(2):
                        lst += [(psY[:, half * 512:(half + 1) * 512], hT[:, f, n * 128:(n + 1) * 128],
                                 Wd[:, f, half * 512:(half + 1) * 512], f == 0, f == NFC - 1) for f in range(NFC)]
                    S.op("pe", MMS(lst), reads=["hT", "Wd"], writes=["psY%d" % y_])
                    post_resid(psY, "psY%d" % y_, xt[:, n, :], kx, Gb[(2 * fi, r)], "Gb", bufs)
                S.dma("sp", [(dst.rearrange("(n p) d -> p n d", p=128), xt)], "xst%d" % sl, reads=[kx])

            f_stats(0)
            f_trans(0)
            for ti in range(len(tiles)):
                if ti + 1 < len(tiles):
                    f_stats(ti + 1)
                f_gu(ti)
                if ti + 1 < len(tiles):
                    f_trans(ti + 1)
                f_down(ti)
            S.barrier()

        tiles = []
        for s in range(2):
            tiles.append((ctx_in[s, :, :], X1[s, 0:TC, :], 2))
            for t0 in range(0, TL, 256):
                tiles.append((x_in[s, t0:t0 + 256, :], X1[s, TC + t0:TC + t0 + 256, :], s))
        ffn_phase(0, tiles, preloaded=True)

        A.reset()
        Win = A.alloc([8, DIN], BF16)
        rps = [A.alloc([2, 512], F32) for _ in range(2)]
        lbt = A.alloc([2, 2, 512], F32)
        qkg = A.alloc([2], F32)
        xts = [A.alloc([4, D], F32) for _ in range(1)]
        uTs = [A.alloc([8, 512], BF16) for _ in range(2)]
        bufs = {"ss": A.alloc([8], F32), "rstd": A.alloc([8], F32), "junk": A.alloc([D], BF16),
                "xn": A.alloc([4, D], BF16)}
        sqs = [A.alloc([512], BF16) for _ in range(2)]
        msqs = [A.alloc([512], F32) for _ in range(2)]
        xgs = [A.alloc([512], BF16) for _ in range(2)]
        t12 = A.alloc([4, 512], F32)
        t1s = [t12[:, 0, :], t12[:, 1, :]]
        t2s = [t12[:, 2, :], t12[:, 3, :]]
        lbl = t12.rearrange("p (a b) c -> p a b c", b=2)
        fo = [A.alloc([512], BF16) for _ in range(3)]
        tsig = [A.alloc([512], F32) for _ in range(2)]
        tv = [A.alloc([512], BF16) for _ in range(2)]
        tv2 = [A.alloc([128], BF16) for _ in range(2)]
        for k in range(8):
            S.dma("pool", [(Win[:, k, :], w_in[k * 128:(k + 1) * 128, :])], "wldi", writes=["Win"])
        S.dma("sp", [(qkg, qk_gain_in)], "c2r", writes=["qkg"])
        lblk = [["t10", "t11"], ["t20", "t21"]]
        for d in range(2):
            S.dma("sp", [(lbl[:, d, sl_, :], bcast(lbraw[d, sl_, :])) for sl_ in range(2)], "c2l",
                  writes=lblk[d])
        for d in range(2):
            S.op("dve", TT(lbl[:, d, 0, :], lbl[:, d, 0, :], lbl[:, d, 1, :], ALU.subtract), reads=lblk[d], writes=[lblk[d][0]])
            S.op("act", ACTF(lbt[:, d, 0, :], lbl[:, d, 0, :], AF.Sigmoid), reads=[lblk[d][0], "lbt"], writes=["lbt"])
            S.op("act", ACTF(lbt[:, d, 1, :], lbl[:, d, 0, :], AF.Sigmoid, scale=-1.0), reads=[lblk[d][0], "lbt"], writes=["lbt"])

        frot = Rot(3)
        prot = Rot(4)
        trot = Rot(2)
        qrot = Rot(2)
        tkr = Rot(2)
        crot = Rot(2)

        def tr_region2():
            i_ = trot.next()
            return bankb(i_, 512), "ps%d" % i_

        def fm_proj(col0, nt, u_):
            pb = 2 + prot.next()
            lst = [(bank(pb, nt), Win[:, k, col0:col0 + 128], uTs[u_][:, k, 0:nt], k == 0, k == 7) for k in range(8)]
            S.op("pe", MMS(lst), reads=["Win", "uT%d" % u_], writes=["ps%d" % pb])
            return pb

        def qk_A(col0, nt, u_):
            pb = fm_proj(col0, nt, u_)
            P1 = bank(pb, nt)
            c_ = crot.next()
            sq, msq = sqs[c_], msqs[c_]
            S.op("act", ACTF(sq[:, 0:nt], P1, AF.Square), reads=["ps%d" % pb], writes=["sq%d" % c_])
            q2 = 6 + qrot.next()
            S.op("pe", MMS([(bank(q2, nt), bd64_b, sq[:, 0:nt], True, True)]), reads=["sq%d" % c_, "cm_b"], writes=["ps%d" % q2])
            S.op("act", ACTF(msq[:, 0:nt], bank(q2, nt), AF.Ln, bias=epsc[:, 0:1]), reads=["ps%d" % q2, "epsc"],
                 writes=["msq%d" % c_])
            S.op("act", ACTF(msq[:, 0:nt], msq[:, 0:nt], AF.Exp, scale=-0.5), reads=["msq%d" % c_], writes=["msq%d" % c_])
            return (pb, c_)

        def qk_B(st_, nt, gcol, rp_, dst):
            pb, c_ = st_
            P1 = bank(pb, nt)
            msq, xg, t1, t2 = msqs[c_], xgs[c_], t1s[c_], t2s[c_]
            f_ = frot.next()
            if rp_ is None:
                S.op("dve", STT(fo[f_][:, 0:nt], P1, qkg[:, gcol:gcol + 1], msq[:, 0:nt], ALU.mult, ALU.mult),
                     reads=["ps%d" % pb, "msq%d" % c_, "qkg"], writes=["fo%d" % f_])
            else:
                S.op("dve", STT(xg[:, 0:nt], P1, qkg[:, gcol:gcol + 1], msq[:, 0:nt], ALU.mult, ALU.mult),
                     reads=["ps%d" % pb, "msq%d" % c_, "qkg"], writes=["xg%d" % c_])
                q3 = 6 + qrot.next()
                S.op("pe", MMS([(bank(q3, nt), perm_b, xg[:, 0:nt], True, True)]), reads=["xg%d" % c_, "cm_b"], writes=["ps%d" % q3])
                S.op("pool", TT(t1[:, 0:nt], xg[:, 0:nt], rps[rp_][:, 0, 0:nt], ALU.mult), reads=["xg%d" % c_, "rp%d" % rp_],
                     writes=["t1%d" % c_])
                S.op("dve", TT(t2[:, 0:nt], bank(q3, nt), rps[rp_][:, 1, 0:nt], ALU.mult), reads=["ps%d" % q3, "rp%d" % rp_],
                     writes=["t2%d" % c_])
                S.op("dve", TT(fo[f_][:, 0:nt], t1[:, 0:nt], t2[:, 0:nt], ALU.add), reads=["t1%d" % c_, "t2%d" % c_],
                     writes=["fo%d" % f_])
            S.dma("sp", [(dst, fo[f_][:, 0:nt])], "fo%d" % f_, reads=["fo%d" % f_])

        def fm_act(col0, nt, func, dst, u_, scale_after=None):
            pb = fm_proj(col0, nt, u_)
            f_ = frot.next()
            S.op("act", ACTF(fo[f_][:, 0:nt], bank(pb, nt), func), reads=["ps%d" % pb], writes=["fo%d" % f_])
            if scale_after is not None:
                S.op("pool", TS(fo[f_][:, 0:nt], fo[f_][:, 0:nt], scale_after, None, ALU.mult), reads=["fo%d" % f_],
                     writes=["fo%d" % f_])
            S.dma("sp", [(dst, fo[f_][:, 0:nt])], "fo%d" % f_, reads=["fo%d" % f_])

        def tm_proj(col0, ncols, n, u_):
            pb = 2 + prot.next()
            lst = [(bank(pb, ncols), uTs[u_][:, k, n * 128:(n + 1) * 128], Win[:, k, col0:col0 + ncols], k == 0, k == 7)
                   for k in range(8)]
            S.op("pe", MMS(lst), reads=["Win", "uT%d" % u_], writes=["ps%d" % pb])
            return pb

        p2tiles = []
        for s in range(2):
            p2tiles.append((s, 0, TC, False))
            for t0 in range(0, TL, 512):
                p2tiles.append((s, TC + t0, 512, True))

        def p2_load(ti):
            s, tg0, nt, lat = p2tiles[ti]
            nsub = nt // 128
            u_ = ti % 2
            S.dma("sp", [(xts[0][:, 0:nsub, :], X1[s, tg0:tg0 + nt, :].rearrange("(n p) d -> p n d", p=128))], "xld0",
                  writes=["xt0"])
            if lat:
                lt0 = tg0 - TC
                S.dma("sp", [(rps[u_], c_rope[:, :, lt0:lt0 + 512])], "rp%d" % u_, writes=["rp%d" % u_])

        def p2_stats(ti):
            s, tg0, nt, lat = p2tiles[ti]
            pre_stats(xts[0], nt // 128, ["xt0"], bufs)

        def p2_trans(ti):
            s, tg0, nt, lat = p2tiles[ti]
            nsub = nt // 128
            u_ = ti % 2
            pre_trans(nsub, 1, (s if lat else 2), uTs[u_], "uT%d" % u_, bufs, tr_region2)

        p2_load(0)
        p2_stats(0)
        p2_trans(0)
        for ti, (s, tg0, nt, lat) in enumerate(p2tiles):
            nsub = nt // 128
            u_ = ti % 2
            lt0 = tg0 - TC
            if ti + 1 < len(p2tiles):
                p2_load(ti + 1)
            if lat:
                for h in range(4):
                    fm_act(OHQ + h * 128, nt, AF.Silu, HQT[s, h, :, lt0:lt0 + nt], u_)
                for n in range(nsub):
                    pb = tm_proj(OHG, 512, n, u_)
                    k_ = tkr.next()
                    S.op("act", ACTF(tv[k_], bank(pb), AF.Silu), reads=["ps%d" % pb], writes=["tv%d" % k_])
                    S.dma("sp", [(HG[s, lt0 + n * 128:lt0 + (n + 1) * 128, :], tv[k_])], "tv%d" % k_, reads=["tv%d" % k_])
            chunks = []
            if lat:
                for j in range(4):
                    chunks.append((OQ + j * 128, 0, u_, QT[s, j, :, lt0:lt0 + nt]))
            chunks.append((OK_, 1, (u_ if lat else None), KT[s, :, tg0:tg0 + nt]))
            prev = None
            for (col0_, gcol_, rp__, dst_) in chunks:
                st_ = qk_A(col0_, nt, u_)
                if prev is not None:
                    qk_B(prev[0], nt, prev[1], prev[2], prev[3])
                prev = (st_, gcol_, rp__, dst_)
            qk_B(prev[0], nt, prev[1], prev[2], prev[3])
            if ti + 1 < len(p2tiles):
                p2_stats(ti + 1)
                p2_trans(ti + 1)
            for n in range(nsub):
                pb = tm_proj(OV, 128, n, u_)
                k_ = tkr.next()
                S.op("act", ACTF(tv2[k_], bank(pb, 128), AF.Identity), reads=["ps%d" % pb], writes=["tv2%d" % k_])
                S.dma("sp", [(VV[s, tg0 + n * 128:tg0 + (n + 1) * 128, :], tv2[k_])], "tv2%d" % k_, reads=["tv2%d" % k_])
                pb = tm_proj(OHV, 512, n, u_)
                S.op("dve", CP(tv[k_], bank(pb)), reads=["ps%d" % pb], writes=["tv%d" % k_])
                S.dma("sp", [(HV[s, tg0 + n * 128:tg0 + (n + 1) * 128, :], tv[k_])], "tv%d" % k_, reads=["tv%d" % k_])
            if lat:
                for c in range(16):
                    fm_act(OGA + c * 128, nt, AF.Sigmoid, SG[s, c, :, lt0:lt0 + nt], u_)
            for n in range(nsub):
                for d in range(2):
                    pb = tm_proj(OFF + d * 512, 512, n, u_)
                    k_ = tkr.next()
                    S.op("act", ACTF(tsig[k_], bank(pb), AF.Sigmoid, scale=-1.0), reads=["ps%d" % pb], writes=["tsig%d" % k_])
                    S.op("dve", TT(tsig[k_], tsig[k_], lbt[:, d, 1, :], ALU.mult), reads=["tsig%d" % k_, "lbt"],
                         writes=["tsig%d" % k_])
                    rows = slice(tg0 + n * 128, tg0 + (n + 1) * 128)
                    S.dma("sp", [(KF[s, d, rows, :], tsig[k_])], "tgk%d" % k_, reads=["tsig%d" % k_])
        S.barrier()

        A.reset()
        chg = A.alloc([2, 640], F32)
        hgn = A.alloc([128], F32)
        Obuf = A.alloc([16, 512], F32)
        Sst = A.alloc([2, 4, 128], F32)
        SbfIn = [A.alloc([4, 128], BF16) for _ in range(2)]
        SbfMid = A.alloc([4, 128], BF16)
        NL = 5
        kts = [A.alloc([512], F32) for _ in range(NL)]
        vts = [A.alloc([512], BF16) for _ in range(NL)]
        hqs = [A.alloc([4, 128], BF16) for _ in range(NL)]
        ogs = [A.alloc([512], BF16) for _ in range(NL)]
        E1 = A.alloc([512], F32)
        E2 = A.alloc([512], F32)
        kds = A.alloc([512], BF16)
        kd2 = A.alloc([512], BF16)
        kd2T = A.alloc([4, 128], BF16)
        qdT = A.alloc([4, 128], BF16)
        EFs = [A.alloc([4, 256], F32) for _ in range(2)]
        dsbs = [A.alloc([2, 4, 128], F32) for _ in range(2)]
        qepads = [A.alloc([4, 2, 128], BF16) for _ in range(2)]
        sTms = [A.alloc([4, 128], BF16) for _ in range(2)]
        osums = [A.alloc([512], F32) for _ in range(2)]
        hjunk = A.alloc([128], F32)
        hss = A.alloc([8], F32)
        hrs = A.alloc([8], F32)
        gg = A.alloc([512], F32)
        on = A.alloc([512], BF16)
        ohT = [A.alloc([4, 128], BF16) for _ in range(2)]
        S.dma("sp", [(chg, c_hg), (hgn, bcast(hg_norm))], "c3", writes=["chg", "hgn"])
        r32 = r32_t[:, :].bitcast(F32R)
        chr_r = r32[:, 0:1024].rearrange("p (a b) -> p a b", b=512)
        gt_r = r32[:, 1024:1536]
        for d_ in range(2):
            S.op("dve", CP(chr_r[:, d_, :], chg[:, d_, 0:512]), reads=["chg"], writes=["chr"])
        for q_ in range(2):
            S.op("pool", MSET(qepads[q_], 0.0), writes=["qepad%d" % q_])
        PE_, PT_, PFa, PFb, PST, POh, PD0, PD1 = range(8)
        lrot = Rot(NL)
        orot = Rot(2)

        items = []
        for s in range(2):
            for d in range(2):
                order = list(range(18)) if d == 0 else [1, 0] + list(range(17, 1, -1))
                for oi, sub in enumerate(order):
                    items.append((s, d, sub, oi == 0))

        def hg_load(it):
            s, d, sub, is_first = items[it]
            lat = sub >= 2
            rows = slice(sub * 128, (sub + 1) * 128)
            lt0 = (sub - 2) * 128
            l_ = lrot.next()
            pairs = [(kts[l_], KF[s, d, rows, :]), (vts[l_], HV[s, rows, :])]
            wk = ["kt%d" % l_, "vt%d" % l_]
            if lat:
                pairs.append((hqs[l_], HQT[s, :, :, lt0:lt0 + 128].rearrange("h p t -> p h t")))
                wk.append("hq%d" % l_)
                if d == 1:
                    pairs.append((ogs[l_], HG[s, lt0:lt0 + 128, :]))
                    wk.append("og%d" % l_)
            S.dma("sp", pairs, "hl%d" % l_, writes=wk)
            return l_

        def hg_A1(it, slot, l_):
            s, d, sub, is_first = items[it]
            M1 = chr_r[:, d, 0:128]
            M2 = chr_r[:, d, 128:256]
            Rm = chr_r[:, d, 256:512]
            mask = chg[:, d, 512:640]
            lat = sub >= 2
            rows = slice(sub * 128, (sub + 1) * 128)
            lt0 = (sub - 2) * 128
            kt, vt, hq, og = kts[l_], vts[l_], hqs[l_], ogs[l_]
            EF, dsb, qepad, sTm = EFs[slot], dsbs[slot], qepads[slot], sTms[slot]
            gt = gt_r
            S.op("act", ACTF(gt, kt, AF.Ln, scale=-1.0, bias=ones_f[:, 0:1]), reads=["kt%d" % l_, "ones_f"],
                 writes=["gtr"])
            S.op("pe", MMS([(bank(PE_), M1, gt, True, True)]), reads=["gtr", "chr"], writes=["ps0"])
            S.op("act", ACTF(E1, bank(PE_), AF.Exp), reads=["ps0"], writes=["E1"])
            S.op("pool", TT(kds, kt, E1, ALU.mult), reads=["kt%d" % l_, "E1"], writes=["kds"])
            if lat:
                S.op("pe", MMS([(bank(PE_), M2, gt, True, True)]), reads=["gtr", "chr"], writes=["ps0"])
                S.op("act", ACTF(E2, bank(PE_), AF.Exp), reads=["ps0"], writes=["E2"])
                S.op("pool", TT(kd2, kt, E2, ALU.mult), reads=["kt%d" % l_, "E2"], writes=["kd2"])
            lst = []
            for h in range(4):
                pbF = PFa if h < 2 else PFb
                lst.append((bank(pbF, 256, (h % 2) * 256), gt[:, h * 128:(h + 1) * 128], Rm, True, True))
            S.op("pe", MMS(lst), reads=["gtr", "chr"], writes=["ps2", "ps3"])
            S.op("act", ACTF(EF[:, 0:2, :], bank(PFa).rearrange("p (h t) -> p h t", t=256), AF.Exp), reads=["ps2"],
                 writes=["EF%d" % slot])
            S.op("act", ACTF(EF[:, 2:4, :], bank(PFb).rearrange("p (h t) -> p h t", t=256), AF.Exp), reads=["ps3", "EF%d" % slot],
                 writes=["EF%d" % slot])
            lst = []
            for j in range(2):
                for h in range(4):
                    lst.append((bank(PD0 + j, 128, h * 128), kds[j * 64:(j + 1) * 64, h * 128:(h + 1) * 128],
                                vt[j * 64:(j + 1) * 64, h * 128:(h + 1) * 128], True, True))
            S.op("pe", MMS(lst), reads=["kds", "vt%d" % l_], writes=["ps6", "ps7"])
            S.op("act", ACTF(dsb.rearrange("p a b c -> p (a b c)"), ps[:, 6 * 512:8 * 512], AF.Identity), reads=["ps6", "ps7"],
                 writes=["dsb%d" % slot])

        def hg_A2(it, slot, l_):
            s, d, sub, is_first = items[it]
            mask = chg[:, d, 512:640]
            lat = sub >= 2
            hq = hqs[l_]
            EF, qepad, sTm = EFs[slot], qepads[slot], sTms[slot]
            if lat:
                S.op("pe", TRS([(bankb(PT_, 128, h * 128), kd2[:, h * 128:(h + 1) * 128]) for h in range(4)], ident_b),
                     reads=["kd2", "cm_b"], writes=["ps1"])
                S.op("dve", CP(kd2T, bankb(PT_, 512).rearrange("p (h t) -> p h t", t=128)), reads=["ps1"], writes=["kd2T"])
                S.op("dve", STT(qdT, EF[:, :, 0:128], HG_SCALE, hq, ALU.mult, ALU.mult), reads=["EF%d" % slot, "hq%d" % l_],
                     writes=["qdT"])
                for j in range(2):
                    S.op("dve", STT(qepad[:, :, j, j * 64:(j + 1) * 64], EF[:, :, 128 + j * 64:128 + (j + 1) * 64], HG_SCALE,
                                    hq[:, :, j * 64:(j + 1) * 64], ALU.mult, ALU.mult),
                         reads=["EF%d" % slot, "hq%d" % l_, "qepad%d" % slot], writes=["qepad%d" % slot])
                lst = [(bank(PST, 128, h * 128), kd2T[:, h, :], qdT[:, h, :], True, True) for h in range(4)]
                S.op("pe", MMS(lst), reads=["kd2T", "qdT"], writes=["ps4"])
                S.op("dve", TT(sTm, bank(PST).rearrange("p (h t) -> p h t", t=128),
                               bass.AP(mask.tensor, mask.offset, [list(mask.ap[0]), [0, 4], list(mask.ap[-1])]), ALU.mult),
                     reads=["ps4", "chg"], writes=["sTm%d" % slot])

        def hg_B1(it, slot):
            s, d, sub, is_first = items[it]
            first, second = (0, 1) if d == 0 else (1, 0)
            lastpos = {0: (63 if d == 0 else 0), 1: (127 if d == 0 else 64)}
            EF, dsb = EFs[slot], dsbs[slot]
            sin, sout = "SbfIn%d" % (it % 2), "SbfIn%d" % ((it + 1) % 2)
            if is_first:
                S.op("pool", MSET(Sst[:, 0, :, :], 0.0), reads=["Sst0"], writes=["Sst0"])
                S.op("pool", MSET(SbfIn[it % 2], 0.0), reads=[sin], writes=[sin])
            for h in range(4):
                S.op("dve", STT(Sst[:, 1, h, :], Sst[:, 0, h, :], EF[:, h, 128 + lastpos[first]:129 + lastpos[first]],
                                dsb[:, first, h, :], ALU.mult, ALU.add),
                     reads=["Sst0", "EF%d" % slot, "dsb%d" % slot, "Sst1"], writes=["Sst1"])
            S.op("act", ACTF(SbfMid, Sst[:, 1, :, :], AF.Identity), reads=["Sst1", "SbfMid"], writes=["SbfMid"])
            for h in range(4):
                S.op("dve", STT(Sst[:, 0, h, :], Sst[:, 1, h, :], EF[:, h, 128 + lastpos[second]:129 + lastpos[second]],
                                dsb[:, second, h, :], ALU.mult, ALU.add),
                     reads=["Sst1", "EF%d" % slot, "dsb%d" % slot, "Sst0"], writes=["Sst0"])
            S.op("act", ACTF(SbfIn[(it + 1) % 2], Sst[:, 0, :, :], AF.Identity), reads=["Sst0", sout], writes=[sout])

        def hg_B2(it, slot, l_):
            s, d, sub, is_first = items[it]
            if sub < 2:
                return
            first, second = (0, 1) if d == 0 else (1, 0)
            vt = vts[l_]
            qepad, sTm = qepads[slot], sTms[slot]
            sin = "SbfIn%d" % (it % 2)
            lst = []
            for h in range(4):
                o_ = bank(POh, 128, h * 128)
                lst.append((o_, sTm[:, h, :], vt[:, h * 128:(h + 1) * 128], True, False))
                lst.append((o_, qepad[:, h, first, :], SbfIn[it % 2][:, h, :], False, False))
                lst.append((o_, qepad[:, h, second, :], SbfMid[:, h, :], False, True))
            S.op("pe", MMS(lst), reads=["sTm%d" % slot, "vt%d" % l_, "qepad%d" % slot, sin, "SbfMid"], writes=["ps5"])
            li = sub - 2
            if d == 0:
                S.op("act", ACTF(Obuf[:, li, :], bank(POh), AF.Identity), reads=["ps5"], writes=["Obuf%d" % li])
            else:
                S.op("dve", TT(osums[slot], bank(POh), Obuf[:, li, :], ALU.add), reads=["ps5", "Obuf%d" % li],
                     writes=["osum%d" % slot])

        def hg_C(it, slot, l_):
            s, d, sub, is_first = items[it]
            if sub < 2 or d == 0:
                return
            lt0 = (sub - 2) * 128
            og = ogs[l_]
            osum = osums[slot]
            ok_ = "osum%d" % slot
            for h in range(4):
                S.op("act", ACTF(hjunk, osum[:, h * 128:(h + 1) * 128], AF.Square, accum_out=hss[:, h:h + 1]),
                     reads=[ok_], writes=["hjunk", "hss"])
            S.op("pool", TS(hrs[:, 0:4], hss[:, 0:4], 1.0 / 128, EPS, ALU.mult, ALU.add), reads=["hss"], writes=["hrs"])
            S.op("pool", TT(hrs[:, 0:4], hrs[:, 0:4], chalf[:, 0:4], ALU.pow), reads=["hrs", "chalf"], writes=["hrs"])
            for h in range(4):
                S.op("pool", TT(gg[:, h * 128:(h + 1) * 128], og[:, h * 128:(h + 1) * 128], hgn, ALU.mult),
                     reads=["og%d" % l_, "hgn", "gg"], writes=["gg"])
                S.op("dve", STT(on[:, h * 128:(h + 1) * 128], osum[:, h * 128:(h + 1) * 128], hrs[:, h:h + 1],
                                gg[:, h * 128:(h + 1) * 128], ALU.mult, ALU.mult),
                     reads=[ok_, "hrs", "gg", "on"], writes=["on"])
            S.op("pe", TRS([(bankb(PT_, 128, h * 128), on[:, h * 128:(h + 1) * 128]) for h in range(4)], ident_b),
                 reads=["on", "cm_b"], writes=["ps1"])
            o2 = orot.next()
            S.op("act", ACTF(ohT[o2], bankb(PT_, 512).rearrange("p (h t) -> p h t", t=128), AF.Identity),
                 reads=["ps1"], writes=["ohT%d" % o2])
            S.dma("sp", [(OHT[s, :, :, lt0:lt0 + 128].rearrange("h p t -> p h t"), ohT[o2])], "oh%d" % o2,
                  reads=["ohT%d" % o2])

        lslot = {}
        lslot[0] = hg_load(0)
        lslot[1] = hg_load(1)
        hg_A1(0, 0, lslot[0])
        hg_A2(0, 0, lslot[0])
        for it in range(len(items)):
            if it + 2 < len(items):
                lslot[it + 2] = hg_load(it + 2)
            nxt = it + 1 < len(items)
            if nxt:
                hg_A1(it + 1, (it + 1) % 2, lslot[it + 1])
            hg_B1(it, it % 2)
            if nxt:
                hg_A2(it + 1, (it + 1) % 2, lslot[it + 1])
            hg_B2(it, it % 2, lslot[it])
            if it >= 1:
                hg_C(it - 1, (it - 1) % 2, lslot[it - 1])
        hg_C(len(items) - 1, (len(items) - 1) % 2, lslot[len(items) - 1])
        S.barrier()

        A.reset()
        kTr = [A.alloc([T], BF16) for _ in range(2)]
        vaugE = A.alloc([18, 2, 128], BF16)
        vaugO = A.alloc([18, 2, 128], BF16)
        qTs = [A.alloc([4, 512], BF16) for _ in range(2)]
        pTs = [A.alloc([2, 512], BF16) for _ in range(3)]
        rrows = [A.alloc([512], F32) for _ in range(2)]
        rrot = Rot(2)
        bsbs = [A.alloc([512], F32) for _ in range(2)]
        oTs = [A.alloc([512], BF16) for _ in range(2)]
        S.op("pool", MSET(vaugE, 1.0), writes=["vaugE"])
        S.op("pool", MSET(vaugO, 1.0), writes=["vaugO"])
        srot = Rot(2)
        orot = Rot(4)
        prot2 = Rot(3)
        qrot2 = Rot(2)
        otr = Rot(2)
        for s in range(2):
            for g in range(2):
                S.dma("sp", [(kTr[g][0:64, :], KT[s, g * 64:(g + 1) * 64, :]), (kTr[g][64:128, :], KT[s, g * 64:(g + 1) * 64, :])],
                      "kl%d" % g, writes=["kTr%d" % g])
                for c0 in range(0, 18, 6):
                    src_v = VV[s, c0 * 128:(c0 + 6) * 128, g * 64:(g + 1) * 64].rearrange("(c p) d -> p c d", p=128)
                    S.dma("sp", [(vaugE[:, c0:c0 + 6, g, 0:64], src_v)], "vle", writes=["vaugE"])
                    S.dma("sp", [(vaugO[:, c0:c0 + 6, g, 64:128], src_v)], "vlo", writes=["vaugO"])
            LOOK = 1
            steps = [(qt, j, c) for qt in range(4) for j in range(4) for c in range(18)]
            qinfo = {}
            sslot = {}
            pobank = {}

            def emit_S(i):
                qt, j, c = steps[i]
                if j == 0 and c == 0:
                    q_ = qrot2.next()
                    qinfo[qt] = q_
                    S.dma("sp", [(qTs[q_], QT[s, :, :, qt * 512:(qt + 1) * 512].rearrange("j p t -> p j t"))], "ql%d" % q_,
                          writes=["qT%d" % q_])
                q_ = qinfo[qt]
                g = j // 2
                r_ = srot.next()
                sslot[i] = r_
                lst = [(bank(2 * r_ + hh), kTr[g][hh * 64:(hh + 1) * 64, c * 128:(c + 1) * 128],
                        qTs[q_][hh * 64:(hh + 1) * 64, j, :], True, True) for hh in range(2)]
                S.op("pe", MMS(lst), reads=["kTr%d" % g, "qT%d" % q_], writes=["ps%d" % (2 * r_), "ps%d" % (2 * r_ + 1)])

            def emit_rest(i):
                qt, j, c = steps[i]
                g = j // 2
                if c == 0:
                    pobank[(qt, j)] = (4 + orot.next(), 4 + orot.next())
                po = pobank[(qt, j)]
                r_ = sslot[i]
                p_ = prot2.next()
                S.op("act", ACTF(pTs[p_].rearrange("p a b -> p (a b)"), ps[:, r_ * 1024:(r_ + 1) * 1024], AF.Exp, scale=ATT_SCALE),
                     reads=["ps%d" % (2 * r_), "ps%d" % (2 * r_ + 1)], writes=["pT%d" % p_])
                lst = [(bank(po[0]), vaugE[:, c, g, :], pTs[p_][:, 0, :], c == 0, c == 17),
                       (bank(po[1]), vaugO[:, c, g, :], pTs[p_][:, 1, :], c == 0, c == 17)]
                S.op("pe", MMS(lst), reads=["vaugE", "vaugO", "pT%d" % p_], writes=["ps%d" % po[0], "ps%d" % po[1]])
                if c == 17:
                    rr_ = rrot.next()
                    for hh in range(2):
                        p0 = 64 if hh == 0 else 0
                        pb_ = po[hh]
                        S.op("dve", (lambda pb_, p0, rr_: lambda e: e.reciprocal(out=rrows[rr_][p0:p0 + 1, :],
                                                                                  in_=ps[p0:p0 + 1, pb_ * 512:(pb_ + 1) * 512]))(pb_, p0, rr_),
                             reads=["ps%d" % pb_], writes=["rrow%d%d" % (rr_, hh)])
                    return (qt, j, po, rr_)
                return None

            def emit_fin(pend):
                qt, j, po, rr_ = pend
                r_ = (srot.i + 1) % srot.n
                o_ = otr.next()
                for hh in range(2):
                    p0 = 64 if hh == 0 else 0
                    pb_ = po[hh]
                    bb = 2 * r_ + hh
                    S.op("pe", MMS([(bank(bb), ones_f[p0:p0 + 1, :], rrows[rr_][p0:p0 + 1, :], True, True)]),
                         reads=["rrow%d%d" % (rr_, hh), "ones_f"], writes=["ps%d" % bb])
                    hs = slice(hh * 64, (hh + 1) * 64)
                    S.op("act", ACTF(bsbs[hh][hs, :], ps[hs, bb * 512:(bb + 1) * 512], AF.Identity), reads=["ps%d" % bb],
                         writes=["bsb%d" % hh])
                    S.op("dve", TT(oTs[o_][hs, :], ps[hs, pb_ * 512:(pb_ + 1) * 512], bsbs[hh][hs, :], ALU.mult),
                         reads=["ps%d" % pb_, "bsb%d" % hh, "oT%d" % o_], writes=["oT%d" % o_])
                S.dma("sp", [(OAT[s, j, :, qt * 512:(qt + 1) * 512], oTs[o_])], "ost%d" % o_, reads=["oT%d" % o_])

            for i in range(min(LOOK, len(steps))):
                emit_S(i)
            pend = None
            pend_at = None
            for i in range(len(steps)):
                if pend is not None and i - pend_at >= 8:
                    emit_fin(pend)
                    pend = None
                if i + LOOK < len(steps):
                    emit_S(i + LOOK)
                r = emit_rest(i)
                if r is not None:
                    if pend is not None:
                        emit_fin(pend)
                    pend, pend_at = r, i
            if pend is not None:
                emit_fin(pend)
        S.barrier()

        A.reset()
        Wao = A.alloc([4, D], BF16)
        Who = A.alloc([4, D], BF16)
        Wo = A.alloc([8, D], BF16)
        xts = [A.alloc([4, D], F32) for _ in range(2)]
        oaT = [A.alloc([4, 512], BF16) for _ in range(2)]
        ohTt = [A.alloc([4, 512], BF16) for _ in range(2)]
        sgT = [A.alloc([16, 512], BF16) for _ in range(2)]
        ymT = A.alloc([8, 512], BF16)
        m1 = [A.alloc([512], F32) for _ in range(2)]
        m2 = [A.alloc([512], F32) for _ in range(2)]
        bufs = {"junk": A.alloc([D], F32), "ss2": A.alloc([8], F32), "rstd2": A.alloc([8], F32), "yt": A.alloc([D], F32)}
        S.dma("pool", [(Wao, w_att_out.rearrange("(h p) n -> p h n", p=128))], "wlda", writes=["Wao"])
        S.dma("pool", [(Who, w_hg_out.rearrange("(h p) n -> p h n", p=128))], "wldh", writes=["Who"])
        for k in range(8):
            S.dma("pool", [(Wo[:, k, :], w_o[k * 128:(k + 1) * 128, :])], "wldo", writes=["Wo"])
        psYs = [ps[:, 4 * 512:6 * 512], ps[:, 6 * 512:8 * 512]]
        yrot = Rot(2)
        arot = Rot(2)
        mrot = Rot(2)
        p4tiles = [(s, t0) for s in range(2) for t0 in range(0, TL, 512)]

        def p4_load(ti):
            s, t0 = p4tiles[ti]
            sl = ti % 2
            S.dma("sp", [(xts[sl], X1[s, TC + t0:TC + t0 + 512, :].rearrange("(n p) d -> p n d", p=128)),
                         (oaT[sl], OAT[s, :, :, t0:t0 + 512].rearrange("h p t -> p h t")),
                         (ohTt[sl], OHT[s, :, :, t0:t0 + 512].rearrange("h p t -> p h t")),
                         (sgT[sl], SG[s, :, :, t0:t0 + 512].rearrange("c p t -> p c t"))],
                  "p4l%d" % sl, writes=["xt%d" % sl, "oaT%d" % sl, "ohTt%d" % sl, "sgT%d" % sl])

        p4_load(0)
        for ti, (s, t0) in enumerate(p4tiles):
            if True:
                sl = ti % 2
                xt = xts[sl]
                kx = "xt%d" % sl
                if ti + 1 < len(p4tiles):
                    p4_load(ti + 1)
                for c in range(8):
                    pa = arot.next()
                    pbk = 2 + arot.i
                    lst = [(bank(pa), Wao[:, h, c * 128:(c + 1) * 128], oaT[sl][:, h, :], h == 0, h == 3) for h in range(4)]
                    lst += [(bank(pbk), Who[:, h, c * 128:(c + 1) * 128], ohTt[sl][:, h, :], h == 0, h == 3) for h in range(4)]
                    S.op("pe", MMS(lst), reads=["Wao", "Who", "oaT%d" % sl, "ohTt%d" % sl], writes=["ps%d" % pa, "ps%d" % pbk])
                    m_ = mrot.next()
                    S.op("dve", TT(m1[m_], bank(pa), sgT[sl][:, c, :], ALU.mult), reads=["ps%d" % pa, "sgT%d" % sl], writes=["m1%d" % m_])
                    S.op("dve", TT(m2[m_], bank(pbk), sgT[sl][:, 8 + c, :], ALU.mult), reads=["ps%d" % pbk, "sgT%d" % sl],
                         writes=["m2%d" % m_])
                    S.op("pool", TT(ymT[:, c, :], m1[m_], m2[m_], ALU.add), reads=["m1%d" % m_, "m2%d" % m_, "ymT"], writes=["ymT"])
                for n in range(4):
                    y_ = yrot.next()
                    psY = psYs[y_]
                    lst = []
                    for half in range(2):
                        lst += [(psY[:, half * 512:(half + 1) * 512], ymT[:, c, n * 128:(n + 1) * 128],
                                 Wo[:, c, half * 512:(half + 1) * 512], c == 0, c == 7) for c in range(8)]
                    S.op("pe", MMS(lst), reads=["ymT", "Wo"], writes=["psY%d" % y_])
                    post_resid(psY, "psY%d" % y_, xt[:, n, :], kx, Gb[(1, s)], "Gb", bufs)
                S.dma("sp", [(X1[s, TC + t0:TC + t0 + 512, :].rearrange("(n p) d -> p n d", p=128), xt)], "xst%d" % sl, reads=[kx])
        S.barrier()

        tiles = []
        for s in range(2):
            for t0 in range(0, TL, 256):
                tiles.append((X1[s, TC + t0:TC + t0 + 256, :], out[s, t0:t0 + 256, :], s))
        ffn_phase(1, tiles)
        S.emit(nc, st)
    return nc


def _constants():
    ident = np.eye(128, dtype=np.float32)
    p = np.arange(128)
    bd64 = ((p[:, None] // 64) == (p[None, :] // 64)).astype(np.float32) / 64.0
    perm = np.zeros((128, 128), np.float32)
    for m in range(128):
        if (m % 32) < 16:
            perm[m + 16, m] = -1.0
        else:
            perm[m - 16, m] = 1.0
    c_mat = np.concatenate([ident, bd64, perm], axis=1)
    t = np.arange(TL)
    row = (t // 64).astype(np.float32)
    col = (t % 64).astype(np.float32)
    inv_freq = (np.float32(10000.0) ** (-np.arange(16, dtype=np.float32) / np.float32(16))).astype(np.float32)
    ang_r = row[:, None] * inv_freq
    ang_c = col[:, None] * inv_freq
    ang = np.concatenate([ang_r, ang_r, ang_c, ang_c], axis=-1).astype(np.float32)
    cosT = np.cos(ang).T.astype(np.float32)
    sinT = np.sin(ang).T.astype(np.float32)
    c_rope = np.stack([np.concatenate([cosT, cosT], 0), np.concatenate([sinT, sinT], 0)], axis=1)
    c_hg = np.zeros((128, 2, 640), np.float32)
    tt = np.arange(128)
    same = (tt[:, None] // 64) == (tt[None, :] // 64)
    for d in range(2):
        if d == 0:
            upto = same & (tt[:, None] <= tt[None, :])
            mid = (tt // 64) * 64 + 31
        else:
            upto = same & (tt[:, None] >= tt[None, :])
            mid = (tt // 64) * 64 + 32
        after = same & ~upto
        upto_mid = upto[:, mid]
        M1 = after.astype(np.float32)
        M2 = upto_mid.astype(np.float32) - upto.astype(np.float32)
        c_hg[:, d, 0:128] = M1
        c_hg[:, d, 128:256] = M2
        c_hg[:, d, 256:384] = -M2
        c_hg[:, d, 384:512] = upto.astype(np.float32)
        c_hg[:, d, 512:640] = upto.astype(np.float32)
    return c_mat, np.ascontiguousarray(c_rope), c_hg


_NC_CACHE = {}


def kernel(x, c, ctx, c_ctx, w_mod, b_mod, norm_pre, norm_post, ffn_w_gate, ffn_w_up, ffn_w_down,
           w_in, q_norm, k_norm, hg_lower_bound, hg_norm, w_att_out, w_hg_out, w_o, _debug=False):
    f32 = lambda a: np.ascontiguousarray(np.asarray(a, dtype=np.float32))
    x, c, ctx, c_ctx = f32(x), f32(c), f32(ctx), f32(c_ctx)
    c_mat, c_rope, c_hg = _constants()
    key = bool(_debug)
    if key not in _NC_CACHE:
        _NC_CACHE[key] = build_program(debug=_debug)
    nc = _NC_CACHE[key]
    b_mod1 = f32(b_mod)[0]
    shared = {
        "w_mod": f32(w_mod)[0], "b_mod": b_mod1,
        "bmodT": np.ascontiguousarray(b_mod1.reshape(9, 8, 128).transpose(2, 0, 1).reshape(128, 72)),
        "npreT": np.ascontiguousarray(f32(norm_pre)[0].reshape(3, 8, 128).transpose(2, 0, 1).reshape(128, 24)),
        "norm_post": f32(norm_post)[0],
        "ffn_w_gate": f32(ffn_w_gate)[0], "ffn_w_up": f32(ffn_w_up)[0], "ffn_w_down": f32(ffn_w_down)[0],
        "w_in": f32(w_in)[0],
        "qk_gain": np.ascontiguousarray(np.stack([np.tile(f32(q_norm)[0], 2), np.tile(f32(k_norm)[0], 2)], axis=1)),
        "hg_lower_bound": f32(hg_lower_bound), "hg_norm": f32(hg_norm)[0],
        "w_att_out": f32(w_att_out)[0], "w_hg_out": f32(w_hg_out)[0], "w_o": f32(w_o)[0],
        "c_mat": c_mat, "c_rope": c_rope, "c_hg": c_hg,
    }
    in_maps = []
    for i in range(8):
        cv = np.stack([c[2 * i], c[2 * i + 1], c_ctx], axis=0)
        cT = np.ascontiguousarray(cv.reshape(3, 8, 128).transpose(2, 1, 0).reshape(128, 24))
        m = dict(shared)
        m["x"] = np.ascontiguousarray(x[2 * i:2 * i + 2])
        m["ctx"] = np.ascontiguousarray(ctx[2 * i:2 * i + 2])
        m["cT"] = cT
        in_maps.append(m)
    res = run_bass_kernel_spmd(nc, in_maps, core_ids=list(range(8)))
    if _debug:
        return res
    return np.concatenate([r["out"] for r in res.results], axis=0)
```

```python
import numpy as np
from contextlib import ExitStack
import concourse.bass as bass
import concourse.mybir as mybir
from concourse.bass_utils import run_bass_kernel_spmd

F32 = mybir.dt.float32
BF16 = mybir.dt.bfloat16
F32R = mybir.dt.float32r
AF = mybir.ActivationFunctionType
ALU = mybir.AluOpType

D = 1024
FF = 2816
NFC = FF // 128
DIN = 5376
TC = 256
TL = 2048
T = TC + TL
EPS = 1e-6
ATT_SCALE = 64 ** -0.5
HG_SCALE = 128 ** -0.5
OQ, OK_, OV, OHQ, OHV, OFF, OFB, OHG, OGA = 0, 512, 640, 768, 1280, 1792, 2304, 2816, 3328


class _Op:
    __slots__ = ("eng", "fn", "deps", "need_inc", "cnt", "dma_sem", "dma_val", "ndma", "idx")


class Sched:
    COMPUTE = ("pe", "act", "dve", "pool")
    ENGS = ("pe", "act", "dve", "pool", "sp")

    def __init__(self):
        self.ops = {e: [] for e in self.ENGS}
        self.last_w = {}
        self.readers = {}
        self.dma_tot = {}
        self.all_dma = []
        self.nops = 0

    def _new(self, eng, fn):
        op = _Op()
        op.eng = eng
        op.fn = fn
        op.deps = []
        op.need_inc = False
        op.cnt = None
        op.dma_sem = None
        op.dma_val = None
        op.ndma = 0
        op.idx = self.nops
        self.nops += 1
        return op

    def _add(self, eng, fn, reads, writes, dma_key=None, ndma=0):
        op = self._new(eng, fn)
        if dma_key is not None:
            op.dma_sem = dma_key
            op.ndma = ndma
            self.dma_tot[dma_key] = self.dma_tot.get(dma_key, 0) + 16 * ndma
            op.dma_val = self.dma_tot[dma_key]
            self.all_dma.append(op)
        deps = {}
        for k in reads:
            w = self.last_w.get(k)
            if w is not None:
                deps[w.idx] = w
        for k in writes:
            w = self.last_w.get(k)
            if w is not None:
                deps[w.idx] = w
            for r in self.readers.get(k, ()):
                deps[r.idx] = r
        for d in deps.values():
            if d is op:
                continue
            if d.dma_sem is None and d.eng == "pe" and eng == "pe" and dma_key is None:
                continue
            op.deps.append(d)
            if d.dma_sem is None:
                d.need_inc = True
        for k in writes:
            self.last_w[k] = op
            self.readers[k] = []
        for k in reads:
            self.readers.setdefault(k, []).append(op)
        self.ops[eng].append(op)
        return op

    def op(self, eng, fn, reads=(), writes=()):
        return self._add(eng, fn, list(reads), list(writes))

    def dma(self, eng, pairs, sem_key, reads=(), writes=()):
        pairs = list(pairs)
        fn = lambda e: [e.dma_start(out=o, in_=i) for (o, i) in pairs]
        return self._add(eng, fn, list(reads), list(writes), dma_key=sem_key, ndma=len(pairs))

    def barrier(self):
        lasts = []
        for e in self.COMPUTE:
            for o in reversed(self.ops[e]):
                if o.dma_sem is None and o.fn is not None:
                    lasts.append(o)
                    break
        dmas = list(self.all_dma)
        self.all_dma = []
        for e in self.ENGS:
            op = self._new(e, None)
            for d in lasts:
                if d.dma_sem is None and d.eng != e and d.fn is not None:
                    d.need_inc = True
                    op.deps.append(d)
            for d in dmas:
                op.deps.append(d)
            self.ops[e].append(op)
        self.last_w = {}
        self.readers = {}

    def emit(self, nc, stack):
        esem = {e: stack.enter_context(nc.semaphore("s_" + e)) for e in self.COMPUTE}
        dsem = {k: stack.enter_context(nc.semaphore("d_" + str(k))) for k in self.dma_tot}
        for e in self.COMPUTE:
            c = 0
            for op in self.ops[e]:
                if op.need_inc:
                    c += 1
                    op.cnt = c
        ops = self.ops

        def run(e, eng):
            waited = {}
            for op in ops[e]:
                for d in op.deps:
                    if d.dma_sem is not None:
                        key, val, sem = ("d", d.dma_sem), d.dma_val, dsem[d.dma_sem]
                    else:
                        key, val, sem = ("e", d.eng), d.cnt, esem[d.eng]
                    if waited.get(key, 0) >= val:
                        continue
                    waited[key] = val
                    eng.wait_ge(sem, val)
                if op.fn is None:
                    continue
                r = op.fn(eng)
                if op.dma_sem is not None:
                    for ins in r:
                        ins.then_inc(dsem[op.dma_sem], 16)
                elif op.need_inc:
                    r.then_inc(esem[e], 1)

        with nc.Block() as block:
            @block.tensor
            def _(eng):
                run("pe", eng)

            @block.scalar
            def _(eng):
                run("act", eng)

            @block.vector
            def _(eng):
                run("dve", eng)

            @block.gpsimd
            def _(eng):
                run("pool", eng)

            @block.sync
            def _(eng):
                run("sp", eng)


def MMS(lst):
    lst = list(lst)

    def f(e):
        r = None
        for (o, l, rr, s, p) in lst:
            r = e.matmul(o, lhsT=l, rhs=rr, start=s, stop=p)
        return r
    return f


def TRS(lst, ident):
    lst = list(lst)

    def f(e):
        r = None
        for (o, i) in lst:
            r = e.transpose(out=o, in_=i, identity=ident)
        return r
    return f


def ACTF(out, in_, func, **kw):
    return lambda e: e.activation(out=out, in_=in_, func=func, **kw)


def TT(out, in0, in1, op):
    return lambda e: e.tensor_tensor(out=out, in0=in0, in1=in1, op=op)


def TS(out, in0, s1, s2, op0, op1=None):
    if op1 is None:
        return lambda e: e.tensor_scalar(out=out, in0=in0, scalar1=s1, scalar2=None, op0=op0)
    return lambda e: e.tensor_scalar(out=out, in0=in0, scalar1=s1, scalar2=s2, op0=op0, op1=op1)


def STT(out, in0, scalar, in1, op0, op1):
    return lambda e: e.scalar_tensor_tensor(out=out, in0=in0, scalar=scalar, in1=in1, op0=op0, op1=op1)


def CP(out, in_):
    return lambda e: e.tensor_copy(out=out, in_=in_)


def MSET(out, v):
    return lambda e: e.memset(out, v)


class Arena:
    def __init__(self, ap, nwords):
        self.ap = ap
        self.n = nwords
        self.off = 0

    def reset(self):
        self.off = 0

    def alloc(self, free_shape, dtype, parts=128):
        n = int(np.prod(free_shape))
        words = n if dtype == F32 else (n + 1) // 2
        words = (words + 7) // 8 * 8
        assert self.off + words <= self.n, ("arena overflow", self.off, words, self.n)
        v = self.ap[0:parts, self.off:self.off + words]
        self.off += words
        if dtype != F32:
            v = v.bitcast(dtype)
        v = v[:, 0:n]
        if len(free_shape) == 2:
            v = v.rearrange("p (a b) -> p a b", b=free_shape[1])
        elif len(free_shape) == 3:
            v = v.rearrange("p (a b c) -> p a b c", b=free_shape[1], c=free_shape[2])
        elif len(free_shape) == 4:
            v = v.rearrange("p (a b c d) -> p a b c d", b=free_shape[1], c=free_shape[2], d=free_shape[3])
        return v


class Rot:
    def __init__(self, n):
        self.n = n
        self.i = -1

    def next(self):
        self.i = (self.i + 1) % self.n
        return self.i


def build_program(debug=False):
    nc = bass.Bass("TRN2", target_bir_lowering=False)

    def din(name, shape, dt=F32):
        return nc.dram_tensor(name, list(shape), dt, kind="ExternalInput").ap()

    def dscr(name, shape, dt):
        kind = "ExternalOutput" if debug else "Internal"
        return nc.dram_tensor(name, list(shape), dt, kind=kind).ap()

    x_in = din("x", [2, TL, D])
    ctx_in = din("ctx", [2, TC, D])
    cT_in = din("cT", [128, 24])
    w_mod = din("w_mod", [D, 9 * D])
    b_mod = din("b_mod", [9 * D])
    bmodT_in = din("bmodT", [128, 72])
    npreT_in = din("npreT", [128, 24])
    norm_post = din("norm_post", [3, D])
    w_gate = din("ffn_w_gate", [2, D, FF])
    w_up = din("ffn_w_up", [2, D, FF])
    w_down = din("ffn_w_down", [2, FF, D])
    w_in = din("w_in", [D, DIN])
    qk_gain_in = din("qk_gain", [128, 2])
    lbraw = din("hg_lower_bound", [2, 2, 512])
    hg_norm = din("hg_norm", [128])
    w_att_out = din("w_att_out", [512, D])
    w_hg_out = din("w_hg_out", [512, D])
    w_o = din("w_o", [D, D])
    c_mat = din("c_mat", [128, 384])
    c_rope = din("c_rope", [128, 2, TL])
    c_hg = din("c_hg", [128, 2, 640])
    out = nc.dram_tensor("out", [2, TL, D], F32, kind="ExternalOutput").ap()

    X1 = dscr("X1", [2, T, D], F32)
    QT = dscr("QT", [2, 4, 128, TL], BF16)
    KT = dscr("KT", [2, 128, T], BF16)
    VV = dscr("VV", [2, T, 128], BF16)
    HQT = dscr("HQT", [2, 4, 128, TL], BF16)
    HV = dscr("HV", [2, T, 512], BF16)
    KF = dscr("KF", [2, 2, T, 512], F32)
    HG = dscr("HG", [2, TL, 512], BF16)
    SG = dscr("SG", [2, 16, 128, TL], BF16)
    OAT = dscr("OAT", [2, 4, 128, TL], BF16)
    OHT = dscr("OHT", [2, 4, 128, TL], BF16)

    def bcast(a):
        return bass.AP(a.tensor, a.offset, [[0, 128]] + [list(v) for v in a.ap])

    S = Sched()
    with ExitStack() as st:
        NP = 7700
        NR = 1536
        r32_t = st.enter_context(nc.sbuf_tensor("r32", [128, NR], F32))
        NW = (int(nc.sbuf_bytes_remaining) // 4 - NP - 64) // 8 * 8
        arena_t = st.enter_context(nc.sbuf_tensor("arena", [128, NW], F32))
        pers_t = st.enter_context(nc.sbuf_tensor("pers", [128, NP], F32))
        ps_t = st.enter_context(nc.psum_tensor("ps", [128, 4096], F32))
        A = Arena(arena_t[:, :], NW)
        PA = Arena(pers_t[:, :], NP)
        ps = ps_t[:, :]
        psb = ps.bitcast(BF16)

        def bank(b, n=512, o=0):
            return ps[:, b * 512 + o:b * 512 + o + n]

        def bankb(b, n=1024, o=0):
            return psb[:, b * 1024 + o:b * 1024 + o + n]

        ABt = PA.alloc([3, 2, 3, 8], F32)
        Gb = {}
        for (i, r) in [(0, 0), (0, 1), (0, 2), (1, 0), (1, 1), (2, 0), (2, 1)]:
            Gb[(i, r)] = PA.alloc([D], F32)
        cm_b = PA.alloc([384], BF16)
        chalf = PA.alloc([8], F32)
        ones_f = PA.alloc([128], F32)
        epsc = PA.alloc([8], F32)
        ident_b = cm_b[:, 0:128]
        bd64_b = cm_b[:, 128:256]
        perm_b = cm_b[:, 256:384]

        S.op("pool", MSET(chalf, -0.5), writes=["chalf"])
        S.op("pool", MSET(ones_f, 1.0), writes=["ones_f"])
        S.op("pool", MSET(epsc, EPS), writes=["epsc"])

        A.reset()
        Wg0 = A.alloc([8, FF], BF16)
        Wu0 = A.alloc([8, FF], BF16)
        for k in range(8):
            S.dma("pool", [(Wg0[:, k, :], w_gate[0, k * 128:(k + 1) * 128, :])], "wldg", writes=["Wg"])
            S.dma("pool", [(Wu0[:, k, :], w_up[0, k * 128:(k + 1) * 128, :])], "wldu", writes=["Wu"])
        cm_f = A.alloc([384], F32)
        S.dma("sp", [(cm_f, c_mat)], "c0", writes=["cm_f"])
        S.op("dve", CP(cm_b, cm_f), reads=["cm_f"], writes=["cm_b"])
        cT = A.alloc([24], F32)
        scT = A.alloc([24], F32)
        bmodT = A.alloc([72], F32)
        npreT = A.alloc([24], F32)
        screp = A.alloc([3, 8, 128], F32)
        wm = [A.alloc([8, 512], F32) for _ in range(2)]
        bmb = [A.alloc([D], F32) for _ in range(2)]
        gpb = [A.alloc([D], F32) for _ in range(2)]
        mtmp = [A.alloc([512], F32) for _ in range(2)]
        mT = A.alloc([6, 8, 3], F32)
        abt = A.alloc([8], F32)
        S.dma("sp", [(cT, cT_in), (bmodT, bmodT_in), (npreT, npreT_in)], "c1", writes=["cT", "bmodT", "npreT"])
        S.op("act", ACTF(scT, cT, AF.Silu), reads=["cT"], writes=["scT"])
        for r in range(3):
            for k in range(8):
                S.op("pool", TS(screp[:, r, k, :], ones_f, scT[:, k * 3 + r:k * 3 + r + 1], None, ALU.mult),
                     reads=["scT", "ones_f"], writes=["screp"])
        psM = bank(0, 144).rearrange("p (a b c) -> p a b c", b=8, c=3)
        fm_js = [0, 1, 3, 4, 6, 7]
        rb = Rot(2)
        rm = Rot(2)
        wrot = Rot(2)
        for j in range(9):
            i = j // 3
            if j not in fm_js:
                b = rb.next()
                S.dma("sp", [(bmb[b], bcast(b_mod[j * D:(j + 1) * D])),
                             (gpb[b], bcast(norm_post[i, :]))], "bg%d" % b,
                      writes=["bmb%d" % b, "gpb%d" % b])
            for half in range(2):
                sl = wrot.next()
                c0_ = j * D + half * 512
                S.dma("sp", [(wm[sl][:, k, :], w_mod[k * 128:(k + 1) * 128, c0_:c0_ + 512]) for k in range(8)],
                      "wm%d" % sl, writes=["wm%d" % sl])
                if j in fm_js:
                    jj = fm_js.index(j)
                    lst = []
                    for c in range(4):
                        for k in range(8):
                            lst.append((psM[:, jj, half * 4 + c, :], wm[sl][:, k, c * 128:(c + 1) * 128], scT[:, k * 3:k * 3 + 3],
                                        k == 0, k == 7))
                    S.op("pe", MMS(lst), reads=["wm%d" % sl, "scT"], writes=["psM"])
                else:
                    for r in ([0, 1, 2] if i == 0 else [0, 1]):
                        pb = 1 + (r * 2 + half) % 4
                        lst = [(bank(pb), screp[:, r, k, :], wm[sl][:, k, :], k == 0, k == 7) for k in range(8)]
                        S.op("pe", MMS(lst), reads=["wm%d" % sl, "screp"], writes=["ps%d" % pb])
                        m = rm.next()
                        S.op("dve", TT(mtmp[m], bank(pb), bmb[b][:, half * 512:(half + 1) * 512], ALU.add),
                             reads=["ps%d" % pb, "bmb%d" % b], writes=["mtmp%d" % m])
                        S.op("dve", STT(Gb[(i, r)][:, half * 512:(half + 1) * 512], mtmp[m], 1.0 if i == 1 else 0.5,
                                         gpb[b][:, half * 512:(half + 1) * 512], ALU.mult, ALU.mult),
                             reads=["mtmp%d" % m, "gpb%d" % b], writes=["Gb%d%d" % (i, r)])
        S.op("act", ACTF(mT, psM, AF.Identity), reads=["psM"], writes=["mT"])
        for i in range(3):
            for r in range(3):
                S.op("dve", TT(ABt[:, i, 1, r, :], mT[:, 2 * i, :, r], bmodT[:, (3 * i) * 8:(3 * i) * 8 + 8], ALU.add),
                     reads=["mT", "bmodT"], writes=["AB"])
                S.op("dve", TT(abt, mT[:, 2 * i + 1, :, r], bmodT[:, (3 * i + 1) * 8:(3 * i + 1) * 8 + 8], ALU.add),
                     reads=["mT", "bmodT"], writes=["abt"])
                S.op("dve", STT(ABt[:, i, 0, r, :], abt, 1.0, npreT[:, i * 8:i * 8 + 8], ALU.add, ALU.mult),
                     reads=["abt", "npreT"], writes=["AB"])
        S.barrier()

        def pre_stats(xt, nsub, keys_x, bufs):
            ss, rstd, junk, xn = bufs["ss"], bufs["rstd"], bufs["junk"], bufs["xn"]
            for n in range(nsub):
                S.op("act", ACTF(junk, xt[:, n, :], AF.Square, accum_out=ss[:, n:n + 1]),
                     reads=keys_x, writes=["junk", "ss"])
            S.op("pool", TS(rstd[:, 0:nsub], ss[:, 0:nsub], 1.0 / D, EPS, ALU.mult, ALU.add), reads=["ss"], writes=["rstd"])
            S.op("pool", TT(rstd[:, 0:nsub], rstd[:, 0:nsub], chalf[:, 0:nsub], ALU.pow), reads=["rstd", "chalf"],
                 writes=["rstd"])
            for n in range(nsub):
                S.op("dve", TS(xn[:, n, :], xt[:, n, :], rstd[:, n:n + 1], None, ALU.mult), reads=keys_x + ["rstd"],
                     writes=[bufs.get("shared_key", "xn%d" % n)])

        def pre_trans(nsub, i_sub, r, uT, key_u, bufs, region_rot):
            xn = bufs["xn"]
            nt = nsub * 128
            for c in range(8):
                reg, rkey = region_rot()
                lst = [(reg[:, n * 128:(n + 1) * 128], xn[:, n, c * 128:(c + 1) * 128]) for n in range(nsub)]
                S.op("pe", TRS(lst, ident_b), reads=[bufs.get("shared_key", "xn%d" % n) for n in range(nsub)] + ["cm_b"],
                     writes=[rkey])
                S.op("act", ACTF(uT[:, c, 0:nt], reg[:, 0:nt], AF.Identity, scale=ABt[:, i_sub, 0, r, c:c + 1],
                                 bias=ABt[:, i_sub, 1, r, c:c + 1]), reads=[rkey, "AB"], writes=[key_u])

        def pre_norm(xt, nsub, i_sub, r, uT, keys_x, key_u, bufs, pbank_rot):
            pre_stats(xt, nsub, keys_x, bufs)
            pre_trans(nsub, i_sub, r, uT, key_u, bufs, pbank_rot)

        def post_resid(psY, ykey, xt_n, key_x, gb, gbkey, bufs):
            junk, ss2, rstd2, yt = bufs["junk"], bufs["ss2"], bufs["rstd2"], bufs["yt"]
            S.op("act", ACTF(junk, psY, AF.Square, accum_out=ss2[:, 0:1]), reads=[ykey], writes=["junk", "ss2"])
            ytk = bufs.get("shared_key", "yt")
            S.op("dve", TT(yt, psY, gb, ALU.mult), reads=[ykey, gbkey, "ss2"], writes=[ytk])
            S.op("pool", TS(rstd2[:, 0:1], ss2[:, 0:1], 1.0 / D, EPS, ALU.mult, ALU.add), reads=["ss2"], writes=["rstd2"])
            S.op("pool", TT(rstd2[:, 0:1], rstd2[:, 0:1], chalf[:, 0:1], ALU.pow), reads=["rstd2", "chalf"], writes=["rstd2"])
            S.op("dve", STT(xt_n, yt, rstd2[:, 0:1], xt_n, ALU.mult, ALU.add), reads=[ytk, "rstd2", key_x], writes=[key_x])

        def ffn_phase(fi, tiles, preloaded=False):
            A.reset()
            Wg = A.alloc([8, FF], BF16)
            Wu = A.alloc([8, FF], BF16)
            Wd = A.alloc([NFC, D], BF16)
            xts = [A.alloc([2, D], F32) for _ in range(2)]
            uT = A.alloc([8, 256], BF16)
            hT = A.alloc([NFC, 256], BF16)
            sg = [A.alloc([256], BF16) for _ in range(2)]
            shr = A.alloc([D], F32)
            bufs = {"ss": A.alloc([8], F32), "rstd": A.alloc([8], F32), "junk": A.alloc([D], BF16),
                    "xn": shr.bitcast(BF16).rearrange("p (a b) -> p a b", b=D), "ss2": A.alloc([8], F32),
                    "rstd2": A.alloc([8], F32), "yt": shr, "shared_key": "shr"}
            for k in range(8):
                if preloaded:
                    break
                S.dma("pool", [(Wg[:, k, :], w_gate[fi, k * 128:(k + 1) * 128, :])], "wldg", writes=["Wg"])
                S.dma("pool", [(Wu[:, k, :], w_up[fi, k * 128:(k + 1) * 128, :])], "wldu", writes=["Wu"])
            for f0 in range(0, NFC, 2):
                S.dma("pool", [(Wd[:, f, :], w_down[fi, f * 128:(f + 1) * 128, :]) for f in (f0, f0 + 1)], "wldd",
                      writes=["Wd"])
            psYs = [ps[:, 4 * 512:6 * 512], ps[:, 6 * 512:8 * 512]]
            trot = Rot(2)
            grot = Rot(2)
            srot = Rot(2)
            yrot = Rot(2)

            def tr_region():
                i_ = trot.next()
                return bankb(i_, 256), "ps%d" % i_

            def f_stats(ti):
                src, dst, r = tiles[ti]
                sl = ti % 2
                S.dma("sp", [(xts[sl], src.rearrange("(n p) d -> p n d", p=128))], "xld%d" % sl, writes=["xt%d" % sl])
                pre_stats(xts[sl], 2, ["xt%d" % sl], bufs)

            def f_trans(ti):
                src, dst, r = tiles[ti]
                pre_trans(2, 2 * fi, r, uT, "uT", bufs, tr_region)

            def f_gu(ti):
                for f in range(NFC):
                    pb = 2 + grot.next()
                    lst = [(bank(pb, 256, 0), Wg[:, k, f * 128:(f + 1) * 128], uT[:, k, :], k == 0, k == 7) for k in range(8)]
                    lst += [(bank(pb, 256, 256), Wu[:, k, f * 128:(f + 1) * 128], uT[:, k, :], k == 0, k == 7) for k in range(8)]
                    S.op("pe", MMS(lst), reads=["Wg", "Wu", "uT"], writes=["ps%d" % pb])
                    s_ = srot.next()
                    S.op("act", ACTF(sg[s_], bank(pb, 256, 0), AF.Silu), reads=["ps%d" % pb], writes=["sg%d" % s_])
                    S.op("dve", TT(hT[:, f, :], bank(pb, 256, 256), sg[s_], ALU.mult), reads=["ps%d" % pb, "sg%d" % s_],
                         writes=["hT"])

            def f_down(ti):
                src, dst, r = tiles[ti]
                sl = ti % 2
                xt = xts[sl]
                kx = "xt%d" % sl
                for n in range(2):
                    y_ = yrot.next()
                    psY = psYs[y_]
                    lst = []
                    for half in range(2):
                        lst += [(psY[:, half * 512:(half + 1) * 512], hT[:, f, n * 128:(n + 1) * 128],
                                 Wd[:, f, half * 512:(half + 1) * 512], f == 0, f == NFC - 1) for f in range(NFC)]
                    S.op("pe", MMS(lst), reads=["hT", "Wd"], writes=["psY%d" % y_])
                    post_resid(psY, "psY%d" % y_, xt[:, n, :], kx, Gb[(2 * fi, r)], "Gb", bufs)
                S.dma("sp", [(dst.rearrange("(n p) d -> p n d", p=128), xt)], "xst%d" % sl, reads=[kx])

            f_stats(0)
            f_trans(0)
            for ti in range(len(tiles)):
                if ti + 1 < len(tiles):
                    f_stats(ti + 1)
                f_gu(ti)
                if ti + 1 < len(tiles):
                    f_trans(ti + 1)
                f_down(ti)
            S.barrier()

        tiles = []
        for s in range(2):
            tiles.append((ctx_in[s, :, :], X1[s, 0:TC, :], 2))
            for t0 in range(0, TL, 256):
                tiles.append((x_in[s, t0:t0 + 256, :], X1[s, TC + t0:TC + t0 + 256, :], s))
        ffn_phase(0, tiles, preloaded=True)

        A.reset()
        Win = A.alloc([8, DIN], BF16)
        rps = [A.alloc([2, 512], F32) for _ in range(2)]
        lbt = A.alloc([2, 2, 512], F32)
        qkg = A.alloc([2], F32)
        xts = [A.alloc([4, D], F32) for _ in range(1)]
        uTs = [A.alloc([8, 512], BF16) for _ in range(2)]
        bufs = {"ss": A.alloc([8], F32), "rstd": A.alloc([8], F32), "junk": A.alloc([D], BF16),
                "xn": A.alloc([4, D], BF16)}
        sqs = [A.alloc([512], BF16) for _ in range(2)]
        msqs = [A.alloc([512], F32) for _ in range(2)]
        xgs = [A.alloc([512], BF16) for _ in range(2)]
        t12 = A.alloc([4, 512], F32)
        t1s = [t12[:, 0, :], t12[:, 1, :]]
        t2s = [t12[:, 2, :], t12[:, 3, :]]
        lbl = t12.rearrange("p (a b) c -> p a b c", b=2)
        fo = [A.alloc([512], BF16) for _ in range(3)]
        tsig = [A.alloc([512], F32) for _ in range(2)]
        tv = [A.alloc([512], BF16) for _ in range(2)]
        tv2 = [A.alloc([128], BF16) for _ in range(2)]
        for k in range(8):
            S.dma("pool", [(Win[:, k, :], w_in[k * 128:(k + 1) * 128, :])], "wldi", writes=["Win"])
        S.dma("sp", [(qkg, qk_gain_in)], "c2r", writes=["qkg"])
        lblk = [["t10", "t11"], ["t20", "t21"]]
        for d in range(2):
            S.dma("sp", [(lbl[:, d, sl_, :], bcast(lbraw[d, sl_, :])) for sl_ in range(2)], "c2l%d" % d,
                  writes=lblk[d])
        for d in range(2):
            S.op("dve", TT(lbl[:, d, 0, :], lbl[:, d, 0, :], lbl[:, d, 1, :], ALU.subtract), reads=lblk[d], writes=[lblk[d][0]])
            S.op("act", ACTF(lbt[:, d, 0, :], lbl[:, d, 0, :], AF.Sigmoid), reads=[lblk[d][0], "lbt"], writes=["lbt"])
            S.op("act", ACTF(lbt[:, d, 1, :], lbl[:, d, 0, :], AF.Sigmoid, scale=-1.0), reads=[lblk[d][0], "lbt"], writes=["lbt"])

        frot = Rot(3)
        prot = Rot(4)
        trot = Rot(2)
        qrot = Rot(2)
        tkr = Rot(2)
        crot = Rot(2)

        def tr_region2():
            i_ = trot.next()
            return bankb(i_, 512), "ps%d" % i_

        def fm_proj(col0, nt, u_):
            pb = 2 + prot.next()
            lst = [(bank(pb, nt), Win[:, k, col0:col0 + 128], uTs[u_][:, k, 0:nt], k == 0, k == 7) for k in range(8)]
            S.op("pe", MMS(lst), reads=["Win", "uT%d" % u_], writes=["ps%d" % pb])
            return pb

        def qk_A(col0, nt, u_):
            pb = fm_proj(col0, nt, u_)
            P1 = bank(pb, nt)
            c_ = crot.next()
            sq, msq = sqs[c_], msqs[c_]
            S.op("act", ACTF(sq[:, 0:nt], P1, AF.Square), reads=["ps%d" % pb], writes=["sq%d" % c_])
            q2 = 6 + qrot.next()
            S.op("pe", MMS([(bank(q2, nt), bd64_b, sq[:, 0:nt], True, True)]), reads=["sq%d" % c_, "cm_b"], writes=["ps%d" % q2])
            S.op("act", ACTF(msq[:, 0:nt], bank(q2, nt), AF.Ln, bias=epsc[:, 0:1]), reads=["ps%d" % q2, "epsc"],
                 writes=["msq%d" % c_])
            S.op("act", ACTF(msq[:, 0:nt], msq[:, 0:nt], AF.Exp, scale=-0.5), reads=["msq%d" % c_], writes=["msq%d" % c_])
            return (pb, c_)

        def qk_B(st_, nt, gcol, rp_, dst):
            pb, c_ = st_
            P1 = bank(pb, nt)
            msq, xg, t1, t2 = msqs[c_], xgs[c_], t1s[c_], t2s[c_]
            f_ = frot.next()
            if rp_ is None:
                S.op("dve", STT(fo[f_][:, 0:nt], P1, qkg[:, gcol:gcol + 1], msq[:, 0:nt], ALU.mult, ALU.mult),
                     reads=["ps%d" % pb, "msq%d" % c_, "qkg"], writes=["fo%d" % f_])
            else:
                S.op("dve", STT(xg[:, 0:nt], P1, qkg[:, gcol:gcol + 1], msq[:, 0:nt], ALU.mult, ALU.mult),
                     reads=["ps%d" % pb, "msq%d" % c_, "qkg"], writes=["xg%d" % c_])
                q3 = 6 + qrot.next()
                S.op("pe", MMS([(bank(q3, nt), perm_b, xg[:, 0:nt], True, True)]), reads=["xg%d" % c_, "cm_b"], writes=["ps%d" % q3])
                S.op("pool", TT(t1[:, 0:nt], xg[:, 0:nt], rps[rp_][:, 0, 0:nt], ALU.mult), reads=["xg%d" % c_, "rp%d" % rp_],
                     writes=["t1%d" % c_])
                S.op("dve", TT(t2[:, 0:nt], bank(q3, nt), rps[rp_][:, 1, 0:nt], ALU.mult), reads=["ps%d" % q3, "rp%d" % rp_],
                     writes=["t2%d" % c_])
                S.op("dve", TT(fo[f_][:, 0:nt], t1[:, 0:nt], t2[:, 0:nt], ALU.add), reads=["t1%d" % c_, "t2%d" % c_],
                     writes=["fo%d" % f_])
            S.dma("sp", [(dst, fo[f_][:, 0:nt])], "fo%d" % f_, reads=["fo%d" % f_])

        def fm_act(col0, nt, func, dst, u_, scale_after=None):
            pb = fm_proj(col0, nt, u_)
            f_ = frot.next()
            S.op("act", ACTF(fo[f_][:, 0:nt], bank(pb, nt), func), reads=["ps%d" % pb], writes=["fo%d" % f_])
            if scale_after is not None:
                S.op("pool", TS(fo[f_][:, 0:nt], fo[f_][:, 0:nt], scale_after, None, ALU.mult), reads=["fo%d" % f_],
                     writes=["fo%d" % f_])
            S.dma("sp", [(dst, fo[f_][:, 0:nt])], "fo%d" % f_, reads=["fo%d" % f_])

        def tm_proj(col0, ncols, n, u_):
            pb = 2 + prot.next()
            lst = [(bank(pb, ncols), uTs[u_][:, k, n * 128:(n + 1) * 128], Win[:, k, col0:col0 + ncols], k == 0, k == 7)
                   for k in range(8)]
            S.op("pe", MMS(lst), reads=["Win", "uT%d" % u_], writes=["ps%d" % pb])
            return pb

        p2tiles = []
        for s in range(2):
            p2tiles.append((s, 0, TC, False))
            for t0 in range(0, TL, 512):
                p2tiles.append((s, TC + t0, 512, True))

        def p2_load(ti):
            s, tg0, nt, lat = p2tiles[ti]
            nsub = nt // 128
            u_ = ti % 2
            S.dma("sp", [(xts[0][:, 0:nsub, :], X1[s, tg0:tg0 + nt, :].rearrange("(n p) d -> p n d", p=128))], "xld0",
                  writes=["xt0"])
            if lat:
                lt0 = tg0 - TC
                S.dma("sp", [(rps[u_], c_rope[:, :, lt0:lt0 + 512])], "rp%d" % u_, writes=["rp%d" % u_])

        def p2_stats(ti):
            s, tg0, nt, lat = p2tiles[ti]
            pre_stats(xts[0], nt // 128, ["xt0"], bufs)

        def p2_trans(ti):
            s, tg0, nt, lat = p2tiles[ti]
            nsub = nt // 128
            u_ = ti % 2
            pre_trans(nsub, 1, (s if lat else 2), uTs[u_], "uT%d" % u_, bufs, tr_region2)

        p2_load(0)
        p2_stats(0)
        p2_trans(0)
        for ti, (s, tg0, nt, lat) in enumerate(p2tiles):
            nsub = nt // 128
            u_ = ti % 2
            lt0 = tg0 - TC
            if ti + 1 < len(p2tiles):
                p2_load(ti + 1)
            if lat:
                for h in range(4):
                    fm_act(OHQ + h * 128, nt, AF.Silu, HQT[s, h, :, lt0:lt0 + nt], u_)
                for n in range(nsub):
                    pb = tm_proj(OHG, 512, n, u_)
                    k_ = tkr.next()
                    S.op("act", ACTF(tv[k_], bank(pb), AF.Silu), reads=["ps%d" % pb], writes=["tv%d" % k_])
                    S.dma("sp", [(HG[s, lt0 + n * 128:lt0 + (n + 1) * 128, :], tv[k_])], "tv%d" % k_, reads=["tv%d" % k_])
            chunks = []
            if lat:
                for j in range(4):
                    chunks.append((OQ + j * 128, 0, u_, QT[s, j, :, lt0:lt0 + nt]))
            chunks.append((OK_, 1, (u_ if lat else None), KT[s, :, tg0:tg0 + nt]))
            prev = None
            for (col0_, gcol_, rp__, dst_) in chunks:
                st_ = qk_A(col0_, nt, u_)
                if prev is not None:
                    qk_B(prev[0], nt, prev[1], prev[2], prev[3])
                prev = (st_, gcol_, rp__, dst_)
            qk_B(prev[0], nt, prev[1], prev[2], prev[3])
            if ti + 1 < len(p2tiles):
                p2_stats(ti + 1)
                p2_trans(ti + 1)
            for n in range(nsub):
                pb = tm_proj(OV, 128, n, u_)
                k_ = tkr.next()
                S.op("act", ACTF(tv2[k_], bank(pb, 128), AF.Identity), reads=["ps%d" % pb], writes=["tv2%d" % k_])
                S.dma("sp", [(VV[s, tg0 + n * 128:tg0 + (n + 1) * 128, :], tv2[k_])], "tv2%d" % k_, reads=["tv2%d" % k_])
                pb = tm_proj(OHV, 512, n, u_)
                S.op("dve", CP(tv[k_], bank(pb)), reads=["ps%d" % pb], writes=["tv%d" % k_])
                S.dma("sp", [(HV[s, tg0 + n * 128:tg0 + (n + 1) * 128, :], tv[k_])], "tv%d" % k_, reads=["tv%d" % k_])
            if lat:
                for c in range(16):
                    fm_act(OGA + c * 128, nt, AF.Sigmoid, SG[s, c, :, lt0:lt0 + nt], u_)
            for n in range(nsub):
                for d in range(2):
                    pb = tm_proj(OFF + d * 512, 512, n, u_)
                    k_ = tkr.next()
                    S.op("act", ACTF(tsig[k_], bank(pb), AF.Sigmoid, scale=-1.0), reads=["ps%d" % pb], writes=["tsig%d" % k_])
                    S.op("dve", TT(tsig[k_], tsig[k_], lbt[:, d, 1, :], ALU.mult), reads=["tsig%d" % k_, "lbt"],
                         writes=["tsig%d" % k_])
                    rows = slice(tg0 + n * 128, tg0 + (n + 1) * 128)
                    S.dma("sp", [(KF[s, d, rows, :], tsig[k_])], "tgk%d" % k_, reads=["tsig%d" % k_])
        S.barrier()

        A.reset()
        chg = A.alloc([2, 640], F32)
        hgn = A.alloc([128], F32)
        Obuf = A.alloc([16, 512], F32)
        Sst = A.alloc([2, 4, 128], F32)
        SbfIn = [A.alloc([4, 128], BF16) for _ in range(2)]
        SbfMid = A.alloc([4, 128], BF16)
        NL = 5
        kts = [A.alloc([512], F32) for _ in range(NL)]
        vts = [A.alloc([512], BF16) for _ in range(NL)]
        hqs = [A.alloc([4, 128], BF16) for _ in range(NL)]
        ogs = [A.alloc([512], BF16) for _ in range(NL)]
        E1 = A.alloc([512], F32)
        E2 = A.alloc([512], F32)
        kds = A.alloc([512], BF16)
        kd2 = A.alloc([512], BF16)
        kd2T = A.alloc([4, 128], BF16)
        qdT = A.alloc([4, 128], BF16)
        EFs = [A.alloc([4, 256], F32) for _ in range(2)]
        dsbs = [A.alloc([2, 4, 128], F32) for _ in range(2)]
        qepads = [A.alloc([4, 2, 128], BF16) for _ in range(2)]
        sTms = [A.alloc([4, 128], BF16) for _ in range(2)]
        osums = [A.alloc([512], F32) for _ in range(2)]
        hjunk = A.alloc([128], F32)
        hss = A.alloc([8], F32)
        hrs = A.alloc([8], F32)
        gg = A.alloc([512], F32)
        on = A.alloc([512], BF16)
        ohT = [A.alloc([4, 128], BF16) for _ in range(2)]
        S.dma("sp", [(chg, c_hg), (hgn, bcast(hg_norm))], "c3", writes=["chg", "hgn"])
        r32 = r32_t[:, :].bitcast(F32R)
        chr_r = r32[:, 0:1024].rearrange("p (a b) -> p a b", b=512)
        gt_r = r32[:, 1024:1536]
        for d_ in range(2):
            S.op("dve", CP(chr_r[:, d_, :], chg[:, d_, 0:512]), reads=["chg"], writes=["chr"])
        for q_ in range(2):
            S.op("pool", MSET(qepads[q_], 0.0), writes=["qepad%d" % q_])
        PE_, PT_, PFa, PFb, PST, POh, PD0, PD1 = range(8)
        lrot = Rot(NL)
        orot = Rot(2)

        items = []
        for s in range(2):
            for d in range(2):
                order = list(range(18)) if d == 0 else [1, 0] + list(range(17, 1, -1))
                for oi, sub in enumerate(order):
                    items.append((s, d, sub, oi == 0))

        def hg_load(it):
            s, d, sub, is_first = items[it]
            lat = sub >= 2
            rows = slice(sub * 128, (sub + 1) * 128)
            lt0 = (sub - 2) * 128
            l_ = lrot.next()
            pairs = [(kts[l_], KF[s, d, rows, :]), (vts[l_], HV[s, rows, :])]
            wk = ["kt%d" % l_, "vt%d" % l_]
            if lat:
                pairs.append((hqs[l_], HQT[s, :, :, lt0:lt0 + 128].rearrange("h p t -> p h t")))
                wk.append("hq%d" % l_)
                if d == 1:
                    pairs.append((ogs[l_], HG[s, lt0:lt0 + 128, :]))
                    wk.append("og%d" % l_)
            S.dma("sp", pairs, "hl%d" % l_, writes=wk)
            return l_

        def hg_A1(it, slot, l_):
            s, d, sub, is_first = items[it]
            M1 = chr_r[:, d, 0:128]
            M2 = chr_r[:, d, 128:256]
            Rm = chr_r[:, d, 256:512]
            mask = chg[:, d, 512:640]
            lat = sub >= 2
            rows = slice(sub * 128, (sub + 1) * 128)
            lt0 = (sub - 2) * 128
            kt, vt, hq, og = kts[l_], vts[l_], hqs[l_], ogs[l_]
            EF, dsb, qepad, sTm = EFs[slot], dsbs[slot], qepads[slot], sTms[slot]
            gt = gt_r
            S.op("act", ACTF(gt, kt, AF.Ln, scale=-1.0, bias=ones_f[:, 0:1]), reads=["kt%d" % l_, "ones_f"],
                 writes=["gtr"])
            S.op("pe", MMS([(bank(PE_), M1, gt, True, True)]), reads=["gtr", "chr"], writes=["ps0"])
            S.op("act", ACTF(E1, bank(PE_), AF.Exp), reads=["ps0"], writes=["E1"])
            S.op("pool", TT(kds, kt, E1, ALU.mult), reads=["kt%d" % l_, "E1"], writes=["kds"])
            if lat:
                S.op("pe", MMS([(bank(PE_), M2, gt, True, True)]), reads=["gtr", "chr"], writes=["ps0"])
                S.op("act", ACTF(E2, bank(PE_), AF.Exp), reads=["ps0"], writes=["E2"])
                S.op("pool", TT(kd2, kt, E2, ALU.mult), reads=["kt%d" % l_, "E2"], writes=["kd2"])
            lst = []
            for h in range(4):
                pbF = PFa if h < 2 else PFb
                lst.append((bank(pbF, 256, (h % 2) * 256), gt[:, h * 128:(h + 1) * 128], Rm, True, True))
            S.op("pe", MMS(lst), reads=["gtr", "chr"], writes=["ps2", "ps3"])
            S.op("act", ACTF(EF[:, 0:2, :], bank(PFa).rearrange("p (h t) -> p h t", t=256), AF.Exp), reads=["ps2"],
                 writes=["EF%d" % slot])
            S.op("act", ACTF(EF[:, 2:4, :], bank(PFb).rearrange("p (h t) -> p h t", t=256), AF.Exp), reads=["ps3", "EF%d" % slot],
                 writes=["EF%d" % slot])
            lst = []
            for j in range(2):
                for h in range(4):
                    lst.append((bank(PD0 + j, 128, h * 128), kds[j * 64:(j + 1) * 64, h * 128:(h + 1) * 128],
                                vt[j * 64:(j + 1) * 64, h * 128:(h + 1) * 128], True, True))
            S.op("pe", MMS(lst), reads=["kds", "vt%d" % l_], writes=["ps6", "ps7"])
            S.op("act", ACTF(dsb.rearrange("p a b c -> p (a b c)"), ps[:, 6 * 512:8 * 512], AF.Identity), reads=["ps6", "ps7"],
                 writes=["dsb%d" % slot])

        def hg_A2(it, slot, l_):
            s, d, sub, is_first = items[it]
            mask = chg[:, d, 512:640]
            lat = sub >= 2
            hq = hqs[l_]
            EF, qepad, sTm = EFs[slot], qepads[slot], sTms[slot]
            if lat:
                S.op("pe", TRS([(bankb(PT_, 128, h * 128), kd2[:, h * 128:(h + 1) * 128]) for h in range(4)], ident_b),
                     reads=["kd2", "cm_b"], writes=["ps1"])
                S.op("dve", CP(kd2T, bankb(PT_, 512).rearrange("p (h t) -> p h t", t=128)), reads=["ps1"], writes=["kd2T"])
                S.op("dve", STT(qdT, EF[:, :, 0:128], HG_SCALE, hq, ALU.mult, ALU.mult), reads=["EF%d" % slot, "hq%d" % l_],
                     writes=["qdT"])
                for j in range(2):
                    S.op("dve", STT(qepad[:, :, j, j * 64:(j + 1) * 64], EF[:, :, 128 + j * 64:128 + (j + 1) * 64], HG_SCALE,
                                    hq[:, :, j * 64:(j + 1) * 64], ALU.mult, ALU.mult),
                         reads=["EF%d" % slot, "hq%d" % l_, "qepad%d" % slot], writes=["qepad%d" % slot])
                lst = [(bank(PST, 128, h * 128), kd2T[:, h, :], qdT[:, h, :], True, True) for h in range(4)]
                S.op("pe", MMS(lst), reads=["kd2T", "qdT"], writes=["ps4"])
                S.op("dve", TT(sTm, bank(PST).rearrange("p (h t) -> p h t", t=128),
                               bass.AP(mask.tensor, mask.offset, [list(mask.ap[0]), [0, 4], list(mask.ap[-1])]), ALU.mult),
                     reads=["ps4", "chg"], writes=["sTm%d" % slot])

        def hg_B1(it, slot):
            s, d, sub, is_first = items[it]
            first, second = (0, 1) if d == 0 else (1, 0)
            lastpos = {0: (63 if d == 0 else 0), 1: (127 if d == 0 else 64)}
            EF, dsb = EFs[slot], dsbs[slot]
            sin, sout = "SbfIn%d" % (it % 2), "SbfIn%d" % ((it + 1) % 2)
            if is_first:
                S.op("pool", MSET(Sst[:, 0, :, :], 0.0), reads=["Sst0"], writes=["Sst0"])
                S.op("pool", MSET(SbfIn[it % 2], 0.0), reads=[sin], writes=[sin])
            for h in range(4):
                S.op("dve", STT(Sst[:, 1, h, :], Sst[:, 0, h, :], EF[:, h, 128 + lastpos[first]:129 + lastpos[first]],
                                dsb[:, first, h, :], ALU.mult, ALU.add),
                     reads=["Sst0", "EF%d" % slot, "dsb%d" % slot, "Sst1"], writes=["Sst1"])
            S.op("act", ACTF(SbfMid, Sst[:, 1, :, :], AF.Identity), reads=["Sst1", "SbfMid"], writes=["SbfMid"])
            for h in range(4):
                S.op("dve", STT(Sst[:, 0, h, :], Sst[:, 1, h, :], EF[:, h, 128 + lastpos[second]:129 + lastpos[second]],
                                dsb[:, second, h, :], ALU.mult, ALU.add),
                     reads=["Sst1", "EF%d" % slot, "dsb%d" % slot, "Sst0"], writes=["Sst0"])
            S.op("act", ACTF(SbfIn[(it + 1) % 2], Sst[:, 0, :, :], AF.Identity), reads=["Sst0", sout], writes=[sout])

        def hg_B2(it, slot, l_):
            s, d, sub, is_first = items[it]
            if sub < 2:
                return
            first, second = (0, 1) if d == 0 else (1, 0)
            vt = vts[l_]
            qepad, sTm = qepads[slot], sTms[slot]
            sin = "SbfIn%d" % (it % 2)
            lst = []
            for h in range(4):
                o_ = bank(POh, 128, h * 128)
                lst.append((o_, sTm[:, h, :], vt[:, h * 128:(h + 1) * 128], True, False))
                lst.append((o_, qepad[:, h, first, :], SbfIn[it % 2][:, h, :], False, False))
                lst.append((o_, qepad[:, h, second, :], SbfMid[:, h, :], False, True))
            S.op("pe", MMS(lst), reads=["sTm%d" % slot, "vt%d" % l_, "qepad%d" % slot, sin, "SbfMid"], writes=["ps5"])
            li = sub - 2
            if d == 0:
                S.op("act", ACTF(Obuf[:, li, :], bank(POh), AF.Identity), reads=["ps5"], writes=["Obuf%d" % li])
            else:
                S.op("dve", TT(osums[slot], bank(POh), Obuf[:, li, :], ALU.add), reads=["ps5", "Obuf%d" % li],
                     writes=["osum%d" % slot])

        def hg_C(it, slot, l_):
            s, d, sub, is_first = items[it]
            if sub < 2 or d == 0:
                return
            lt0 = (sub - 2) * 128
            og = ogs[l_]
            osum = osums[slot]
            ok_ = "osum%d" % slot
            for h in range(4):
                S.op("act", ACTF(hjunk, osum[:, h * 128:(h + 1) * 128], AF.Square, accum_out=hss[:, h:h + 1]),
                     reads=[ok_], writes=["hjunk", "hss"])
            S.op("pool", TS(hrs[:, 0:4], hss[:, 0:4], 1.0 / 128, EPS, ALU.mult, ALU.add), reads=["hss"], writes=["hrs"])
            S.op("pool", TT(hrs[:, 0:4], hrs[:, 0:4], chalf[:, 0:4], ALU.pow), reads=["hrs", "chalf"], writes=["hrs"])
            for h in range(4):
                S.op("pool", TT(gg[:, h * 128:(h + 1) * 128], og[:, h * 128:(h + 1) * 128], hgn, ALU.mult),
                     reads=["og%d" % l_, "hgn", "gg"], writes=["gg"])
                S.op("dve", STT(on[:, h * 128:(h + 1) * 128], osum[:, h * 128:(h + 1) * 128], hrs[:, h:h + 1],
                                gg[:, h * 128:(h + 1) * 128], ALU.mult, ALU.mult),
                     reads=[ok_, "hrs", "gg", "on"], writes=["on"])
            S.op("pe", TRS([(bankb(PT_, 128, h * 128), on[:, h * 128:(h + 1) * 128]) for h in range(4)], ident_b),
                 reads=["on", "cm_b"], writes=["ps1"])
            o2 = orot.next()
            S.op("act", ACTF(ohT[o2], bankb(PT_, 512).rearrange("p (h t) -> p h t", t=128), AF.Identity),
                 reads=["ps1"], writes=["ohT%d" % o2])
            S.dma("sp", [(OHT[s, :, :, lt0:lt0 + 128].rearrange("h p t -> p h t"), ohT[o2])], "oh%d" % o2,
                  reads=["ohT%d" % o2])

        lslot = {}
        lslot[0] = hg_load(0)
        lslot[1] = hg_load(1)
        hg_A1(0, 0, lslot[0])
        hg_A2(0, 0, lslot[0])
        for it in range(len(items)):
            if it + 2 < len(items):
                lslot[it + 2] = hg_load(it + 2)
            nxt = it + 1 < len(items)
            if nxt:
                hg_A1(it + 1, (it + 1) % 2, lslot[it + 1])
            hg_B1(it, it % 2)
            if nxt:
                hg_A2(it + 1, (it + 1) % 2, lslot[it + 1])
            hg_B2(it, it % 2, lslot[it])
            if it >= 1:
                hg_C(it - 1, (it - 1) % 2, lslot[it - 1])
        hg_C(len(items) - 1, (len(items) - 1) % 2, lslot[len(items) - 1])
        S.barrier()

        A.reset()
        kTr = [A.alloc([T], BF16) for _ in range(2)]
        vaugE = A.alloc([18, 2, 128], BF16)
        vaugO = A.alloc([18, 2, 128], BF16)
        qTs = [A.alloc([4, 512], BF16) for _ in range(2)]
        pTs = [A.alloc([2, 512], BF16) for _ in range(3)]
        rrows = [A.alloc([512], F32) for _ in range(2)]
        rrot = Rot(2)
        bsbs = [A.alloc([512], F32) for _ in range(2)]
        oTs = [A.alloc([512], BF16) for _ in range(2)]
        S.op("pool", MSET(vaugE, 1.0), writes=["vaugE"])
        S.op("pool", MSET(vaugO, 1.0), writes=["vaugO"])
        srot = Rot(2)
        orot = Rot(4)
        prot2 = Rot(3)
        qrot2 = Rot(2)
        otr = Rot(2)
        for s in range(2):
            for g in range(2):
                S.dma("sp", [(kTr[g][0:64, :], KT[s, g * 64:(g + 1) * 64, :]), (kTr[g][64:128, :], KT[s, g * 64:(g + 1) * 64, :])],
                      "kl%d" % g, writes=["kTr%d" % g])
                for c0 in range(0, 18, 6):
                    src_v = VV[s, c0 * 128:(c0 + 6) * 128, g * 64:(g + 1) * 64].rearrange("(c p) d -> p c d", p=128)
                    S.dma("sp", [(vaugE[:, c0:c0 + 6, g, 0:64], src_v)], "vle", writes=["vaugE"])
                    S.dma("sp", [(vaugO[:, c0:c0 + 6, g, 64:128], src_v)], "vlo", writes=["vaugO"])
            LOOK = 1
            steps = [(qt, j, c) for qt in range(4) for j in range(4) for c in range(18)]
            qinfo = {}
            sslot = {}
            pobank = {}

            def emit_S(i):
                qt, j, c = steps[i]
                if j == 0 and c == 0:
                    q_ = qrot2.next()
                    qinfo[qt] = q_
                    S.dma("sp", [(qTs[q_], QT[s, :, :, qt * 512:(qt + 1) * 512].rearrange("j p t -> p j t"))], "ql%d" % q_,
                          writes=["qT%d" % q_])
                q_ = qinfo[qt]
                g = j // 2
                r_ = srot.next()
                sslot[i] = r_
                lst = [(bank(2 * r_ + hh), kTr[g][hh * 64:(hh + 1) * 64, c * 128:(c + 1) * 128],
                        qTs[q_][hh * 64:(hh + 1) * 64, j, :], True, True) for hh in range(2)]
                S.op("pe", MMS(lst), reads=["kTr%d" % g, "qT%d" % q_], writes=["ps%d" % (2 * r_), "ps%d" % (2 * r_ + 1)])

            def emit_rest(i):
                qt, j, c = steps[i]
                g = j // 2
                if c == 0:
                    pobank[(qt, j)] = (4 + orot.next(), 4 + orot.next())
                po = pobank[(qt, j)]
                r_ = sslot[i]
                p_ = prot2.next()
                S.op("act", ACTF(pTs[p_].rearrange("p a b -> p (a b)"), ps[:, r_ * 1024:(r_ + 1) * 1024], AF.Exp, scale=ATT_SCALE),
                     reads=["ps%d" % (2 * r_), "ps%d" % (2 * r_ + 1)], writes=["pT%d" % p_])
                lst = [(bank(po[0]), vaugE[:, c, g, :], pTs[p_][:, 0, :], c == 0, c == 17),
                       (bank(po[1]), vaugO[:, c, g, :], pTs[p_][:, 1, :], c == 0, c == 17)]
                S.op("pe", MMS(lst), reads=["vaugE", "vaugO", "pT%d" % p_], writes=["ps%d" % po[0], "ps%d" % po[1]])
                if c == 17:
                    rr_ = rrot.next()
                    for hh in range(2):
                        p0 = 64 if hh == 0 else 0
                        pb_ = po[hh]
                        S.op("dve", (lambda pb_, p0, rr_: lambda e: e.reciprocal(out=rrows[rr_][p0:p0 + 1, :],
                                                                                  in_=ps[p0:p0 + 1, pb_ * 512:(pb_ + 1) * 512]))(pb_, p0, rr_),
                             reads=["ps%d" % pb_], writes=["rrow%d%d" % (rr_, hh)])
                    return (qt, j, po, rr_)
                return None

            def emit_fin(pend):
                qt, j, po, rr_ = pend
                r_ = (srot.i + 1) % srot.n
                o_ = otr.next()
                for hh in range(2):
                    p0 = 64 if hh == 0 else 0
                    pb_ = po[hh]
                    bb = 2 * r_ + hh
                    S.op("pe", MMS([(bank(bb), ones_f[p0:p0 + 1, :], rrows[rr_][p0:p0 + 1, :], True, True)]),
                         reads=["rrow%d%d" % (rr_, hh), "ones_f"], writes=["ps%d" % bb])
                    hs = slice(hh * 64, (hh + 1) * 64)
                    S.op("act", ACTF(bsbs[hh][hs, :], ps[hs, bb * 512:(bb + 1) * 512], AF.Identity), reads=["ps%d" % bb],
                         writes=["bsb%d" % hh])
                    S.op("dve", TT(oTs[o_][hs, :], ps[hs, pb_ * 512:(pb_ + 1) * 512], bsbs[hh][hs, :], ALU.mult),
                         reads=["ps%d" % pb_, "bsb%d" % hh, "oT%d" % o_], writes=["oT%d" % o_])
                S.dma("sp", [(OAT[s, j, :, qt * 512:(qt + 1) * 512], oTs[o_])], "ost%d" % o_, reads=["oT%d" % o_])

            for i in range(min(LOOK, len(steps))):
                emit_S(i)
            pend = None
            pend_at = None
            for i in range(len(steps)):
                if pend is not None and i - pend_at >= 8:
                    emit_fin(pend)
                    pend = None
                if i + LOOK < len(steps):
                    emit_S(i + LOOK)
                r = emit_rest(i)
                if r is not None:
                    if pend is not None:
                        emit_fin(pend)
                    pend, pend_at = r, i
            if pend is not None:
                emit_fin(pend)
        S.barrier()

        A.reset()
        Wao = A.alloc([4, D], BF16)
        Who = A.alloc([4, D], BF16)
        Wo = A.alloc([8, D], BF16)
        xts = [A.alloc([4, D], F32) for _ in range(2)]
        oaT = [A.alloc([4, 512], BF16) for _ in range(2)]
        ohTt = [A.alloc([4, 512], BF16) for _ in range(2)]
        sgT = [A.alloc([16, 512], BF16) for _ in range(2)]
        ymT = A.alloc([8, 512], BF16)
        m1 = [A.alloc([512], F32) for _ in range(2)]
        m2 = [A.alloc([512], F32) for _ in range(2)]
        bufs = {"junk": A.alloc([D], F32), "ss2": A.alloc([8], F32), "rstd2": A.alloc([8], F32), "yt": A.alloc([D], F32)}
        S.dma("pool", [(Wao, w_att_out.rearrange("(h p) n -> p h n", p=128))], "wlda", writes=["Wao"])
        S.dma("pool", [(Who, w_hg_out.rearrange("(h p) n -> p h n", p=128))], "wldh", writes=["Who"])
        for k in range(8):
            S.dma("pool", [(Wo[:, k, :], w_o[k * 128:(k + 1) * 128, :])], "wldo", writes=["Wo"])
        psYs = [ps[:, 4 * 512:6 * 512], ps[:, 6 * 512:8 * 512]]
        yrot = Rot(2)
        arot = Rot(2)
        mrot = Rot(2)
        p4tiles = [(s, t0) for s in range(2) for t0 in range(0, TL, 512)]

        def p4_load(ti):
            s, t0 = p4tiles[ti]
            sl = ti % 2
            S.dma("sp", [(xts[sl], X1[s, TC + t0:TC + t0 + 512, :].rearrange("(n p) d -> p n d", p=128)),
                         (oaT[sl], OAT[s, :, :, t0:t0 + 512].rearrange("h p t -> p h t")),
                         (ohTt[sl], OHT[s, :, :, t0:t0 + 512].rearrange("h p t -> p h t")),
                         (sgT[sl], SG[s, :, :, t0:t0 + 512].rearrange("c p t -> p c t"))],
                  "p4l%d" % sl, writes=["xt%d" % sl, "oaT%d" % sl, "ohTt%d" % sl, "sgT%d" % sl])

        p4_load(0)
        for ti, (s, t0) in enumerate(p4tiles):
            if True:
                sl = ti % 2
                xt = xts[sl]
                kx = "xt%d" % sl
                if ti + 1 < len(p4tiles):
                    p4_load(ti + 1)
                for c in range(8):
                    pa = arot.next()
                    pbk = 2 + arot.i
                    lst = [(bank(pa), Wao[:, h, c * 128:(c + 1) * 128], oaT[sl][:, h, :], h == 0, h == 3) for h in range(4)]
                    lst += [(bank(pbk), Who[:, h, c * 128:(c + 1) * 128], ohTt[sl][:, h, :], h == 0, h == 3) for h in range(4)]
                    S.op("pe", MMS(lst), reads=["Wao", "Who", "oaT%d" % sl, "ohTt%d" % sl], writes=["ps%d" % pa, "ps%d" % pbk])
                    m_ = mrot.next()
                    S.op("dve", TT(m1[m_], bank(pa), sgT[sl][:, c, :], ALU.mult), reads=["ps%d" % pa, "sgT%d" % sl], writes=["m1%d" % m_])
                    S.op("dve", TT(m2[m_], bank(pbk), sgT[sl][:, 8 + c, :], ALU.mult), reads=["ps%d" % pbk, "sgT%d" % sl],
                         writes=["m2%d" % m_])
                    S.op("pool", TT(ymT[:, c, :], m1[m_], m2[m_], ALU.add), reads=["m1%d" % m_, "m2%d" % m_, "ymT"], writes=["ymT"])
                for n in range(4):
                    y_ = yrot.next()
                    psY = psYs[y_]
                    lst = []
                    for half in range(2):
                        lst += [(psY[:, half * 512:(half + 1) * 512], ymT[:, c, n * 128:(n + 1) * 128],
                                 Wo[:, c, half * 512:(half + 1) * 512], c == 0, c == 7) for c in range(8)]
                    S.op("pe", MMS(lst), reads=["ymT", "Wo"], writes=["psY%d" % y_])
                    post_resid(psY, "psY%d" % y_, xt[:, n, :], kx, Gb[(1, s)], "Gb", bufs)
                S.dma("sp", [(X1[s, TC + t0:TC + t0 + 512, :].rearrange("(n p) d -> p n d", p=128), xt)], "xst%d" % sl, reads=[kx])
        S.barrier()

        tiles = []
        for s in range(2):
            for t0 in range(0, TL, 256):
                tiles.append((X1[s, TC + t0:TC + t0 + 256, :], out[s, t0:t0 + 256, :], s))
        ffn_phase(1, tiles)
        S.emit(nc, st)
    return nc


def _constants():
    ident = np.eye(128, dtype=np.float32)
    p = np.arange(128)
    bd64 = ((p[:, None] // 64) == (p[None, :] // 64)).astype(np.float32) / 64.0
    perm = np.zeros((128, 128), np.float32)
    for m in range(128):
        if (m % 32) < 16:
            perm[m + 16, m] = -1.0
        else:
            perm[m - 16, m] = 1.0
    c_mat = np.concatenate([ident, bd64, perm], axis=1)
    t = np.arange(TL)
    row = (t // 64).astype(np.float32)
    col = (t % 64).astype(np.float32)
    inv_freq = (np.float32(10000.0) ** (-np.arange(16, dtype=np.float32) / np.float32(16))).astype(np.float32)
    ang_r = row[:, None] * inv_freq
    ang_c = col[:, None] * inv_freq
    ang = np.concatenate([ang_r, ang_r, ang_c, ang_c], axis=-1).astype(np.float32)
    cosT = np.cos(ang).T.astype(np.float32)
    sinT = np.sin(ang).T.astype(np.float32)
    c_rope = np.stack([np.concatenate([cosT, cosT], 0), np.concatenate([sinT, sinT], 0)], axis=1)
    c_hg = np.zeros((128, 2, 640), np.float32)
    tt = np.arange(128)
    same = (tt[:, None] // 64) == (tt[None, :] // 64)
    for d in range(2):
        if d == 0:
            upto = same & (tt[:, None] <= tt[None, :])
            mid = (tt // 64) * 64 + 31
        else:
            upto = same & (tt[:, None] >= tt[None, :])
            mid = (tt // 64) * 64 + 32
        after = same & ~upto
        upto_mid = upto[:, mid]
        M1 = after.astype(np.float32)
        M2 = upto_mid.astype(np.float32) - upto.astype(np.float32)
        c_hg[:, d, 0:128] = M1
        c_hg[:, d, 128:256] = M2
        c_hg[:, d, 256:384] = -M2
        c_hg[:, d, 384:512] = upto.astype(np.float32)
        c_hg[:, d, 512:640] = upto.astype(np.float32)
    return c_mat, np.ascontiguousarray(c_rope), c_hg


_NC_CACHE = {}


def kernel(x, c, ctx, c_ctx, w_mod, b_mod, norm_pre, norm_post, ffn_w_gate, ffn_w_up, ffn_w_down,
           w_in, q_norm, k_norm, hg_lower_bound, hg_norm, w_att_out, w_hg_out, w_o, _debug=False):
    f32 = lambda a: np.ascontiguousarray(np.asarray(a, dtype=np.float32))
    x, c, ctx, c_ctx = f32(x), f32(c), f32(ctx), f32(c_ctx)
    c_mat, c_rope, c_hg = _constants()
    key = bool(_debug)
    if key not in _NC_CACHE:
        _NC_CACHE[key] = build_program(debug=_debug)
    nc = _NC_CACHE[key]
    b_mod1 = f32(b_mod)[0]
    shared = {
        "w_mod": f32(w_mod)[0], "b_mod": b_mod1,
        "bmodT": np.ascontiguousarray(b_mod1.reshape(9, 8, 128).transpose(2, 0, 1).reshape(128, 72)),
        "npreT": np.ascontiguousarray(f32(norm_pre)[0].reshape(3, 8, 128).transpose(2, 0, 1).reshape(128, 24)),
        "norm_post": f32(norm_post)[0],
        "ffn_w_gate": f32(ffn_w_gate)[0], "ffn_w_up": f32(ffn_w_up)[0], "ffn_w_down": f32(ffn_w_down)[0],
        "w_in": f32(w_in)[0],
        "qk_gain": np.ascontiguousarray(np.stack([np.tile(f32(q_norm)[0], 2), np.tile(f32(k_norm)[0], 2)], axis=1)),
        "hg_lower_bound": f32(hg_lower_bound), "hg_norm": f32(hg_norm)[0],
        "w_att_out": f32(w_att_out)[0], "w_hg_out": f32(w_hg_out)[0], "w_o": f32(w_o)[0],
        "c_mat": c_mat, "c_rope": c_rope, "c_hg": c_hg,
    }
    in_maps = []
    for i in range(8):
        cv = np.stack([c[2 * i], c[2 * i + 1], c_ctx], axis=0)
        cT = np.ascontiguousarray(cv.reshape(3, 8, 128).transpose(2, 1, 0).reshape(128, 24))
        m = dict(shared)
        m["x"] = np.ascontiguousarray(x[2 * i:2 * i + 2])
        m["ctx"] = np.ascontiguousarray(ctx[2 * i:2 * i + 2])
        m["cT"] = cT
        in_maps.append(m)
    res = run_bass_kernel_spmd(nc, in_maps, core_ids=list(range(8)))
    if _debug:
        return res
    return np.concatenate([r["out"] for r in res.results], axis=0)
```

```python
import numpy as np
from contextlib import ExitStack
import concourse.bass as bass
import concourse.mybir as mybir
from concourse.bass_utils import run_bass_kernel_spmd

F32 = mybir.dt.float32
BF16 = mybir.dt.bfloat16
F32R = mybir.dt.float32r
AF = mybir.ActivationFunctionType
ALU = mybir.AluOpType

D = 1024
FF = 2816
NFC = FF // 128
DIN = 5376
TC = 256
TL = 2048
T = TC + TL
EPS = 1e-6
ATT_SCALE = 64 ** -0.5
HG_SCALE = 128 ** -0.5
OQ, OK_, OV, OHQ, OHV, OFF, OFB, OHG, OGA = 0, 512, 640, 768, 1280, 1792, 2304, 2816, 3328


class _Op:
    __slots__ = ("eng", "fn", "deps", "need_inc", "cnt", "dma_sem", "dma_val", "ndma", "idx")


class Sched:
    COMPUTE = ("pe", "act", "dve", "pool")
    ENGS = ("pe", "act", "dve", "pool", "sp")

    def __init__(self):
        self.ops = {e: [] for e in self.ENGS}
        self.last_w = {}
        self.readers = {}
        self.dma_tot = {}
        self.all_dma = []
        self.nops = 0

    def _new(self, eng, fn):
        op = _Op()
        op.eng = eng
        op.fn = fn
        op.deps = []
        op.need_inc = False
        op.cnt = None
        op.dma_sem = None
        op.dma_val = None
        op.ndma = 0
        op.idx = self.nops
        self.nops += 1
        return op

    def _add(self, eng, fn, reads, writes, dma_key=None, ndma=0):
        op = self._new(eng, fn)
        if dma_key is not None:
            op.dma_sem = dma_key
            op.ndma = ndma
            self.dma_tot[dma_key] = self.dma_tot.get(dma_key, 0) + 16 * ndma
            op.dma_val = self.dma_tot[dma_key]
            self.all_dma.append(op)
        deps = {}
        for k in reads:
            w = self.last_w.get(k)
            if w is not None:
                deps[w.idx] = w
        for k in writes:
            w = self.last_w.get(k)
            if w is not None:
                deps[w.idx] = w
            for r in self.readers.get(k, ()):
                deps[r.idx] = r
        for d in deps.values():
            if d is op:
                continue
            if d.dma_sem is None and d.eng == "pe" and eng == "pe" and dma_key is None:
                continue
            op.deps.append(d)
            if d.dma_sem is None:
                d.need_inc = True
        for k in writes:
            self.last_w[k] = op
            self.readers[k] = []
        for k in reads:
            self.readers.setdefault(k, []).append(op)
        self.ops[eng].append(op)
        return op

    def op(self, eng, fn, reads=(), writes=()):
        return self._add(eng, fn, list(reads), list(writes))

    def dma(self, eng, pairs, sem_key, reads=(), writes=()):
        pairs = list(pairs)
        fn = lambda e: [e.dma_start(out=o, in_=i) for (o, i) in pairs]
        return self._add(eng, fn, list(reads), list(writes), dma_key=sem_key, ndma=len(pairs))

    def barrier(self):
        lasts = []
        for e in self.COMPUTE:
            for o in reversed(self.ops[e]):
                if o.dma_sem is None and o.fn is not None:
                    lasts.append(o)
                    break
        dmas = list(self.all_dma)
        self.all_dma = []
        for e in self.ENGS:
            op = self._new(e, None)
            for d in lasts:
                if d.dma_sem is None and d.eng != e and d.fn is not None:
                    d.need_inc = True
                    op.deps.append(d)
            for d in dmas:
                op.deps.append(d)
            self.ops[e].append(op)
        self.last_w = {}
        self.readers = {}

    def emit(self, nc, stack):
        esem = {e: stack.enter_context(nc.semaphore("s_" + e)) for e in self.COMPUTE}
        dsem = {k: stack.enter_context(nc.semaphore("d_" + str(k))) for k in self.dma_tot}
        for e in self.COMPUTE:
            c = 0
            for op in self.ops[e]:
                if op.need_inc:
                    c += 1
                    op.cnt = c
        ops = self.ops

        def run(e, eng):
            waited = {}
            for op in ops[e]:
                for d in op.deps:
                    if d.dma_sem is not None:
                        key, val, sem = ("d", d.dma_sem), d.dma_val, dsem[d.dma_sem]
                    else:
                        key, val, sem = ("e", d.eng), d.cnt, esem[d.eng]
                    if waited.get(key, 0) >= val:
                        continue
                    waited[key] = val
                    eng.wait_ge(sem, val)
                if op.fn is None:
                    continue
                r = op.fn(eng)
                if op.dma_sem is not None:
                    for ins in r:
                        ins.then_inc(dsem[op.dma_sem], 16)
                elif op.need_inc:
                    r.then_inc(esem[e], 1)

        with nc.Block() as block:
            @block.tensor
            def _(eng):
                run("pe", eng)

            @block.scalar
            def _(eng):
                run("act", eng)

            @block.vector
            def _(eng):
                run("dve", eng)

            @block.gpsimd
            def _(eng):
                run("pool", eng)

            @block.sync
            def _(eng):
                run("sp", eng)


def MMS(lst):
    lst = list(lst)

    def f(e):
        r = None
        for (o, l, rr, s, p) in lst:
            r = e.matmul(o, lhsT=l, rhs=rr, start=s, stop=p)
        return r
    return f


def TRS(lst, ident):
    lst = list(lst)

    def f(e):
        r = None
        for (o, i) in lst:
            r = e.transpose(out=o, in_=i, identity=ident)
        return r
    return f


def ACTF(out, in_, func, **kw):
    return lambda e: e.activation(out=out, in_=in_, func=func, **kw)


def TT(out, in0, in1, op):
    return lambda e: e.tensor_tensor(out=out, in0=in0, in1=in1, op=op)


def TS(out, in0, s1, s2, op0, op1=None):
    if op1 is None:
        return lambda e: e.tensor_scalar(out=out, in0=in0, scalar1=s1, scalar2=None, op0=op0)
    return lambda e: e.tensor_scalar(out=out, in0=in0, scalar1=s1, scalar2=s2, op0=op0, op1=op1)


def STT(out, in0, scalar, in1, op0, op1):
    return lambda e: e.scalar_tensor_tensor(out=out, in0=in0, scalar=scalar, in1=in1, op0=op0, op1=op1)


def CP(out, in_):
    return lambda e: e.tensor_copy(out=out, in_=in_)


def MSET(out, v):
    return lambda e: e.memset(out, v)


class Arena:
    def __init__(self, ap, nwords):
        self.ap = ap
        self.n = nwords
        self.off = 0

    def reset(self):
        self.off = 0

    def alloc(self, free_shape, dtype, parts=128):
        n = int(np.prod(free_shape))
        words = n if dtype == F32 else (n + 1) // 2
        words = (words + 7) // 8 * 8
        assert self.off + words <= self.n, ("arena overflow", self.off, words, self.n)
        v = self.ap[0:parts, self.off:self.off + words]
        self.off += words
        if dtype != F32:
            v = v.bitcast(dtype)
        v = v[:, 0:n]
        if len(free_shape) == 2:
            v = v.rearrange("p (a b) -> p a b", b=free_shape[1])
        elif len(free_shape) == 3:
            v = v.rearrange("p (a b c) -> p a b c", b=free_shape[1], c=free_shape[2])
        elif len(free_shape) == 4:
            v = v.rearrange("p (a b c d) -> p a b c d", b=free_shape[1], c=free_shape[2], d=free_shape[3])
        return v


class Rot:
    def __init__(self, n):
        self.n = n
        self.i = -1

    def next(self):
        self.i = (self.i + 1) % self.n
        return self.i


def build_program(debug=False):
    nc = bass.Bass("TRN2", target_bir_lowering=False)

    def din(name, shape, dt=F32):
        return nc.dram_tensor(name, list(shape), dt, kind="ExternalInput").ap()

    def dscr(name, shape, dt):
        kind = "ExternalOutput" if debug else "Internal"
        return nc.dram_tensor(name, list(shape), dt, kind=kind).ap()

    x_in = din("x", [2, TL, D])
    ctx_in = din("ctx", [2, TC, D])
    cT_in = din("cT", [128, 24])
    w_mod = din("w_mod", [D, 9 * D])
    b_mod = din("b_mod", [9 * D])
    bmodT_in = din("bmodT", [128, 72])
    npreT_in = din("npreT", [128, 24])
    norm_post = din("norm_post", [3, D])
    w_gate = din("ffn_w_gate", [2, D, FF])
    w_up = din("ffn_w_up", [2, D, FF])
    w_down = din("ffn_w_down", [2, FF, D])
    w_in = din("w_in", [D, DIN])
    qk_gain_in = din("qk_gain", [128, 2])
    lbraw = din("hg_lower_bound", [2, 2, 512])
    hg_norm = din("hg_norm", [128])
    w_att_out = din("w_att_out", [512, D])
    w_hg_out = din("w_hg_out", [512, D])
    w_o = din("w_o", [D, D])
    c_mat = din("c_mat", [128, 384])
    c_rope = din("c_rope", [128, 2, TL])
    c_hg = din("c_hg", [128, 2, 640])
    out = nc.dram_tensor("out", [2, TL, D], F32, kind="ExternalOutput").ap()

    X1 = dscr("X1", [2, T, D], F32)
    QT = dscr("QT", [2, 4, 128, TL], BF16)
    KT = dscr("KT", [2, 128, T], BF16)
    VV = dscr("VV", [2, T, 128], BF16)
    HQT = dscr("HQT", [2, 4, 128, TL], BF16)
    HV = dscr("HV", [2, T, 512], BF16)
    KF = dscr("KF", [2, 2, T, 512], F32)
    HG = dscr("HG", [2, TL, 512], BF16)
    SG = dscr("SG", [2, 16, 128, TL], BF16)
    OAT = dscr("OAT", [2, 4, 128, TL], BF16)
    OHT = dscr("OHT", [2, 4, 128, TL], BF16)

    def bcast(a):
        return bass.AP(a.tensor, a.offset, [[0, 128]] + [list(v) for v in a.ap])

    S = Sched()
    with ExitStack() as st:
        NP = 7700
        NR = 1536
        r32_t = st.enter_context(nc.sbuf_tensor("r32", [128, NR], F32))
        NW = (int(nc.sbuf_bytes_remaining) // 4 - NP - 64) // 8 * 8
        arena_t = st.enter_context(nc.sbuf_tensor("arena", [128, NW], F32))
        pers_t = st.enter_context(nc.sbuf_tensor("pers", [128, NP], F32))
        ps_t = st.enter_context(nc.psum_tensor("ps", [128, 4096], F32))
        A = Arena(arena_t[:, :], NW)
        PA = Arena(pers_t[:, :], NP)
        ps = ps_t[:, :]
        psb = ps.bitcast(BF16)

        def bank(b, n=512, o=0):
            return ps[:, b * 512 + o:b * 512 + o + n]

        def bankb(b, n=1024, o=0):
            return psb[:, b * 1024 + o:b * 1024 + o + n]

        ABt = PA.alloc([3, 2, 3, 8], F32)
        Gb = {}
        for (i, r) in [(0, 0), (0, 1), (0, 2), (1, 0), (1, 1), (2, 0), (2, 1)]:
            Gb[(i, r)] = PA.alloc([D], F32)
        cm_b = PA.alloc([384], BF16)
        chalf = PA.alloc([8], F32)
        ones_f = PA.alloc([128], F32)
        epsc = PA.alloc([8], F32)
        ident_b = cm_b[:, 0:128]
        bd64_b = cm_b[:, 128:256]
        perm_b = cm_b[:, 256:384]

        S.op("pool", MSET(chalf, -0.5), writes=["chalf"])
        S.op("pool", MSET(ones_f, 1.0), writes=["ones_f"])
        S.op("pool", MSET(epsc, EPS), writes=["epsc"])

        A.reset()
        Wg0 = A.alloc([8, FF], BF16)
        Wu0 = A.alloc([8, FF], BF16)
        for k in range(8):
            S.dma("pool", [(Wg0[:, k, :], w_gate[0, k * 128:(k + 1) * 128, :])], "wldg", writes=["Wg"])
            S.dma("pool", [(Wu0[:, k, :], w_up[0, k * 128:(k + 1) * 128, :])], "wldu", writes=["Wu"])
        cm_f = A.alloc([384], F32)
        S.dma("sp", [(cm_f, c_mat)], "c0", writes=["cm_f"])
        S.op("dve", CP(cm_b, cm_f), reads=["cm_f"], writes=["cm_b"])
        cT = A.alloc([24], F32)
        scT = A.alloc([24], F32)
        bmodT = A.alloc([72], F32)
        npreT = A.alloc([24], F32)
        screp = A.alloc([3, 8, 128], F32)
        wm = [A.alloc([8, 512], F32) for _ in range(2)]
        bmb = [A.alloc([D], F32) for _ in range(2)]
        gpb = [A.alloc([D], F32) for _ in range(2)]
        mtmp = [A.alloc([512], F32) for _ in range(2)]
        mT = A.alloc([6, 8, 3], F32)
        abt = A.alloc([8], F32)
        S.dma("sp", [(cT, cT_in), (bmodT, bmodT_in), (npreT, npreT_in)], "c1", writes=["cT", "bmodT", "npreT"])
        S.op("act", ACTF(scT, cT, AF.Silu), reads=["cT"], writes=["scT"])
        for r in range(3):
            for k in range(8):
                S.op("pool", TS(screp[:, r, k, :], ones_f, scT[:, k * 3 + r:k * 3 + r + 1], None, ALU.mult),
                     reads=["scT", "ones_f"], writes=["screp"])
        psM = bank(0, 144).rearrange("p (a b c) -> p a b c", b=8, c=3)
        fm_js = [0, 1, 3, 4, 6, 7]
        rb = Rot(2)
        rm = Rot(2)
        wrot = Rot(2)
        for j in range(9):
            i = j // 3
            if j not in fm_js:
                b = rb.next()
                S.dma("sp", [(bmb[b], bcast(b_mod[j * D:(j + 1) * D])),
                             (gpb[b], bcast(norm_post[i, :]))], "bg%d" % b,
                      writes=["bmb%d" % b, "gpb%d" % b])
            for half in range(2):
                sl = wrot.next()
                c0_ = j * D + half * 512
                S.dma("sp", [(wm[sl][:, k, :], w_mod[k * 128:(k + 1) * 128, c0_:c0_ + 512]) for k in range(8)],
                      "wm%d" % sl, writes=["wm%d" % sl])
                if j in fm_js:
                    jj = fm_js.index(j)
                    lst = []
                    for c in range(4):
                        for k in range(8):
                            lst.append((psM[:, jj, half * 4 + c, :], wm[sl][:, k, c * 128:(c + 1) * 128], scT[:, k * 3:k * 3 + 3],
                                        k == 0, k == 7))
                    S.op("pe", MMS(lst), reads=["wm%d" % sl, "scT"], writes=["psM"])
                else:
                    for r in ([0, 1, 2] if i == 0 else [0, 1]):
                        pb = 1 + (r * 2 + half) % 4
                        lst = [(bank(pb), screp[:, r, k, :], wm[sl][:, k, :], k == 0, k == 7) for k in range(8)]
                        S.op("pe", MMS(lst), reads=["wm%d" % sl, "screp"], writes=["ps%d" % pb])
                        m = rm.next()
                        S.op("dve", TT(mtmp[m], bank(pb), bmb[b][:, half * 512:(half + 1) * 512], ALU.add),
                             reads=["ps%d" % pb, "bmb%d" % b], writes=["mtmp%d" % m])
                        S.op("dve", STT(Gb[(i, r)][:, half * 512:(half + 1) * 512], mtmp[m], 1.0 if i == 1 else 0.5,
                                         gpb[b][:, half * 512:(half + 1) * 512], ALU.mult, ALU.mult),
                             reads=["mtmp%d" % m, "gpb%d" % b], writes=["Gb%d%d" % (i, r)])
        S.op("act", ACTF(mT, psM, AF.Identity), reads=["psM"], writes=["mT"])
        for i in range(3):
            for r in range(3):
                S.op("dve", TT(ABt[:, i, 1, r, :], mT[:, 2 * i, :, r], bmodT[:, (3 * i) * 8:(3 * i) * 8 + 8], ALU.add),
                     reads=["mT", "bmodT"], writes=["AB"])
                S.op("dve", TT(abt, mT[:, 2 * i + 1, :, r], bmodT[:, (3 * i + 1) * 8:(3 * i + 1) * 8 + 8], ALU.add),
                     reads=["mT", "bmodT"], writes=["abt"])
                S.op("dve", STT(ABt[:, i, 0, r, :], abt, 1.0, npreT[:, i * 8:i * 8 + 8], ALU.add, ALU.mult),
                     reads=["abt", "npreT"], writes=["AB"])
        S.barrier()

        def pre_stats(xt, nsub, keys_x, bufs):
            ss, rstd, junk, xn = bufs["ss"], bufs["rstd"], bufs["junk"], bufs["xn"]
            for n in range(nsub):
                S.op("act", ACTF(junk, xt[:, n, :], AF.Square, accum_out=ss[:, n:n + 1]),
                     reads=keys_x, writes=["junk", "ss"])
            S.op("pool", TS(rstd[:, 0:nsub], ss[:, 0:nsub], 1.0 / D, EPS, ALU.mult, ALU.add), reads=["ss"], writes=["rstd"])
            S.op("pool", TT(rstd[:, 0:nsub], rstd[:, 0:nsub], chalf[:, 0:nsub], ALU.pow), reads=["rstd", "chalf"],
                 writes=["rstd"])
            for n in range(nsub):
                S.op("dve", TS(xn[:, n, :], xt[:, n, :], rstd[:, n:n + 1], None, ALU.mult), reads=keys_x + ["rstd"],
                     writes=[bufs.get("shared_key", "xn%d" % n)])

        def pre_trans(nsub, i_sub, r, uT, key_u, bufs, region_rot):
            xn = bufs["xn"]
            nt = nsub * 128
            for c in range(8):
                reg, rkey = region_rot()
                lst = [(reg[:, n * 128:(n + 1) * 128], xn[:, n, c * 128:(c + 1) * 128]) for n in range(nsub)]
                S.op("pe", TRS(lst, ident_b), reads=[bufs.get("shared_key", "xn%d" % n) for n in range(nsub)] + ["cm_b"],
                     writes=[rkey])
                S.op("act", ACTF(uT[:, c, 0:nt], reg[:, 0:nt], AF.Identity, scale=ABt[:, i_sub, 0, r, c:c + 1],
                                 bias=ABt[:, i_sub, 1, r, c:c + 1]), reads=[rkey, "AB"], writes=[key_u])

        def pre_norm(xt, nsub, i_sub, r, uT, keys_x, key_u, bufs, pbank_rot):
            pre_stats(xt, nsub, keys_x, bufs)
            pre_trans(nsub, i_sub, r, uT, key_u, bufs, pbank_rot)

        def post_resid(psY, ykey, xt_n, key_x, gb, gbkey, bufs):
            junk, ss2, rstd2, yt = bufs["junk"], bufs["ss2"], bufs["rstd2"], bufs["yt"]
            S.op("act", ACTF(junk, psY, AF.Square, accum_out=ss2[:, 0:1]), reads=[ykey], writes=["junk", "ss2"])
            ytk = bufs.get("shared_key", "yt")
            S.op("dve", TT(yt, psY, gb, ALU.mult), reads=[ykey, gbkey, "ss2"], writes=[ytk])
            S.op("pool", TS(rstd2[:, 0:1], ss2[:, 0:1], 1.0 / D, EPS, ALU.mult, ALU.add), reads=["ss2"], writes=["rstd2"])
            S.op("pool", TT(rstd2[:, 0:1], rstd2[:, 0:1], chalf[:, 0:1], ALU.pow), reads=["rstd2", "chalf"], writes=["rstd2"])
            S.op("dve", STT(xt_n, yt, rstd2[:, 0:1], xt_n, ALU.mult, ALU.add), reads=[ytk, "rstd2", key_x], writes=[key_x])

        def ffn_phase(fi, tiles, preloaded=False):
            A.reset()
            Wg = A.alloc([8, FF], BF16)
            Wu = A.alloc([8, FF], BF16)
            Wd = A.alloc([NFC, D], BF16)
            xts = [A.alloc([2, D], F32) for _ in range(2)]
            uT = A.alloc([8, 256], BF16)
            hT = A.alloc([NFC, 256], BF16)
            sg = [A.alloc([256], BF16) for _ in range(2)]
            shr = A.alloc([D], F32)
            bufs = {"ss": A.alloc([8], F32), "rstd": A.alloc([8], F32), "junk": A.alloc([D], BF16),
                    "xn": shr.bitcast(BF16).rearrange("p (a b) -> p a b", b=D), "ss2": A.alloc([8], F32),
                    "rstd2": A.alloc([8], F32), "yt": shr, "shared_key": "shr"}
            fgroups = [(0, 6), (6, 12), (12, 18), (18, NFC)]

            def fgrp(f):
                for gi_, (a_, b_) in enumerate(fgroups):
                    if a_ <= f < b_:
                        return gi_

            if not preloaded:
                for gi_, (a_, b_) in enumerate(fgroups):
                    S.dma("pool", [(Wg[:, k, a_ * 128:b_ * 128], w_gate[fi, k * 128:(k + 1) * 128, a_ * 128:b_ * 128])
                                   for k in range(8)], "wldg%d" % gi_, writes=["Wg%d" % gi_])
                    S.dma("pool", [(Wu[:, k, a_ * 128:b_ * 128], w_up[fi, k * 128:(k + 1) * 128, a_ * 128:b_ * 128])
                                   for k in range(8)], "wldu%d" % gi_, writes=["Wu%d" % gi_])
            for f0 in range(0, NFC, 2):
                S.dma("pool", [(Wd[:, f, :], w_down[fi, f * 128:(f + 1) * 128, :]) for f in (f0, f0 + 1)], "wldd",
                      writes=["Wd"])
            psYs = [ps[:, 4 * 512:6 * 512], ps[:, 6 * 512:8 * 512]]
            trot = Rot(2)
            grot = Rot(2)
            srot = Rot(2)
            yrot = Rot(2)

            def tr_region():
                i_ = trot.next()
                return bankb(i_, 256), "ps%d" % i_

            def f_stats(ti):
                src, dst, r = tiles[ti]
                sl = ti % 2
                S.dma("sp", [(xts[sl], src.rearrange("(n p) d -> p n d", p=128))], "xld%d" % sl, writes=["xt%d" % sl])
                pre_stats(xts[sl], 2, ["xt%d" % sl], bufs)

            def f_trans(ti):
                src, dst, r = tiles[ti]
                pre_trans(2, 2 * fi, r, uT, "uT", bufs, tr_region)

            def f_gu(ti):
                for f in range(NFC):
                    pb = 2 + grot.next()
                    lst = [(bank(pb, 256, 0), Wg[:, k, f * 128:(f + 1) * 128], uT[:, k, :], k == 0, k == 7) for k in range(8)]
                    lst += [(bank(pb, 256, 256), Wu[:, k, f * 128:(f + 1) * 128], uT[:, k, :], k == 0, k == 7) for k in range(8)]
                    S.op("pe", MMS(lst), reads=["Wg%d" % fgrp(f), "Wu%d" % fgrp(f), "uT"], writes=["ps%d" % pb])
                    s_ = srot.next()
                    S.op("act", ACTF(sg[s_], bank(pb, 256, 0), AF.Silu), reads=["ps%d" % pb], writes=["sg%d" % s_])
                    S.op("dve", TT(hT[:, f, :], bank(pb, 256, 256), sg[s_], ALU.mult), reads=["ps%d" % pb, "sg%d" % s_],
                         writes=["hT"])

            def f_down(ti):
                src, dst, r = tiles[ti]
                sl = ti % 2
                xt = xts[sl]
                kx = "xt%d" % sl
                for n in range(2):
                    y_ = yrot.next()
                    psY = psYs[y_]
                    lst = []
                    for half in range(2):
                        lst += [(psY[:, half * 512:(half + 1) * 512], hT[:, f, n * 128:(n + 1) * 128],
                                 Wd[:, f, half * 512:(half + 1) * 512], f == 0, f == NFC - 1) for f in range(NFC)]
                    S.op("pe", MMS(lst), reads=["hT", "Wd"], writes=["psY%d" % y_])
                    post_resid(psY, "psY%d" % y_, xt[:, n, :], kx, Gb[(2 * fi, r)], "Gb", bufs)
                S.dma("sp", [(dst.rearrange("(n p) d -> p n d", p=128), xt)], "xst%d" % sl, reads=[kx])

            f_stats(0)
            f_trans(0)
            for ti in range(len(tiles)):
                if ti + 1 < len(tiles):
                    f_stats(ti + 1)
                f_gu(ti)
                if ti + 1 < len(tiles):
                    f_trans(ti + 1)
                f_down(ti)
            S.barrier()

        tiles = []
        for s in range(2):
            tiles.append((ctx_in[s, :, :], X1[s, 0:TC, :], 2))
            for t0 in range(0, TL, 256):
                tiles.append((x_in[s, t0:t0 + 256, :], X1[s, TC + t0:TC + t0 + 256, :], s))
        ffn_phase(0, tiles, preloaded=True)

        A.reset()
        Win = A.alloc([8, DIN], BF16)
        rps = [A.alloc([2, 512], F32) for _ in range(2)]
        lbt = A.alloc([2, 2, 512], F32)
        qkg = A.alloc([2], F32)
        xts = [A.alloc([4, D], F32) for _ in range(1)]
        uTs = [A.alloc([8, 512], BF16) for _ in range(2)]
        bufs = {"ss": A.alloc([8], F32), "rstd": A.alloc([8], F32), "junk": A.alloc([D], BF16),
                "xn": A.alloc([4, D], BF16)}
        sqs = [A.alloc([512], BF16) for _ in range(2)]
        msqs = [A.alloc([512], F32) for _ in range(2)]
        xgs = [A.alloc([512], BF16) for _ in range(2)]
        t12 = A.alloc([4, 512], F32)
        t1s = [t12[:, 0, :], t12[:, 1, :]]
        t2s = [t12[:, 2, :], t12[:, 3, :]]
        lbl = t12.rearrange("p (a b) c -> p a b c", b=2)
        fo = [A.alloc([512], BF16) for _ in range(3)]
        tsig = [A.alloc([512], F32) for _ in range(2)]
        tv = [A.alloc([512], BF16) for _ in range(2)]
        tv2 = [A.alloc([128], BF16) for _ in range(2)]
        wgroups = [(0, 768), (1280, 1792), (1792, 2816), (768, 1280), (2816, 3328), (3328, DIN)]

        def wgrp(col0):
            for gi_, (a_, b_) in enumerate(wgroups):
                if a_ <= col0 < b_:
                    return gi_

        for gi_, (a_, b_) in enumerate(wgroups):
            S.dma("pool", [(Win[:, k, a_:b_], w_in[k * 128:(k + 1) * 128, a_:b_]) for k in range(8)], "wldi%d" % gi_,
                  writes=["Win%d" % gi_])
        S.dma("sp", [(qkg, qk_gain_in)], "c2r", writes=["qkg"])
        lblk = [["t10", "t11"], ["t20", "t21"]]
        for d in range(2):
            S.dma("sp", [(lbl[:, d, sl_, :], bcast(lbraw[d, sl_, :])) for sl_ in range(2)], "c2l%d" % d,
                  writes=lblk[d])
        for d in range(2):
            S.op("dve", TT(lbl[:, d, 0, :], lbl[:, d, 0, :], lbl[:, d, 1, :], ALU.subtract), reads=lblk[d], writes=[lblk[d][0]])
            S.op("act", ACTF(lbt[:, d, 0, :], lbl[:, d, 0, :], AF.Sigmoid), reads=[lblk[d][0], "lbt"], writes=["lbt"])
            S.op("act", ACTF(lbt[:, d, 1, :], lbl[:, d, 0, :], AF.Sigmoid, scale=-1.0), reads=[lblk[d][0], "lbt"], writes=["lbt"])

        frot = Rot(3)
        prot = Rot(4)
        trot = Rot(2)
        qrot = Rot(2)
        tkr = Rot(2)
        crot = Rot(2)

        def tr_region2():
            i_ = trot.next()
            return bankb(i_, 512), "ps%d" % i_

        def fm_proj(col0, nt, u_):
            pb = 2 + prot.next()
            lst = [(bank(pb, nt), Win[:, k, col0:col0 + 128], uTs[u_][:, k, 0:nt], k == 0, k == 7) for k in range(8)]
            S.op("pe", MMS(lst), reads=["Win%d" % wgrp(col0), "uT%d" % u_], writes=["ps%d" % pb])
            return pb

        def qk_A(col0, nt, u_):
            pb = fm_proj(col0, nt, u_)
            P1 = bank(pb, nt)
            c_ = crot.next()
            sq, msq = sqs[c_], msqs[c_]
            S.op("act", ACTF(sq[:, 0:nt], P1, AF.Square), reads=["ps%d" % pb], writes=["sq%d" % c_])
            q2 = 6 + qrot.next()
            S.op("pe", MMS([(bank(q2, nt), bd64_b, sq[:, 0:nt], True, True)]), reads=["sq%d" % c_, "cm_b"], writes=["ps%d" % q2])
            S.op("act", ACTF(msq[:, 0:nt], bank(q2, nt), AF.Ln, bias=epsc[:, 0:1]), reads=["ps%d" % q2, "epsc"],
                 writes=["msq%d" % c_])
            S.op("act", ACTF(msq[:, 0:nt], msq[:, 0:nt], AF.Exp, scale=-0.5), reads=["msq%d" % c_], writes=["msq%d" % c_])
            return (pb, c_)

        def qk_B(st_, nt, gcol, rp_, dst):
            pb, c_ = st_
            P1 = bank(pb, nt)
            msq, xg, t1, t2 = msqs[c_], xgs[c_], t1s[c_], t2s[c_]
            f_ = frot.next()
            if rp_ is None:
                S.op("dve", STT(fo[f_][:, 0:nt], P1, qkg[:, gcol:gcol + 1], msq[:, 0:nt], ALU.mult, ALU.mult),
                     reads=["ps%d" % pb, "msq%d" % c_, "qkg"], writes=["fo%d" % f_])
            else:
                S.op("dve", STT(xg[:, 0:nt], P1, qkg[:, gcol:gcol + 1], msq[:, 0:nt], ALU.mult, ALU.mult),
                     reads=["ps%d" % pb, "msq%d" % c_, "qkg"], writes=["xg%d" % c_])
                q3 = 6 + qrot.next()
                S.op("pe", MMS([(bank(q3, nt), perm_b, xg[:, 0:nt], True, True)]), reads=["xg%d" % c_, "cm_b"], writes=["ps%d" % q3])
                S.op("pool", TT(t1[:, 0:nt], xg[:, 0:nt], rps[rp_][:, 0, 0:nt], ALU.mult), reads=["xg%d" % c_, "rp%d" % rp_],
                     writes=["t1%d" % c_])
                S.op("dve", TT(t2[:, 0:nt], bank(q3, nt), rps[rp_][:, 1, 0:nt], ALU.mult), reads=["ps%d" % q3, "rp%d" % rp_],
                     writes=["t2%d" % c_])
                S.op("dve", TT(fo[f_][:, 0:nt], t1[:, 0:nt], t2[:, 0:nt], ALU.add), reads=["t1%d" % c_, "t2%d" % c_],
                     writes=["fo%d" % f_])
            S.dma("sp", [(dst, fo[f_][:, 0:nt])], "fo%d" % f_, reads=["fo%d" % f_])

        def fm_act(col0, nt, func, dst, u_, scale_after=None):
            pb = fm_proj(col0, nt, u_)
            f_ = frot.next()
            S.op("act", ACTF(fo[f_][:, 0:nt], bank(pb, nt), func), reads=["ps%d" % pb], writes=["fo%d" % f_])
            if scale_after is not None:
                S.op("pool", TS(fo[f_][:, 0:nt], fo[f_][:, 0:nt], scale_after, None, ALU.mult), reads=["fo%d" % f_],
                     writes=["fo%d" % f_])
            S.dma("sp", [(dst, fo[f_][:, 0:nt])], "fo%d" % f_, reads=["fo%d" % f_])

        def tm_proj(col0, ncols, n, u_):
            pb = 2 + prot.next()
            lst = [(bank(pb, ncols), uTs[u_][:, k, n * 128:(n + 1) * 128], Win[:, k, col0:col0 + ncols], k == 0, k == 7)
                   for k in range(8)]
            S.op("pe", MMS(lst), reads=["Win%d" % wgrp(col0), "uT%d" % u_], writes=["ps%d" % pb])
            return pb

        p2tiles = []
        for s in range(2):
            p2tiles.append((s, 0, TC, False))
            for t0 in range(0, TL, 512):
                p2tiles.append((s, TC + t0, 512, True))

        def p2_load(ti):
            s, tg0, nt, lat = p2tiles[ti]
            nsub = nt // 128
            u_ = ti % 2
            S.dma("sp", [(xts[0][:, 0:nsub, :], X1[s, tg0:tg0 + nt, :].rearrange("(n p) d -> p n d", p=128))], "xld0",
                  writes=["xt0"])
            if lat:
                lt0 = tg0 - TC
                S.dma("sp", [(rps[u_], c_rope[:, :, lt0:lt0 + 512])], "rp%d" % u_, writes=["rp%d" % u_])

        def p2_stats(ti):
            s, tg0, nt, lat = p2tiles[ti]
            pre_stats(xts[0], nt // 128, ["xt0"], bufs)

        def p2_trans(ti):
            s, tg0, nt, lat = p2tiles[ti]
            nsub = nt // 128
            u_ = ti % 2
            pre_trans(nsub, 1, (s if lat else 2), uTs[u_], "uT%d" % u_, bufs, tr_region2)

        p2_load(0)
        p2_stats(0)
        p2_trans(0)
        for ti, (s, tg0, nt, lat) in enumerate(p2tiles):
            nsub = nt // 128
            u_ = ti % 2
            lt0 = tg0 - TC
            if ti + 1 < len(p2tiles):
                p2_load(ti + 1)
            if lat:
                for h in range(4):
                    fm_act(OHQ + h * 128, nt, AF.Silu, HQT[s, h, :, lt0:lt0 + nt], u_)
                for n in range(nsub):
                    pb = tm_proj(OHG, 512, n, u_)
                    k_ = tkr.next()
                    S.op("act", ACTF(tv[k_], bank(pb), AF.Silu), reads=["ps%d" % pb], writes=["tv%d" % k_])
                    S.dma("sp", [(HG[s, lt0 + n * 128:lt0 + (n + 1) * 128, :], tv[k_])], "tv%d" % k_, reads=["tv%d" % k_])
            chunks = []
            if lat:
                for j in range(4):
                    chunks.append((OQ + j * 128, 0, u_, QT[s, j, :, lt0:lt0 + nt]))
            chunks.append((OK_, 1, (u_ if lat else None), KT[s, :, tg0:tg0 + nt]))
            prev = None
            for (col0_, gcol_, rp__, dst_) in chunks:
                st_ = qk_A(col0_, nt, u_)
                if prev is not None:
                    qk_B(prev[0], nt, prev[1], prev[2], prev[3])
                prev = (st_, gcol_, rp__, dst_)
            qk_B(prev[0], nt, prev[1], prev[2], prev[3])
            if ti + 1 < len(p2tiles):
                p2_stats(ti + 1)
                p2_trans(ti + 1)
            for n in range(nsub):
                pb = tm_proj(OV, 128, n, u_)
                k_ = tkr.next()
                S.op("act", ACTF(tv2[k_], bank(pb, 128), AF.Identity), reads=["ps%d" % pb], writes=["tv2%d" % k_])
                S.dma("sp", [(VV[s, tg0 + n * 128:tg0 + (n + 1) * 128, :], tv2[k_])], "tv2%d" % k_, reads=["tv2%d" % k_])
                pb = tm_proj(OHV, 512, n, u_)
                S.op("dve", CP(tv[k_], bank(pb)), reads=["ps%d" % pb], writes=["tv%d" % k_])
                S.dma("sp", [(HV[s, tg0 + n * 128:tg0 + (n + 1) * 128, :], tv[k_])], "tv%d" % k_, reads=["tv%d" % k_])
            if lat:
                for c in range(16):
                    fm_act(OGA + c * 128, nt, AF.Sigmoid, SG[s, c, :, lt0:lt0 + nt], u_)
            for n in range(nsub):
                for d in range(2):
                    pb = tm_proj(OFF + d * 512, 512, n, u_)
                    k_ = tkr.next()
                    S.op("act", ACTF(tsig[k_], bank(pb), AF.Sigmoid, scale=-1.0), reads=["ps%d" % pb], writes=["tsig%d" % k_])
                    S.op("dve", TT(tsig[k_], tsig[k_], lbt[:, d, 1, :], ALU.mult), reads=["tsig%d" % k_, "lbt"],
                         writes=["tsig%d" % k_])
                    rows = slice(tg0 + n * 128, tg0 + (n + 1) * 128)
                    S.dma("sp", [(KF[s, d, rows, :], tsig[k_])], "tgk%d" % k_, reads=["tsig%d" % k_])
        S.barrier()

        A.reset()
        chg = A.alloc([2, 640], F32)
        hgn = A.alloc([128], F32)
        Obuf = A.alloc([16, 512], F32)
        Sst = A.alloc([2, 4, 128], F32)
        SbfIn = [A.alloc([4, 128], BF16) for _ in range(2)]
        SbfMid = A.alloc([4, 128], BF16)
        NL = 5
        kts = [A.alloc([512], F32) for _ in range(NL)]
        vts = [A.alloc([512], BF16) for _ in range(NL)]
        hqs = [A.alloc([4, 128], BF16) for _ in range(NL)]
        ogs = [A.alloc([512], BF16) for _ in range(NL)]
        E1 = A.alloc([512], F32)
        E2 = A.alloc([512], F32)
        kds = A.alloc([512], BF16)
        kd2 = A.alloc([512], BF16)
        kd2T = A.alloc([4, 128], BF16)
        qdT = A.alloc([4, 128], BF16)
        EFs = [A.alloc([4, 256], F32) for _ in range(2)]
        dsbs = [A.alloc([2, 4, 128], F32) for _ in range(2)]
        qepads = [A.alloc([4, 2, 128], BF16) for _ in range(2)]
        sTms = [A.alloc([4, 128], BF16) for _ in range(2)]
        osums = [A.alloc([512], F32) for _ in range(2)]
        hjunk = A.alloc([128], F32)
        hss = A.alloc([8], F32)
        hrs = A.alloc([8], F32)
        gg = A.alloc([512], F32)
        on = A.alloc([512], BF16)
        ohT = [A.alloc([4, 128], BF16) for _ in range(2)]
        S.dma("sp", [(chg, c_hg), (hgn, bcast(hg_norm))], "c3", writes=["chg", "hgn"])
        r32 = r32_t[:, :].bitcast(F32R)
        chr_r = r32[:, 0:1024].rearrange("p (a b) -> p a b", b=512)
        gt_r = r32[:, 1024:1536]
        for d_ in range(2):
            S.op("dve", CP(chr_r[:, d_, :], chg[:, d_, 0:512]), reads=["chg"], writes=["chr"])
        for q_ in range(2):
            S.op("pool", MSET(qepads[q_], 0.0), writes=["qepad%d" % q_])
        PE_, PT_, PFa, PFb, PST, POh, PD0, PD1 = range(8)
        lrot = Rot(NL)
        orot = Rot(2)

        items = []
        for s in range(2):
            for d in range(2):
                order = list(range(18)) if d == 0 else [1, 0] + list(range(17, 1, -1))
                for oi, sub in enumerate(order):
                    items.append((s, d, sub, oi == 0))

        def hg_load(it):
            s, d, sub, is_first = items[it]
            lat = sub >= 2
            rows = slice(sub * 128, (sub + 1) * 128)
            lt0 = (sub - 2) * 128
            l_ = lrot.next()
            pairs = [(kts[l_], KF[s, d, rows, :]), (vts[l_], HV[s, rows, :])]
            wk = ["kt%d" % l_, "vt%d" % l_]
            if lat:
                pairs.append((hqs[l_], HQT[s, :, :, lt0:lt0 + 128].rearrange("h p t -> p h t")))
                wk.append("hq%d" % l_)
                if d == 1:
                    pairs.append((ogs[l_], HG[s, lt0:lt0 + 128, :]))
                    wk.append("og%d" % l_)
            S.dma("sp", pairs, "hl%d" % l_, writes=wk)
            return l_

        def hg_A1(it, slot, l_):
            s, d, sub, is_first = items[it]
            M1 = chr_r[:, d, 0:128]
            M2 = chr_r[:, d, 128:256]
            Rm = chr_r[:, d, 256:512]
            mask = chg[:, d, 512:640]
            lat = sub >= 2
            rows = slice(sub * 128, (sub + 1) * 128)
            lt0 = (sub - 2) * 128
            kt, vt, hq, og = kts[l_], vts[l_], hqs[l_], ogs[l_]
            EF, dsb, qepad, sTm = EFs[slot], dsbs[slot], qepads[slot], sTms[slot]
            gt = gt_r
            S.op("act", ACTF(gt, kt, AF.Ln, scale=-1.0, bias=ones_f[:, 0:1]), reads=["kt%d" % l_, "ones_f"],
                 writes=["gtr"])
            S.op("pe", MMS([(bank(PE_), M1, gt, True, True)]), reads=["gtr", "chr"], writes=["ps0"])
            S.op("act", ACTF(E1, bank(PE_), AF.Exp), reads=["ps0"], writes=["E1"])
            S.op("pool", TT(kds, kt, E1, ALU.mult), reads=["kt%d" % l_, "E1"], writes=["kds"])
            if lat:
                S.op("pe", MMS([(bank(PE_), M2, gt, True, True)]), reads=["gtr", "chr"], writes=["ps0"])
                S.op("act", ACTF(E2, bank(PE_), AF.Exp), reads=["ps0"], writes=["E2"])
                S.op("pool", TT(kd2, kt, E2, ALU.mult), reads=["kt%d" % l_, "E2"], writes=["kd2"])
            lst = []
            for h in range(4):
                pbF = PFa if h < 2 else PFb
                lst.append((bank(pbF, 256, (h % 2) * 256), gt[:, h * 128:(h + 1) * 128], Rm, True, True))
            S.op("pe", MMS(lst), reads=["gtr", "chr"], writes=["ps2", "ps3"])
            S.op("act", ACTF(EF[:, 0:2, :], bank(PFa).rearrange("p (h t) -> p h t", t=256), AF.Exp), reads=["ps2"],
                 writes=["EF%d" % slot])
            S.op("act", ACTF(EF[:, 2:4, :], bank(PFb).rearrange("p (h t) -> p h t", t=256), AF.Exp), reads=["ps3", "EF%d" % slot],
                 writes=["EF%d" % slot])
            lst = []
            for j in range(2):
                for h in range(4):
                    lst.append((bank(PD0 + j, 128, h * 128), kds[j * 64:(j + 1) * 64, h * 128:(h + 1) * 128],
                                vt[j * 64:(j + 1) * 64, h * 128:(h + 1) * 128], True, True))
            S.op("pe", MMS(lst), reads=["kds", "vt%d" % l_], writes=["ps6", "ps7"])
            S.op("act", ACTF(dsb.rearrange("p a b c -> p (a b c)"), ps[:, 6 * 512:8 * 512], AF.Identity), reads=["ps6", "ps7"],
                 writes=["dsb%d" % slot])

        def hg_A2(it, slot, l_):
            s, d, sub, is_first = items[it]
            mask = chg[:, d, 512:640]
            lat = sub >= 2
            hq = hqs[l_]
            EF, qepad, sTm = EFs[slot], qepads[slot], sTms[slot]
            if lat:
                S.op("pe", TRS([(bankb(PT_, 128, h * 128), kd2[:, h * 128:(h + 1) * 128]) for h in range(4)], ident_b),
                     reads=["kd2", "cm_b"], writes=["ps1"])
                S.op("dve", CP(kd2T, bankb(PT_, 512).rearrange("p (h t) -> p h t", t=128)), reads=["ps1"], writes=["kd2T"])
                S.op("dve", STT(qdT, EF[:, :, 0:128], HG_SCALE, hq, ALU.mult, ALU.mult), reads=["EF%d" % slot, "hq%d" % l_],
                     writes=["qdT"])
                for j in range(2):
                    S.op("dve", STT(qepad[:, :, j, j * 64:(j + 1) * 64], EF[:, :, 128 + j * 64:128 + (j + 1) * 64], HG_SCALE,
                                    hq[:, :, j * 64:(j + 1) * 64], ALU.mult, ALU.mult),
                         reads=["EF%d" % slot, "hq%d" % l_, "qepad%d" % slot], writes=["qepad%d" % slot])
                lst = [(bank(PST, 128, h * 128), kd2T[:, h, :], qdT[:, h, :], True, True) for h in range(4)]
                S.op("pe", MMS(lst), reads=["kd2T", "qdT"], writes=["ps4"])
                S.op("dve", TT(sTm, bank(PST).rearrange("p (h t) -> p h t", t=128),
                               bass.AP(mask.tensor, mask.offset, [list(mask.ap[0]), [0, 4], list(mask.ap[-1])]), ALU.mult),
                     reads=["ps4", "chg"], writes=["sTm%d" % slot])

        def hg_B1(it, slot):
            s, d, sub, is_first = items[it]
            first, second = (0, 1) if d == 0 else (1, 0)
            lastpos = {0: (63 if d == 0 else 0), 1: (127 if d == 0 else 64)}
            EF, dsb = EFs[slot], dsbs[slot]
            sin, sout = "SbfIn%d" % (it % 2), "SbfIn%d" % ((it + 1) % 2)
            if is_first:
                S.op("pool", MSET(Sst[:, 0, :, :], 0.0), reads=["Sst0"], writes=["Sst0"])
                S.op("pool", MSET(SbfIn[it % 2], 0.0), reads=[sin], writes=[sin])
            for h in range(4):
                S.op("dve", STT(Sst[:, 1, h, :], Sst[:, 0, h, :], EF[:, h, 128 + lastpos[first]:129 + lastpos[first]],
                                dsb[:, first, h, :], ALU.mult, ALU.add),
                     reads=["Sst0", "EF%d" % slot, "dsb%d" % slot, "Sst1"], writes=["Sst1"])
            S.op("act", ACTF(SbfMid, Sst[:, 1, :, :], AF.Identity), reads=["Sst1", "SbfMid"], writes=["SbfMid"])
            for h in range(4):
                S.op("dve", STT(Sst[:, 0, h, :], Sst[:, 1, h, :], EF[:, h, 128 + lastpos[second]:129 + lastpos[second]],
                                dsb[:, second, h, :], ALU.mult, ALU.add),
                     reads=["Sst1", "EF%d" % slot, "dsb%d" % slot, "Sst0"], writes=["Sst0"])
            S.op("act", ACTF(SbfIn[(it + 1) % 2], Sst[:, 0, :, :], AF.Identity), reads=["Sst0", sout], writes=[sout])

        def hg_B2(it, slot, l_):
            s, d, sub, is_first = items[it]
            if sub < 2:
                return
            first, second = (0, 1) if d == 0 else (1, 0)
            vt = vts[l_]
            qepad, sTm = qepads[slot], sTms[slot]
            sin = "SbfIn%d" % (it % 2)
            lst = []
            for h in range(4):
                o_ = bank(POh, 128, h * 128)
                lst.append((o_, sTm[:, h, :], vt[:, h * 128:(h + 1) * 128], True, False))
                lst.append((o_, qepad[:, h, first, :], SbfIn[it % 2][:, h, :], False, False))
                lst.append((o_, qepad[:, h, second, :], SbfMid[:, h, :], False, True))
            S.op("pe", MMS(lst), reads=["sTm%d" % slot, "vt%d" % l_, "qepad%d" % slot, sin, "SbfMid"], writes=["ps5"])
            li = sub - 2
            if d == 0:
                S.op("act", ACTF(Obuf[:, li, :], bank(POh), AF.Identity), reads=["ps5"], writes=["Obuf%d" % li])
            else:
                S.op("dve", TT(osums[slot], bank(POh), Obuf[:, li, :], ALU.add), reads=["ps5", "Obuf%d" % li],
                     writes=["osum%d" % slot])

        def hg_C(it, slot, l_):
            s, d, sub, is_first = items[it]
            if sub < 2 or d == 0:
                return
            lt0 = (sub - 2) * 128
            og = ogs[l_]
            osum = osums[slot]
            ok_ = "osum%d" % slot
            for h in range(4):
                S.op("act", ACTF(hjunk, osum[:, h * 128:(h + 1) * 128], AF.Square, accum_out=hss[:, h:h + 1]),
                     reads=[ok_], writes=["hjunk", "hss"])
            S.op("pool", TS(hrs[:, 0:4], hss[:, 0:4], 1.0 / 128, EPS, ALU.mult, ALU.add), reads=["hss"], writes=["hrs"])
            S.op("pool", TT(hrs[:, 0:4], hrs[:, 0:4], chalf[:, 0:4], ALU.pow), reads=["hrs", "chalf"], writes=["hrs"])
            for h in range(4):
                S.op("pool", TT(gg[:, h * 128:(h + 1) * 128], og[:, h * 128:(h + 1) * 128], hgn, ALU.mult),
                     reads=["og%d" % l_, "hgn", "gg"], writes=["gg"])
                S.op("dve", STT(on[:, h * 128:(h + 1) * 128], osum[:, h * 128:(h + 1) * 128], hrs[:, h:h + 1],
                                gg[:, h * 128:(h + 1) * 128], ALU.mult, ALU.mult),
                     reads=[ok_, "hrs", "gg", "on"], writes=["on"])
            S.op("pe", TRS([(bankb(PT_, 128, h * 128), on[:, h * 128:(h + 1) * 128]) for h in range(4)], ident_b),
                 reads=["on", "cm_b"], writes=["ps1"])
            o2 = orot.next()
            S.op("act", ACTF(ohT[o2], bankb(PT_, 512).rearrange("p (h t) -> p h t", t=128), AF.Identity),
                 reads=["ps1"], writes=["ohT%d" % o2])
            S.dma("sp", [(OHT[s, :, :, lt0:lt0 + 128].rearrange("h p t -> p h t"), ohT[o2])], "oh%d" % o2,
                  reads=["ohT%d" % o2])

        lslot = {}
        lslot[0] = hg_load(0)
        lslot[1] = hg_load(1)
        hg_A1(0, 0, lslot[0])
        hg_A2(0, 0, lslot[0])
        for it in range(len(items)):
            if it + 2 < len(items):
                lslot[it + 2] = hg_load(it + 2)
            nxt = it + 1 < len(items)
            if nxt:
                hg_A1(it + 1, (it + 1) % 2, lslot[it + 1])
            hg_B1(it, it % 2)
            if nxt:
                hg_A2(it + 1, (it + 1) % 2, lslot[it + 1])
            hg_B2(it, it % 2, lslot[it])
            if it >= 1:
                hg_C(it - 1, (it - 1) % 2, lslot[it - 1])
        hg_C(len(items) - 1, (len(items) - 1) % 2, lslot[len(items) - 1])
        S.barrier()

        A.reset()
        kTr = [A.alloc([T], BF16) for _ in range(2)]
        vaugE = A.alloc([18, 2, 128], BF16)
        vaugO = A.alloc([18, 2, 128], BF16)
        qTs = [A.alloc([4, 512], BF16) for _ in range(2)]
        pTs = [A.alloc([2, 512], BF16) for _ in range(3)]
        rrows = [A.alloc([512], F32) for _ in range(2)]
        rrot = Rot(2)
        bsbs = [A.alloc([512], F32) for _ in range(2)]
        oTs = [A.alloc([512], BF16) for _ in range(2)]
        S.op("pool", MSET(vaugE, 1.0), writes=["vaugE"])
        S.op("pool", MSET(vaugO, 1.0), writes=["vaugO"])
        srot = Rot(2)
        orot = Rot(4)
        prot2 = Rot(3)
        qrot2 = Rot(2)
        otr = Rot(2)
        for s in range(2):
            for g in range(2):
                S.dma("sp", [(kTr[g][0:64, :], KT[s, g * 64:(g + 1) * 64, :]), (kTr[g][64:128, :], KT[s, g * 64:(g + 1) * 64, :])],
                      "kl%d" % g, writes=["kTr%d" % g])
                for c0 in range(0, 18, 6):
                    src_v = VV[s, c0 * 128:(c0 + 6) * 128, g * 64:(g + 1) * 64].rearrange("(c p) d -> p c d", p=128)
                    S.dma("sp", [(vaugE[:, c0:c0 + 6, g, 0:64], src_v)], "vle", writes=["vaugE"])
                    S.dma("sp", [(vaugO[:, c0:c0 + 6, g, 64:128], src_v)], "vlo", writes=["vaugO"])
            LOOK = 1
            steps = [(qt, j, c) for qt in range(4) for j in range(4) for c in range(18)]
            qinfo = {}
            sslot = {}
            pobank = {}

            def emit_S(i):
                qt, j, c = steps[i]
                if j == 0 and c == 0:
                    q_ = qrot2.next()
                    qinfo[qt] = q_
                    S.dma("sp", [(qTs[q_], QT[s, :, :, qt * 512:(qt + 1) * 512].rearrange("j p t -> p j t"))], "ql%d" % q_,
                          writes=["qT%d" % q_])
                q_ = qinfo[qt]
                g = j // 2
                r_ = srot.next()
                sslot[i] = r_
                lst = [(bank(2 * r_ + hh), kTr[g][hh * 64:(hh + 1) * 64, c * 128:(c + 1) * 128],
                        qTs[q_][hh * 64:(hh + 1) * 64, j, :], True, True) for hh in range(2)]
                S.op("pe", MMS(lst), reads=["kTr%d" % g, "qT%d" % q_], writes=["ps%d" % (2 * r_), "ps%d" % (2 * r_ + 1)])

            def emit_rest(i):
                qt, j, c = steps[i]
                g = j // 2
                if c == 0:
                    pobank[(qt, j)] = (4 + orot.next(), 4 + orot.next())
                po = pobank[(qt, j)]
                r_ = sslot[i]
                p_ = prot2.next()
                S.op("act", ACTF(pTs[p_].rearrange("p a b -> p (a b)"), ps[:, r_ * 1024:(r_ + 1) * 1024], AF.Exp, scale=ATT_SCALE),
                     reads=["ps%d" % (2 * r_), "ps%d" % (2 * r_ + 1)], writes=["pT%d" % p_])
                lst = [(bank(po[0]), vaugE[:, c, g, :], pTs[p_][:, 0, :], c == 0, c == 17),
                       (bank(po[1]), vaugO[:, c, g, :], pTs[p_][:, 1, :], c == 0, c == 17)]
                S.op("pe", MMS(lst), reads=["vaugE", "vaugO", "pT%d" % p_], writes=["ps%d" % po[0], "ps%d" % po[1]])
                if c == 17:
                    rr_ = rrot.next()
                    for hh in range(2):
                        p0 = 64 if hh == 0 else 0
                        pb_ = po[hh]
                        S.op("dve", (lambda pb_, p0, rr_: lambda e: e.reciprocal(out=rrows[rr_][p0:p0 + 1, :],
                                                                                  in_=ps[p0:p0 + 1, pb_ * 512:(pb_ + 1) * 512]))(pb_, p0, rr_),
                             reads=["ps%d" % pb_], writes=["rrow%d%d" % (rr_, hh)])
                    return (qt, j, po, rr_)
                return None

            def emit_fin(pend):
                qt, j, po, rr_ = pend
                r_ = (srot.i + 1) % srot.n
                o_ = otr.next()
                for hh in range(2):
                    p0 = 64 if hh == 0 else 0
                    pb_ = po[hh]
                    bb = 2 * r_ + hh
                    S.op("pe", MMS([(bank(bb), ones_f[p0:p0 + 1, :], rrows[rr_][p0:p0 + 1, :], True, True)]),
                         reads=["rrow%d%d" % (rr_, hh), "ones_f"], writes=["ps%d" % bb])
                    hs = slice(hh * 64, (hh + 1) * 64)
                    S.op("act", ACTF(bsbs[hh][hs, :], ps[hs, bb * 512:(bb + 1) * 512], AF.Identity), reads=["ps%d" % bb],
                         writes=["bsb%d" % hh])
                    S.op("dve", TT(oTs[o_][hs, :], ps[hs, pb_ * 512:(pb_ + 1) * 512], bsbs[hh][hs, :], ALU.mult),
                         reads=["ps%d" % pb_, "bsb%d" % hh, "oT%d" % o_], writes=["oT%d" % o_])
                S.dma("sp", [(OAT[s, j, :, qt * 512:(qt + 1) * 512], oTs[o_])], "ost%d" % o_, reads=["oT%d" % o_])

            for i in range(min(LOOK, len(steps))):
                emit_S(i)
            pend = None
            pend_at = None
            for i in range(len(steps)):
                if pend is not None and i - pend_at >= 8:
                    emit_fin(pend)
                    pend = None
                if i + LOOK < len(steps):
                    emit_S(i + LOOK)
                r = emit_rest(i)
                if r is not None:
                    if pend is not None:
                        emit_fin(pend)
                    pend, pend_at = r, i
            if pend is not None:
                emit_fin(pend)
        S.barrier()

        A.reset()
        Wao = A.alloc([4, D], BF16)
        Who = A.alloc([4, D], BF16)
        Wo = A.alloc([8, D], BF16)
        xts = [A.alloc([4, D], F32) for _ in range(2)]
        oaT = [A.alloc([4, 512], BF16) for _ in range(2)]
        ohTt = [A.alloc([4, 512], BF16) for _ in range(2)]
        sgT = [A.alloc([16, 512], BF16) for _ in range(2)]
        ymT = A.alloc([8, 512], BF16)
        m1 = [A.alloc([512], F32) for _ in range(2)]
        m2 = [A.alloc([512], F32) for _ in range(2)]
        bufs = {"junk": A.alloc([D], F32), "ss2": A.alloc([8], F32), "rstd2": A.alloc([8], F32), "yt": A.alloc([D], F32)}
        S.dma("pool", [(Wao, w_att_out.rearrange("(h p) n -> p h n", p=128))], "wlda", writes=["Wao"])
        S.dma("pool", [(Who, w_hg_out.rearrange("(h p) n -> p h n", p=128))], "wldh", writes=["Who"])
        for k in range(8):
            S.dma("pool", [(Wo[:, k, :], w_o[k * 128:(k + 1) * 128, :])], "wldo", writes=["Wo"])
        psYs = [ps[:, 4 * 512:6 * 512], ps[:, 6 * 512:8 * 512]]
        yrot = Rot(2)
        arot = Rot(2)
        mrot = Rot(2)
        p4tiles = [(s, t0) for s in range(2) for t0 in range(0, TL, 512)]

        def p4_load(ti):
            s, t0 = p4tiles[ti]
            sl = ti % 2
            S.dma("sp", [(xts[sl], X1[s, TC + t0:TC + t0 + 512, :].rearrange("(n p) d -> p n d", p=128)),
                         (oaT[sl], OAT[s, :, :, t0:t0 + 512].rearrange("h p t -> p h t")),
                         (ohTt[sl], OHT[s, :, :, t0:t0 + 512].rearrange("h p t -> p h t")),
                         (sgT[sl], SG[s, :, :, t0:t0 + 512].rearrange("c p t -> p c t"))],
                  "p4l%d" % sl, writes=["xt%d" % sl, "oaT%d" % sl, "ohTt%d" % sl, "sgT%d" % sl])

        p4_load(0)
        for ti, (s, t0) in enumerate(p4tiles):
            if True:
                sl = ti % 2
                xt = xts[sl]
                kx = "xt%d" % sl
                if ti + 1 < len(p4tiles):
                    p4_load(ti + 1)
                for c in range(8):
                    pa = arot.next()
                    pbk = 2 + arot.i
                    lst = [(bank(pa), Wao[:, h, c * 128:(c + 1) * 128], oaT[sl][:, h, :], h == 0, h == 3) for h in range(4)]
                    lst += [(bank(pbk), Who[:, h, c * 128:(c + 1) * 128], ohTt[sl][:, h, :], h == 0, h == 3) for h in range(4)]
                    S.op("pe", MMS(lst), reads=["Wao", "Who", "oaT%d" % sl, "ohTt%d" % sl], writes=["ps%d" % pa, "ps%d" % pbk])
                    m_ = mrot.next()
                    S.op("dve", TT(m1[m_], bank(pa), sgT[sl][:, c, :], ALU.mult), reads=["ps%d" % pa, "sgT%d" % sl], writes=["m1%d" % m_])
                    S.op("dve", TT(m2[m_], bank(pbk), sgT[sl][:, 8 + c, :], ALU.mult), reads=["ps%d" % pbk, "sgT%d" % sl],
                         writes=["m2%d" % m_])
                    S.op("pool", TT(ymT[:, c, :], m1[m_], m2[m_], ALU.add), reads=["m1%d" % m_, "m2%d" % m_, "ymT"], writes=["ymT"])
                for n in range(4):
                    y_ = yrot.next()
                    psY = psYs[y_]
                    lst = []
                    for half in range(2):
                        lst += [(psY[:, half * 512:(half + 1) * 512], ymT[:, c, n * 128:(n + 1) * 128],
                                 Wo[:, c, half * 512:(half + 1) * 512], c == 0, c == 7) for c in range(8)]
                    S.op("pe", MMS(lst), reads=["ymT", "Wo"], writes=["psY%d" % y_])
                    post_resid(psY, "psY%d" % y_, xt[:, n, :], kx, Gb[(1, s)], "Gb", bufs)
                S.dma("sp", [(X1[s, TC + t0:TC + t0 + 512, :].rearrange("(n p) d -> p n d", p=128), xt)], "xst%d" % sl, reads=[kx])
        S.barrier()

        tiles = []
        for s in range(2):
            for t0 in range(0, TL, 256):
                tiles.append((X1[s, TC + t0:TC + t0 + 256, :], out[s, t0:t0 + 256, :], s))
        ffn_phase(1, tiles)
        S.emit(nc, st)
    return nc


def _constants():
    ident = np.eye(128, dtype=np.float32)
    p = np.arange(128)
    bd64 = ((p[:, None] // 64) == (p[None, :] // 64)).astype(np.float32) / 64.0
    perm = np.zeros((128, 128), np.float32)
    for m in range(128):
        if (m % 32) < 16:
            perm[m + 16, m] = -1.0
        else:
            perm[m - 16, m] = 1.0
    c_mat = np.concatenate([ident, bd64, perm], axis=1)
    t = np.arange(TL)
    row = (t // 64).astype(np.float32)
    col = (t % 64).astype(np.float32)
    inv_freq = (np.float32(10000.0) ** (-np.arange(16, dtype=np.float32) / np.float32(16))).astype(np.float32)
    ang_r = row[:, None] * inv_freq
    ang_c = col[:, None] * inv_freq
    ang = np.concatenate([ang_r, ang_r, ang_c, ang_c], axis=-1).astype(np.float32)
    cosT = np.cos(ang).T.astype(np.float32)
    sinT = np.sin(ang).T.astype(np.float32)
    c_rope = np.stack([np.concatenate([cosT, cosT], 0), np.concatenate([sinT, sinT], 0)], axis=1)
    c_hg = np.zeros((128, 2, 640), np.float32)
    tt = np.arange(128)
    same = (tt[:, None] // 64) == (tt[None, :] // 64)
    for d in range(2):
        if d == 0:
            upto = same & (tt[:, None] <= tt[None, :])
            mid = (tt // 64) * 64 + 31
        else:
            upto = same & (tt[:, None] >= tt[None, :])
            mid = (tt // 64) * 64 + 32
        after = same & ~upto
        upto_mid = upto[:, mid]
        M1 = after.astype(np.float32)
        M2 = upto_mid.astype(np.float32) - upto.astype(np.float32)
        c_hg[:, d, 0:128] = M1
        c_hg[:, d, 128:256] = M2
        c_hg[:, d, 256:384] = -M2
        c_hg[:, d, 384:512] = upto.astype(np.float32)
        c_hg[:, d, 512:640] = upto.astype(np.float32)
    return c_mat, np.ascontiguousarray(c_rope), c_hg


_NC_CACHE = {}


def kernel(x, c, ctx, c_ctx, w_mod, b_mod, norm_pre, norm_post, ffn_w_gate, ffn_w_up, ffn_w_down,
           w_in, q_norm, k_norm, hg_lower_bound, hg_norm, w_att_out, w_hg_out, w_o, _debug=False):
    f32 = lambda a: np.ascontiguousarray(np.asarray(a, dtype=np.float32))
    x, c, ctx, c_ctx = f32(x), f32(c), f32(ctx), f32(c_ctx)
    c_mat, c_rope, c_hg = _constants()
    key = bool(_debug)
    if key not in _NC_CACHE:
        _NC_CACHE[key] = build_program(debug=_debug)
    nc = _NC_CACHE[key]
    b_mod1 = f32(b_mod)[0]
    shared = {
        "w_mod": f32(w_mod)[0], "b_mod": b_mod1,
        "bmodT": np.ascontiguousarray(b_mod1.reshape(9, 8, 128).transpose(2, 0, 1).reshape(128, 72)),
        "npreT": np.ascontiguousarray(f32(norm_pre)[0].reshape(3, 8, 128).transpose(2, 0, 1).reshape(128, 24)),
        "norm_post": f32(norm_post)[0],
        "ffn_w_gate": f32(ffn_w_gate)[0], "ffn_w_up": f32(ffn_w_up)[0], "ffn_w_down": f32(ffn_w_down)[0],
        "w_in": f32(w_in)[0],
        "qk_gain": np.ascontiguousarray(np.stack([np.tile(f32(q_norm)[0], 2), np.tile(f32(k_norm)[0], 2)], axis=1)),
        "hg_lower_bound": f32(hg_lower_bound), "hg_norm": f32(hg_norm)[0],
        "w_att_out": f32(w_att_out)[0], "w_hg_out": f32(w_hg_out)[0], "w_o": f32(w_o)[0],
        "c_mat": c_mat, "c_rope": c_rope, "c_hg": c_hg,
    }
    in_maps = []
    for i in range(8):
        cv = np.stack([c[2 * i], c[2 * i + 1], c_ctx], axis=0)
        cT = np.ascontiguousarray(cv.reshape(3, 8, 128).transpose(2, 1, 0).reshape(128, 24))
        m = dict(shared)
        m["x"] = np.ascontiguousarray(x[2 * i:2 * i + 2])
        m["ctx"] = np.ascontiguousarray(ctx[2 * i:2 * i + 2])
        m["cT"] = cT
        in_maps.append(m)
    res = run_bass_kernel_spmd(nc, in_maps, core_ids=list(range(8)))
    if _debug:
        return res
    return np.concatenate([r["out"] for r in res.results], axis=0)
```
